# Optimizing a Trainium2 kernel written in Bass

```python
import jax, jax.numpy as jnp
from jax import lax
import numpy as np

D_MODEL = 2048
BATCH = 4
SEQ = 2048
DEPTH = 2
DEC_BATCH = 8
DEC_SEQ = 16
PAST_LEN = 1024

CHUNK = 64
QBLOCK = 128
HEAD_DIM = 128
H_A = D_MODEL // (2 * HEAD_DIM)
W_A = H_A * HEAD_DIM
H_B = D_MODEL // (2 * HEAD_DIM)
DK_B = 128
DV_B = 128
W_B = H_B * DV_B
H_C = D_MODEL // HEAD_DIM
W_C = H_C * HEAD_DIM
H_X = 4
HD_X = D_MODEL // H_X
N_MEM = 256
D_FF = ((8 * D_MODEL // 3 + 127) // 128) * 128
N_EVEN = (DEPTH + 1) // 2
N_ODD = DEPTH // 2
D_IN_EVEN = 3 * W_A + H_A + 4 * W_B
D_IN_ODD = 3 * W_C
ALPHA = (2.0 * DEPTH) ** 0.25
BETA = (8.0 * DEPTH) ** -0.25
LN_EPS = 1e-5
RMS_EPS = 1e-6
F32 = jnp.float32

kernel_name = 'hybrid_streaming_encoder_step'


def layer_norm(x, g, b):
    mu = jnp.mean(x, axis=-1, keepdims=True)
    var = jnp.mean(jnp.square(x - mu), axis=-1, keepdims=True)
    return (x - mu) * lax.rsqrt(var + LN_EPS) * g.astype(F32) + b.astype(F32)


def post_norm(x, sub, g, b):
    return layer_norm(ALPHA * x.astype(F32) + sub.astype(F32), g, b).astype(x.dtype)


def swiglu_half(x, w_gate, w_up, w_down):
    return 0.5 * ((jax.nn.silu(x @ w_gate) * (x @ w_up)) @ w_down)


def fox_core(q, k, v, cum_q, cum_k, qpos, kpos):
    logits = jnp.einsum('bqhd,bkhd->bhqk', q, k).astype(F32) * (HEAD_DIM ** -0.5)
    bias = jnp.swapaxes(cum_q, 1, 2)[:, :, :, None] - jnp.swapaxes(cum_k, 1, 2)[:, :, None, :]
    mask = qpos[:, None] >= kpos[None, :]
    p = jax.nn.softmax(jnp.where(mask, logits + bias, -jnp.inf), axis=-1)
    return jnp.einsum('bhqk,bkhd->bqhd', p.astype(v.dtype), v)


def fox_prompt(q, k, v, logf):
    B, S, H, D = q.shape
    nb = S // QBLOCK
    cum = jnp.cumsum(logf, axis=1)
    pos = jnp.arange(S)
    qb = q.reshape(B, nb, QBLOCK, H, D).swapaxes(0, 1)
    cb = cum.reshape(B, nb, QBLOCK, H).swapaxes(0, 1)
    pb = pos.reshape(nb, QBLOCK)
    out = lax.map(lambda a: fox_core(a[0], k, v, a[1], cum, a[2], pos), (qb, cb, pb))
    return out.swapaxes(0, 1).reshape(B, S, H * D)


def sb_core(q, k, v, qpos, kpos):
    z = jnp.einsum('bqhd,bkhd->bhqk', q, k).astype(F32) * (HEAD_DIM ** -0.5)
    mask = kpos[None, :] < qpos[:, None]
    log_1m = jnp.where(mask, jax.nn.log_sigmoid(-z), 0.0)
    rem = lax.cumsum(log_1m, axis=3, reverse=True) - log_1m
    w = jnp.where(mask, jnp.exp(jax.nn.log_sigmoid(z) + rem), 0.0)
    return jnp.einsum('bhqk,bkhd->bqhd', w.astype(v.dtype), v)


def sb_prompt(q, k, v):
    B, S, H, D = q.shape
    nb = S // QBLOCK
    pos = jnp.arange(S)
    qb = q.reshape(B, nb, QBLOCK, H, D).swapaxes(0, 1)
    pb = pos.reshape(nb, QBLOCK)
    out = lax.map(lambda a: sb_core(a[0], k, v, a[1], pos), (qb, pb))
    return out.swapaxes(0, 1).reshape(B, S, H * D)


def hgrn_chunk(s0, q, logf, k, i):
    L = q.shape[1]
    b = jnp.cumsum(logf, axis=1)
    causal = (jnp.arange(L)[:, None] >= jnp.arange(L)[None, :])[None, :, :, None, None]
    decay = jnp.exp(jnp.where(causal, b[:, :, None] - b[:, None, :], -jnp.inf))
    scores = jnp.einsum('bthc,btshc,bshc->bhts', q, decay, k)
    o = jnp.einsum('bhts,bshv->bthv', scores, i) + jnp.einsum('bthc,bhcv->bthv', q * jnp.exp(b), s0)
    b_last = b[:, -1]
    s_new = jnp.exp(b_last)[..., None] * s0 + jnp.einsum('bshc,bshv->bhcv', k * jnp.exp(b_last[:, None] - b), i)
    return o, s_new


def hgrn_prompt(q, logf, k, i):
    B, S, H, DK = q.shape
    DV = i.shape[-1]
    n = S // CHUNK
    to_blocks = lambda t: t.reshape(B, n, CHUNK, H, t.shape[-1]).swapaxes(0, 1)

    def step(state, xs):
        o, state = hgrn_chunk(state, *xs)
        return state, o

    s0 = jnp.zeros((B, H, DK, DV), F32)
    s_fin, o = lax.scan(step, s0, (to_blocks(q), to_blocks(logf), to_blocks(k), to_blocks(i)))
    return o.swapaxes(0, 1).reshape(B, S, H, DV), s_fin


def hgrn_lower_bound(lb_logits, j):
    return jnp.cumsum(jax.nn.softmax(lb_logits.astype(F32), axis=0), axis=0)[j]


def even_project(x, w_in, b_f, lb):
    B, T, _ = x.shape
    h = x @ w_in
    offs = np.cumsum([W_A, W_A, W_A, H_A, W_B, W_B, W_B, W_B])[:-1].tolist()
    qa, ka, va, fa, qb, fb, ib, gb = jnp.split(h, offs, axis=-1)
    qa = qa.reshape(B, T, H_A, HEAD_DIM)
    ka = ka.reshape(B, T, H_A, HEAD_DIM)
    va = va.reshape(B, T, H_A, HEAD_DIM)
    logf_a = jax.nn.log_sigmoid(fa.astype(F32) + b_f.astype(F32))
    f_b = lb + (1.0 - lb) * jax.nn.sigmoid(fb.astype(F32))
    logf_b = jnp.log(f_b).reshape(B, T, H_B, DK_B)
    k_b = (1.0 - f_b).reshape(B, T, H_B, DK_B)
    q_b = jax.nn.silu(qb.astype(F32)).reshape(B, T, H_B, DK_B)
    i_b = ib.astype(F32).reshape(B, T, H_B, DV_B)
    return (qa, ka, va, logf_a), (q_b, logf_b, k_b, i_b, gb)


def even_out(o_a, o_b, g_b, norm_g, w_out):
    B, T = o_a.shape[:2]
    ob = o_b * lax.rsqrt(jnp.mean(jnp.square(o_b), axis=-1, keepdims=True) + RMS_EPS)
    ob = ob.reshape(B, T, W_B) * norm_g.astype(F32) * jax.nn.silu(g_b.astype(F32))
    merged = jnp.concatenate([o_a, ob.astype(o_a.dtype)], axis=-1)
    return merged @ w_out


def even_mixer_prompt(x, w_in, b_f, lb, norm_g, w_out):
    (qa, ka, va, lfa), (qb, lfb, kb, ib, gb) = even_project(x, w_in, b_f, lb)
    oa = fox_prompt(qa, ka, va, lfa)
    ob, s_fin = hgrn_prompt(qb, lfb, kb, ib)
    y = even_out(oa, ob, gb, norm_g, w_out)
    return y, ka, va, lfa.astype(x.dtype), s_fin.astype(x.dtype)


def even_mixer_sample(x, c_k, c_v, c_logf, s0, w_in, b_f, lb, norm_g, w_out):
    (qa, ka, va, lfa), (qb, lfb, kb, ib, gb) = even_project(x, w_in, b_f, lb)
    B, T = x.shape[:2]
    P = c_k.shape[1]
    k_all = jnp.concatenate([c_k.astype(ka.dtype), ka], axis=1)
    v_all = jnp.concatenate([c_v.astype(va.dtype), va], axis=1)
    cum = jnp.cumsum(jnp.concatenate([c_logf.astype(F32), lfa], axis=1), axis=1)
    pos = jnp.arange(P + T)
    oa = fox_core(qa, k_all, v_all, cum[:, P:], cum, pos[P:], pos).reshape(B, T, W_A)
    ob, s_new = hgrn_chunk(s0.astype(F32), qb, lfb, kb, ib)
    y = even_out(oa, ob, gb, norm_g, w_out)
    return y, ka, va, lfa.astype(x.dtype), s_new.astype(x.dtype)


def odd_project(x, w_in):
    B, T, _ = x.shape
    q, k, v = jnp.split(x @ w_in, 3, axis=-1)
    r = lambda t: t.reshape(B, T, H_C, HEAD_DIM)
    return r(q), r(k), r(v)


def odd_mixer_prompt(x, w_in, w_out):
    q, k, v = odd_project(x, w_in)
    return sb_prompt(q, k, v) @ w_out, k, v


def odd_mixer_sample(x, c_k, c_v, w_in, w_out):
    q, k, v = odd_project(x, w_in)
    B, T = x.shape[:2]
    P = c_k.shape[1]
    k_all = jnp.concatenate([c_k.astype(k.dtype), k], axis=1)
    v_all = jnp.concatenate([c_v.astype(v.dtype), v], axis=1)
    pos = jnp.arange(P + T)
    o = sb_core(q, k_all, v_all, pos[P:], pos).reshape(B, T, W_C)
    return o @ w_out, k, v


def mem_kv(mem, w_kv):
    B, N, _ = mem.shape
    mk, mv = jnp.split(mem @ w_kv, 2, axis=-1)
    return mk.reshape(B, N, H_X, HD_X), mv.reshape(B, N, H_X, HD_X)


def cross_attend(x, mk, mv, w_q, w_o):
    B, T, _ = x.shape
    q = (x @ w_q).reshape(B, T, H_X, HD_X)
    logits = jnp.einsum('bthd,bmhd->bhtm', q, mk.astype(q.dtype)).astype(F32) * (HD_X ** -0.5)
    p = jax.nn.softmax(logits, axis=-1)
    o = jnp.einsum('bhtm,bmhd->bthd', p.astype(x.dtype), mv.astype(x.dtype)).reshape(B, T, D_MODEL)
    return o @ w_o


def setup_inputs(seed: int = 0) -> dict:
    key = jax.random.key(seed)
    ks = list(jax.random.split(key, 32))
    cnt = [0]

    def nrm(shape, scale):
        k = ks[cnt[0]]
        cnt[0] += 1
        return jax.random.normal(k, shape, F32) * scale

    d_in = float(D_MODEL) ** -0.5
    return {
        'x_prompt': nrm((BATCH, SEQ, D_MODEL), 1.0),
        'x_sample': nrm((DEC_BATCH, DEC_SEQ, D_MODEL), 1.0),
        'mem_prompt': nrm((BATCH, N_MEM, D_MODEL), 1.0),
        'cache_fox_k': nrm((N_EVEN, DEC_BATCH, PAST_LEN, H_A, HEAD_DIM), 1.0),
        'cache_fox_v': nrm((N_EVEN, DEC_BATCH, PAST_LEN, H_A, HEAD_DIM), 1.0),
        'cache_fox_logf': jax.nn.log_sigmoid(2.0 + nrm((N_EVEN, DEC_BATCH, PAST_LEN, H_A), 1.0)),
        'state_hgrn': nrm((N_EVEN, DEC_BATCH, H_B, DK_B, DV_B), 0.5),
        'cache_sb_k': nrm((N_ODD, DEC_BATCH, PAST_LEN, H_C, HEAD_DIM), 1.0),
        'cache_sb_v': nrm((N_ODD, DEC_BATCH, PAST_LEN, H_C, HEAD_DIM), 1.0),
        'cache_mem_k': nrm((DEPTH, DEC_BATCH, N_MEM, H_X, HD_X), 1.0),
        'cache_mem_v': nrm((DEPTH, DEC_BATCH, N_MEM, H_X, HD_X), 1.0),
        'ln_g': 1.0 + nrm((DEPTH, 4, D_MODEL), 0.01),
        'ln_b': nrm((DEPTH, 4, D_MODEL), 0.01),
        'ffn1_w_gate': nrm((DEPTH, D_MODEL, D_FF), d_in),
        'ffn1_w_up': nrm((DEPTH, D_MODEL, D_FF), d_in),
        'ffn1_w_down': nrm((DEPTH, D_FF, D_MODEL), BETA * float(D_FF) ** -0.5),
        'ffn2_w_gate': nrm((DEPTH, D_MODEL, D_FF), d_in),
        'ffn2_w_up': nrm((DEPTH, D_MODEL, D_FF), d_in),
        'ffn2_w_down': nrm((DEPTH, D_FF, D_MODEL), BETA * float(D_FF) ** -0.5),
        'x_w_q': nrm((DEPTH, D_MODEL, D_MODEL), d_in),
        'x_w_kv': nrm((DEPTH, D_MODEL, 2 * D_MODEL), d_in),
        'x_w_o': nrm((DEPTH, D_MODEL, D_MODEL), BETA * d_in),
        'ev_w_in': nrm((N_EVEN, D_MODEL, D_IN_EVEN), d_in),
        'fox_b_f': 2.0 + nrm((N_EVEN, H_A), 0.5),
        'hgrn_lb_logits': nrm((N_EVEN + 1, W_B), 0.1),
        'hgrn_norm_g': 1.0 + nrm((N_EVEN, W_B), 0.01),
        'ev_w_out': nrm((N_EVEN, W_A + W_B, D_MODEL), BETA * float(W_A + W_B) ** -0.5),
        'od_w_in': nrm((N_ODD, D_MODEL, D_IN_ODD), d_in),
        'od_w_out': nrm((N_ODD, W_C, D_MODEL), BETA * float(W_C) ** -0.5),
    }


def reference(x_prompt, x_sample, mem_prompt, cache_fox_k, cache_fox_v, cache_fox_logf, state_hgrn,
              cache_sb_k, cache_sb_v, cache_mem_k, cache_mem_v, ln_g, ln_b,
              ffn1_w_gate, ffn1_w_up, ffn1_w_down, ffn2_w_gate, ffn2_w_up, ffn2_w_down,
              x_w_q, x_w_kv, x_w_o, ev_w_in, fox_b_f, hgrn_lb_logits, hgrn_norm_g, ev_w_out,
              od_w_in, od_w_out):
    xp, xs = x_prompt, x_sample
    fk_p, fv_p, flf_p, hs_p, sk_p, sv_p, mk_p, mv_p = [], [], [], [], [], [], [], []
    fk_s, fv_s, flf_s, hs_s, sk_s, sv_s = [], [], [], [], [], []
    for l in range(DEPTH):
        xp = post_norm(xp, swiglu_half(xp, ffn1_w_gate[l], ffn1_w_up[l], ffn1_w_down[l]), ln_g[l, 0], ln_b[l, 0])
        xs = post_norm(xs, swiglu_half(xs, ffn1_w_gate[l], ffn1_w_up[l], ffn1_w_down[l]), ln_g[l, 0], ln_b[l, 0])
        j = l // 2
        if l % 2 == 0:
            lb = hgrn_lower_bound(hgrn_lb_logits, j)
            mp, kp, vp, lfp, sp = even_mixer_prompt(xp, ev_w_in[j], fox_b_f[j], lb, hgrn_norm_g[j], ev_w_out[j])
            ms, kss, vss, lfs, ss = even_mixer_sample(xs, cache_fox_k[j], cache_fox_v[j], cache_fox_logf[j],
                                                      state_hgrn[j], ev_w_in[j], fox_b_f[j], lb,
                                                      hgrn_norm_g[j], ev_w_out[j])
            fk_p.append(kp); fv_p.append(vp); flf_p.append(lfp); hs_p.append(sp)
            fk_s.append(kss); fv_s.append(vss); flf_s.append(lfs); hs_s.append(ss)
        else:
            mp, kp, vp = odd_mixer_prompt(xp, od_w_in[j], od_w_out[j])
            ms, kss, vss = odd_mixer_sample(xs, cache_sb_k[j], cache_sb_v[j], od_w_in[j], od_w_out[j])
            sk_p.append(kp); sv_p.append(vp)
            sk_s.append(kss); sv_s.append(vss)
        xp = post_norm(xp, mp, ln_g[l, 1], ln_b[l, 1])
        xs = post_norm(xs, ms, ln_g[l, 1], ln_b[l, 1])
        mkp, mvp = mem_kv(mem_prompt, x_w_kv[l])
        mk_p.append(mkp); mv_p.append(mvp)
        xp = post_norm(xp, cross_attend(xp, mkp, mvp, x_w_q[l], x_w_o[l]), ln_g[l, 2], ln_b[l, 2])
        xs = post_norm(xs, cross_attend(xs, cache_mem_k[l], cache_mem_v[l], x_w_q[l], x_w_o[l]), ln_g[l, 2], ln_b[l, 2])
        xp = post_norm(xp, swiglu_half(xp, ffn2_w_gate[l], ffn2_w_up[l], ffn2_w_down[l]), ln_g[l, 3], ln_b[l, 3])
        xs = post_norm(xs, swiglu_half(xs, ffn2_w_gate[l], ffn2_w_up[l], ffn2_w_down[l]), ln_g[l, 3], ln_b[l, 3])
    fox_k_prompt = jnp.stack(fk_p)
    fox_v_prompt = jnp.stack(fv_p)
    fox_logf_prompt = jnp.stack(flf_p)
    hgrn_state_prompt = jnp.stack(hs_p)
    sb_k_prompt = jnp.stack(sk_p)
    sb_v_prompt = jnp.stack(sv_p)
    mem_k_prompt = jnp.stack(mk_p)
    mem_v_prompt = jnp.stack(mv_p)
    fox_k_sample = jnp.stack(fk_s)
    fox_v_sample = jnp.stack(fv_s)
    fox_logf_sample = jnp.stack(flf_s)
    hgrn_state_sample = jnp.stack(hs_s)
    sb_k_sample = jnp.stack(sk_s)
    sb_v_sample = jnp.stack(sv_s)
    return (xp, xs, fox_k_prompt, fox_v_prompt, fox_logf_prompt, hgrn_state_prompt, sb_k_prompt, sb_v_prompt,
            mem_k_prompt, mem_v_prompt, fox_k_sample, fox_v_sample, fox_logf_sample, hgrn_state_sample,
            sb_k_sample, sb_v_sample)
```

```python
from concourse.bass_utils import run_bass_kernel_spmd
import numpy as np
import concourse.bass as bass
import concourse.mybir as mybir

F32 = mybir.dt.float32
BF16 = mybir.dt.bfloat16
AF = mybir.ActivationFunctionType
ALU = mybir.AluOpType


class Buf:
    __slots__ = ("name", "w", "r", "x")

    def __init__(self, name="", x=False):
        self.name = name
        self.w = None
        self.r = []
        self.x = x


class Lane:
    __slots__ = ("sem", "cnt")

    def __init__(self, sem):
        self.sem = sem
        self.cnt = 0


class Prog:
    ROT = 30000

    def __init__(self, nc):
        self.nc = nc
        self.eng = {"pe": nc.tensor, "dve": nc.vector, "act": nc.scalar, "pool": nc.gpsimd, "sp": nc.sync}
        self.cnt = {e: 0 for e in self.eng}
        self.sem = {e: nc.alloc_semaphore(name=f"c_{e}_0") for e in self.eng}
        self.nrot = {e: 0 for e in self.eng}
        self.known = {e: {} for e in self.eng}
        self.lanes = []
        self.all_sems = list(self.sem.values())
        self.ninstr = 0

    def lane(self, name="lane"):
        s = self.nc.alloc_semaphore(name=f"{name}_{len(self.lanes)}")
        l = Lane(s)
        self.lanes.append(l)
        return l

    def _collect(self, e, reads, writes):
        need = {}

        def add(tok):
            if tok is None:
                return
            s, v = tok
            if e == "pe" and s is self.sem["pe"]:
                return
            k = id(s)
            if k not in need or need[k][1] < v:
                need[k] = (s, v)

        for b in reads:
            add(b.w)
            if b.x:
                for t in b.r:
                    if t[0] is not self.sem[e]:
                        add(t)
        for b in writes:
            add(b.w)
            for t in b.r:
                add(t)
        kn = self.known[e]
        out = []
        for k, (s, v) in need.items():
            if kn.get(k, 0) >= v:
                continue
            out.append((s, v))
            kn[k] = v
        return out

    def _waits(self, e, deps):
        for s, v in deps:
            self.eng[e].wait_ge(s, v)
            self.ninstr += 1

    def _rot(self, e):
        if self.cnt[e] >= self.ROT:
            self.nrot[e] += 1
            self.sem[e] = self.nc.alloc_semaphore(name=f"c_{e}_{self.nrot[e]}")
            self.cnt[e] = 0

    def op(self, e, fn, reads=(), writes=()):
        self._rot(e)
        deps = self._collect(e, reads, writes)
        self._waits(e, deps)
        ins = fn(self.eng[e])
        self.cnt[e] += 1
        tok = (self.sem[e], self.cnt[e])
        ins.then_inc(tok[0], 1)
        self.ninstr += 1
        for b in reads:
            b.r.append(tok)
        for b in writes:
            b.w = tok
            b.r = []
        return tok

    def dma(self, q, out, in_, lane, reads=(), writes=(), **kw):
        deps = self._collect(q, reads, writes)
        self._waits(q, deps)
        ins = self.eng[q].dma_start(out=out, in_=in_, **kw)
        lane.cnt += 16
        ins.then_inc(lane.sem, 16)
        self.ninstr += 1
        tok = (lane.sem, lane.cnt)
        for b in reads:
            b.r.append(tok)
        for b in writes:
            b.w = tok
            b.r = []
        return tok

    def coll(self, kind, ins, outs, groups, lane, reads=(), writes=()):
        if getattr(self, "nocoll", False):
            return None
        q = "pool"
        deps = self._collect(q, reads, writes)
        self._waits(q, deps)
        i = self.eng[q].collective_compute(kind, ALU.bypass, replica_groups=groups, ins=ins, outs=outs)
        lane.cnt += 1
        i.then_inc(lane.sem, 1)
        tok = (lane.sem, lane.cnt)
        for b in reads:
            b.r.append(tok)
        for b in writes:
            b.w = tok
            b.r = []
        return tok

    def barrier(self):
        toks = [(self.sem[e], self.cnt[e]) for e in self.eng if self.cnt[e] > 0]
        toks += [(l.sem, l.cnt) for l in self.lanes if l.cnt > 0]
        for e in self.eng:
            kn = self.known[e]
            for s, v in toks:
                if e == "pe" and s is self.sem["pe"]:
                    continue
                if kn.get(id(s), 0) >= v:
                    continue
                self.eng[e].wait_ge(s, v)
                kn[id(s)] = v
                self.ninstr += 1


class PsPool:
    def __init__(self, tiles):
        self.tiles = tiles
        self.i = 0

    def get(self):
        t = self.tiles[self.i % len(self.tiles)]
        self.i += 1
        return t


class Rot:
    def __init__(self, tiles):
        self.tiles = tiles
        self.i = 0

    def get(self):
        t = self.tiles[self.i % len(self.tiles)]
        self.i += 1
        return t


import contextlib

D = 2048
KC = 16
T = 1024
TS = 16
NT = T + TS
TILES = [(0, 512), (512, 512), (1024, 16)]
TILES_F = [(0, 352), (352, 344), (696, 344)]
DFF = 5504
NG = 43
ALPHA = 4.0 ** 0.25
LN_EPS = 1e-5
NB_W = 10


class Ctx:
    pass


def setup(nc, cx):
    P = cx.P
    es = cx.es
    sb = lambda name, shape, dt: es.enter_context(nc.sbuf_tensor(name, shape, dt))
    cx.sb = sb
    cx.xf = sb("s_xf", [128, KC, NT], F32)
    cx.xf_b = [[Buf(f"xf{k}_{t}") for t in range(3)] for k in range(KC)]
    cx.xb = sb("s_xb", [128, KC, NT], BF16)
    cx.xb_b = [[Buf(f"xb{k}_{t}") for t in range(3)] for k in range(KC)]
    cx.wring = None
    cx.uid_ = 0
    cx.lanes_ = [P.lane("g") for _ in range(20)]
    cx.lanes2_ = [P.lane("h") for _ in range(21)]
    cx.L = lambda i: cx.lanes_[i]
    cx.cc_lane = P.lane("cc")
    for nm in ["xgo_b", "xgco_b", "xgso_b", "xg2ko_b", "xg2vo_b"]:
        setattr(cx, nm, Buf(nm))
    cx.w_b = [Buf(f"w{i}") for i in range(NB_W)]
    cx.w_lane = [P.lane("wl") for i in range(NB_W)]
    cx.w_i = 0
    cx.ps = []
    for i in range(8):
        t = es.enter_context(nc.psum_tensor(f"ps{i}", [128, 512], F32))
        cx.ps.append((t, Buf(f"ps{i}", x=True)))
    cx.psA = PsPool(cx.ps[0:4])
    cx.psB = PsPool(cx.ps[4:8])
    cx.c_b = Buf("consts")
    cx.inv2048 = sb("inv2048", [128, 128], BF16)
    cx.ones_bf = sb("ones_bf", [128, 128], BF16)
    cx.ones_f = sb("ones_f", [128, 128], F32)
    cx.lng = sb("s_lng", [128, 8, KC], F32)
    cx.lnb = sb("s_lnb", [128, 8, KC], F32)
    cx.lnga = sb("s_lnga", [128, 8, KC], F32)
    cx.lnba = sb("s_lnba", [128, 8, KC], F32)
    P.op("pool", lambda e: e.memset(cx.inv2048[:], 1.0 / 2048.0), writes=[cx.c_b])
    P.op("pool", lambda e: e.memset(cx.ones_bf[:], 1.0), writes=[cx.c_b])
    P.op("pool", lambda e: e.memset(cx.ones_f[:], 1.0), writes=[cx.c_b])
    ll = P.lane("ln")
    cx.misc_lane = ll
    P.dma("sp", cx.lng[:], cx.d["lng"], ll, writes=[cx.c_b])
    P.dma("sp", cx.lnb[:], cx.d["lnb"], ll, writes=[cx.c_b])
    P.op("dve", lambda e: e.tensor_scalar(cx.lnga[:], cx.lng[:], ALPHA, None, ALU.mult), reads=[cx.c_b], writes=[cx.c_b])
    P.op("dve", lambda e: e.tensor_scalar(cx.lnba[:], cx.lnb[:], ALPHA, None, ALU.mult), reads=[cx.c_b], writes=[cx.c_b])


def load_w(cx, src_ap, ncols=2048):
    P = cx.P
    i = cx.w_i % cx.nslots
    cx.w_i += 1
    P.dma("pool", cx.wring[:, i, 0:ncols], src_ap, cx.w_lane[i], writes=[cx.w_b[i]])
    return cx.wring[:, i, :], cx.w_b[i]


def uid(cx, name):
    cx.uid_ += 1
    return f"{name}_{cx.uid_}"


@contextlib.contextmanager
def ring(cx, nslots=NB_W):
    with cx.nc.sbuf_tensor(uid(cx, "wring"), [128, nslots, 2048], BF16) as t:
        cx.wring = t
        cx.nslots = nslots
        cx.w_i = 0
        yield
    cx.wring = None


class WStream:
    def __init__(self, cx, srcs, la=6):
        self.cx = cx
        self.srcs = srcs
        self.slots = {}
        self.issued = 0
        self.la = la

    def get(self, k):
        while self.issued < len(self.srcs) and self.issued <= k + self.la:
            ap, ncols = self.srcs[self.issued]
            self.slots[self.issued] = load_w(self.cx, ap, ncols)
            self.issued += 1
        return self.slots.pop(k)


def load_x(cx):
    P = cx.P
    l = P.lane("xin")
    for kc in range(KC):
        P.dma("sp", cx.xf[:, kc, :], cx.d["xT"][kc], l, writes=cx.xf_b[kc])
    for kc in range(KC):
        for b in cx.xf_b[kc]:
            b.w = (l.sem, l.cnt)
    for kc in range(KC):
        for ti, (t0, n) in enumerate(TILES_F):
            P.op("act", lambda e, kc=kc, t0=t0, n=n: e.copy(cx.xb[:, kc, t0:t0 + n], cx.xf[:, kc, t0:t0 + n]),
                 reads=[cx.xf_b[kc][ti]], writes=[cx.xb_b[kc][ti]])
            P.op("dve", lambda e, kc=kc, t0=t0, n=n: e.tensor_scalar(cx.xf[:, kc, t0:t0 + n], cx.xf[:, kc, t0:t0 + n], ALPHA, None, ALU.mult),
                 reads=[cx.xf_b[kc][ti]], writes=[cx.xf_b[kc][ti]])


def ffn(cx, wg_d, wu_d, wd_d):
    P = cx.P
    nc = cx.nc
    with ring(cx), nc.sbuf_tensor(uid(cx, "ffn_h"), [128, 4, NT], BF16) as h, nc.sbuf_tensor(uid(cx, "ffn_s"), [128, 2, 512], F32) as stmp:
        h_b = [[Buf(f"h{i}_{t}") for t in range(3)] for i in range(4)]
        s_b = [Buf("s0"), Buf("s1")]
        s_i = 0
        srcs = []
        for g in range(NG):
            srcs += [(wg_d[g], 2048), (wu_d[g], 2048), (wd_d[g], 2048)]
        ws = WStream(cx, srcs, la=6)
        G = 2
        sgs = [list(range(a, min(a + G, NG))) for a in range(0, NG, G)]
        for si, sg in enumerate(sgs):
            wds = []
            for gi, g in enumerate(sg):
                hi = (si % 2) * 2 + gi
                wg, wg_b = ws.get(3 * g)
                wu, wu_b = ws.get(3 * g + 1)
                wd, wd_b = ws.get(3 * g + 2)
                wds.append((wd, wd_b, hi))
                for ti, (t0, n) in enumerate(TILES_F):
                    gp, gp_b = cx.psA.get()
                    up, up_b = cx.psA.get()
                    for kc in range(KC):
                        P.op("pe", lambda e, kc=kc, gp=gp, wg=wg, t0=t0, n=n: e.matmul(gp[:, 0:n], wg[:, kc * 128:(kc + 1) * 128], cx.xb[:, kc, t0:t0 + n], start=(kc == 0), stop=(kc == KC - 1)),
                             reads=[wg_b, cx.xb_b[kc][ti]], writes=[gp_b])
                    for kc in range(KC):
                        P.op("pe", lambda e, kc=kc, up=up, wu=wu, t0=t0, n=n: e.matmul(up[:, 0:n], wu[:, kc * 128:(kc + 1) * 128], cx.xb[:, kc, t0:t0 + n], start=(kc == 0), stop=(kc == KC - 1)),
                             reads=[wu_b, cx.xb_b[kc][ti]], writes=[up_b])
                    sj = s_i % 2
                    s_i += 1
                    P.op("act", lambda e, sj=sj, gp=gp, n=n: e.activation(stmp[:, sj, 0:n], gp[:, 0:n], AF.Silu), reads=[gp_b], writes=[s_b[sj]])
                    P.op("dve", lambda e, sj=sj, up=up, hi=hi, t0=t0, n=n: e.tensor_tensor(h[:, hi, t0:t0 + n], stmp[:, sj, 0:n], up[:, 0:n], ALU.mult),
                         reads=[s_b[sj], up_b], writes=[h_b[hi][ti]])
            for oc in range(KC):
                for ti, (t0, n) in enumerate(TILES_F):
                    dp, dp_b = cx.psB.get()
                    for j, (wd, wd_b, hi) in enumerate(wds):
                        P.op("pe", lambda e, dp=dp, wd=wd, hi=hi, oc=oc, t0=t0, n=n, j=j: e.matmul(dp[:, 0:n], wd[:, oc * 128:(oc + 1) * 128], h[:, hi, t0:t0 + n], start=(j == 0), stop=(j == len(wds) - 1)),
                             reads=[wd_b, h_b[hi][ti]], writes=[dp_b])
                    P.op("dve", lambda e, dp=dp, oc=oc, t0=t0, n=n: e.scalar_tensor_tensor(cx.xf[:, oc, t0:t0 + n], dp[:, 0:n], 0.5, cx.xf[:, oc, t0:t0 + n], ALU.mult, ALU.add),
                         reads=[dp_b, cx.xf_b[oc][ti]], writes=[cx.xf_b[oc][ti]])
        P.barrier()


def layer_norm(cx, li, final=False):
    P = cx.P
    nc = cx.nc
    with nc.sbuf_tensor(uid(cx, "ln_rb"), [128, 2, KC, 512], BF16) as rb, nc.sbuf_tensor(uid(cx, "ln_rsq"), [128, 2, KC, 512], BF16) as rsq, \
            nc.sbuf_tensor(uid(cx, "ln_st"), [128, 2, 4, 512], F32) as st, nc.sbuf_tensor(uid(cx, "ln_t"), [128, 4, 512], F32) as tt:
        rb_b = [[Buf() for _ in range(KC)] for _ in range(2)]
        rsq_b = [[Buf() for _ in range(KC)] for _ in range(2)]
        st_b = [[Buf() for _ in range(4)] for _ in range(2)]
        t_b = [Buf() for _ in range(4)]
        tc = [0]

        def stats(ti):
            t0, n = TILES_F[ti]
            u = ti % 2
            for kc in range(KC):
                P.op("act", lambda e, kc=kc: e.copy(rb[:, u, kc, 0:n], cx.xf[:, kc, t0:t0 + n]), reads=[cx.xf_b[kc][ti]], writes=[rb_b[u][kc]])
                P.op("dve" if kc % 3 else "pool", lambda e, kc=kc: e.tensor_tensor(rsq[:, u, kc, 0:n], cx.xf[:, kc, t0:t0 + n], cx.xf[:, kc, t0:t0 + n], ALU.mult), reads=[cx.xf_b[kc][ti]], writes=[rsq_b[u][kc]])
            mp, mp_b = cx.psA.get()
            ep, ep_b = cx.psA.get()
            for kc in range(KC):
                P.op("pe", lambda e, kc=kc: e.matmul(mp[:, 0:n], cx.inv2048[:], rb[:, u, kc, 0:n], start=(kc == 0), stop=(kc == KC - 1)), reads=[rb_b[u][kc], cx.c_b], writes=[mp_b])
            for kc in range(KC):
                P.op("pe", lambda e, kc=kc: e.matmul(ep[:, 0:n], cx.inv2048[:], rsq[:, u, kc, 0:n], start=(kc == 0), stop=(kc == KC - 1)), reads=[rsq_b[u][kc], cx.c_b], writes=[ep_b])
            sb_ = st_b[u]
            P.op("act", lambda e: e.copy(st[:, u, 0, 0:n], mp[:, 0:n]), reads=[mp_b], writes=[sb_[0]])
            P.op("dve", lambda e: e.tensor_tensor(st[:, u, 1, 0:n], st[:, u, 0, 0:n], st[:, u, 0, 0:n], ALU.mult), reads=[sb_[0]], writes=[sb_[1]])
            P.op("dve", lambda e: e.tensor_tensor(st[:, u, 1, 0:n], ep[:, 0:n], st[:, u, 1, 0:n], ALU.subtract), reads=[ep_b, sb_[1]], writes=[sb_[1]])
            P.op("dve", lambda e: e.tensor_scalar(st[:, u, 1, 0:n], st[:, u, 1, 0:n], LN_EPS, None, ALU.add), reads=[sb_[1]], writes=[sb_[1]])
            P.op("act", lambda e: e.activation(st[:, u, 2, 0:n], st[:, u, 1, 0:n], AF.Sqrt), reads=[sb_[1]], writes=[sb_[2]])
            P.op("dve", lambda e: e.reciprocal(st[:, u, 2, 0:n], st[:, u, 2, 0:n]), reads=[sb_[2]], writes=[sb_[2]])
            P.op("dve", lambda e: e.scalar_tensor_tensor(st[:, u, 3, 0:n], st[:, u, 0, 0:n], -1.0, st[:, u, 2, 0:n], ALU.mult, ALU.mult), reads=[sb_[0], sb_[2]], writes=[sb_[3]])

        def norm(ti):
            t0, n = TILES_F[ti]
            u = ti % 2
            sb_ = st_b[u]
            for kc in range(KC):
                a = tc[0] % 4
                tc[0] += 1
                P.op("dve", lambda e, kc=kc, a=a: e.tensor_tensor(tt[:, a, 0:n], cx.xf[:, kc, t0:t0 + n], st[:, u, 2, 0:n], ALU.mult), reads=[cx.xf_b[kc][ti], sb_[2]], writes=[t_b[a]])
                P.op("dve" if kc % 3 else "pool", lambda e, a=a: e.tensor_tensor(tt[:, a, 0:n], tt[:, a, 0:n], st[:, u, 3, 0:n], ALU.add), reads=[t_b[a], sb_[3]], writes=[t_b[a]])
                P.op("act", lambda e, kc=kc, a=a: e.activation(cx.xb[:, kc, t0:t0 + n], tt[:, a, 0:n], AF.Identity, bias=cx.lnb[:, li, kc:kc + 1], scale=cx.lng[:, li, kc:kc + 1]),
                     reads=[t_b[a], cx.c_b], writes=[cx.xb_b[kc][ti]])
                if final:
                    P.op("act", lambda e, kc=kc, a=a: e.activation(cx.xf[:, kc, t0:t0 + n], tt[:, a, 0:n], AF.Identity, bias=cx.lnb[:, li, kc:kc + 1], scale=cx.lng[:, li, kc:kc + 1]),
                         reads=[t_b[a], cx.c_b], writes=[cx.xf_b[kc][ti]])
                else:
                    P.op("act", lambda e, kc=kc, a=a: e.activation(cx.xf[:, kc, t0:t0 + n], tt[:, a, 0:n], AF.Identity, bias=cx.lnba[:, li, kc:kc + 1], scale=cx.lnga[:, li, kc:kc + 1]),
                         reads=[t_b[a], cx.c_b], writes=[cx.xf_b[kc][ti]])

        stats(0)
        stats(1)
        norm(0)
        stats(2)
        norm(1)
        norm(2)
        P.barrier()


def store_y(cx, scale=None):
    P = cx.P
    l = P.lane("yout")
    for kc in range(KC):
        P.dma("sp", cx.d["yT"][kc], cx.xf[:, kc, :], l, reads=cx.xf_b[kc])


def run_chains(factories, K):
    free = list(range(K))
    active = []
    it = iter(factories)
    done = False
    while True:
        while free and not done:
            f = next(it, None)
            if f is None:
                done = True
                break
            sl = free.pop(0)
            active.append((f(sl), sl))
        if not active:
            break
        for g, sl in list(active):
            try:
                next(g)
            except StopIteration:
                active.remove((g, sl))
                free.append(sl)


def proj_fm(cx, w, w_b, src, src_b, tiles, consumer, pool=None):
    P = cx.P
    pool = pool or cx.psA
    for ti, (t0, n) in enumerate(tiles):
        ps, ps_b = pool.get()
        for kc in range(KC):
            P.op("pe", lambda e, kc=kc, ps=ps, t0=t0, n=n: e.matmul(ps[:, 0:n], w[:, kc * 128:(kc + 1) * 128], src[:, kc, t0:t0 + n], start=(kc == 0), stop=(kc == KC - 1)),
                 reads=[w_b, src_b[kc][ti]], writes=[ps_b])
        consumer(ti, t0, n, ps, ps_b)


TCH = [(i * 128, 128, i // 4) for i in range(8)] + [(1024, 16, 2)]


def proj_tm(cx, w, w_b, src, src_b, tch, consumer, ncols=128, pool=None):
    P = cx.P
    pool = pool or cx.psA
    for ci, (t0, m, ti) in enumerate(tch):
        ps, ps_b = pool.get()
        for kc in range(KC):
            P.op("pe", lambda e, kc=kc, ps=ps, t0=t0, m=m: e.matmul(ps[0:m, 0:ncols], src[:, kc, t0:t0 + m], w[:, kc * ncols:(kc + 1) * ncols], start=(kc == 0), stop=(kc == KC - 1)),
                 reads=[w_b, src_b[kc][ti]], writes=[ps_b])
        consumer(ci, t0, m, ps, ps_b)


def out_proj_residual(cx, w_d, src, src_b):
    P = cx.P
    with ring(cx):
        ws = WStream(cx, [(w_d[c], 2048) for c in range(KC)], la=8)
        for oc in range(KC):
            w, w_b = ws.get(oc)

            def cons(ti, t0, n, ps, ps_b, oc=oc):
                P.op("dve", lambda e: e.tensor_tensor(cx.xf[:, oc, t0:t0 + n], ps[:, 0:n], cx.xf[:, oc, t0:t0 + n], ALU.add),
                     reads=[ps_b, cx.xf_b[oc][ti]], writes=[cx.xf_b[oc][ti]])
            proj_fm(cx, w, w_b, src, src_b, TILES_F, cons)
        P.barrier()


CF = {"ones": 0, "triinc": 128, "trigt": 256, "sel127": 384, "sel15": 512}
CF_N = 640
CB = {"ident": 0, "ones": 128, "inv2048": 256, "inv128": 384, "triinc": 512, "negtrige": 640, "trilt": 768, "zeros": 896, "negtrilt": 1024}
CB_N = 1152
BIG = 30000.0
SC128 = 128.0 ** -0.5
SC512 = 512.0 ** -0.5
RMS_EPS = 1e-6


def host_consts():
    import numpy as _np
    f = _np.zeros((128, CF_N), _np.float32)
    r = _np.arange(128)
    f[:, 0:128] = 1.0
    f[:, 128:256] = (r[:, None] <= r[None, :])
    f[:, 256:384] = (r[:, None] > r[None, :])
    f[127, 384:512] = 1.0
    f[15, 512:640] = 1.0
    b = _np.zeros((128, CB_N), _np.float32)
    b[:, 0:128] = _np.eye(128)
    b[:, 128:256] = 1.0
    b[:, 256:384] = 1.0 / 2048.0
    b[:, 384:512] = 1.0 / 128.0
    b[:, 512:640] = (r[:, None] <= r[None, :])
    b[:, 640:768] = -1.0 * (r[:, None] >= r[None, :])
    b[:, 768:896] = (r[:, None] < r[None, :])
    b[:, 1024:1152] = -1.0 * (r[:, None] < r[None, :])
    return f, b


def setup_consts2(cx):
    P = cx.P
    sb = cx.sb
    cx.cf = sb("s_cf", [128, CF_N], F32)
    cx.cb = sb("s_cb", [128, CB_N], BF16)
    cx.flag = sb("s_flag", [128, 2], F32)
    P.dma("sp", cx.cf[:], cx.d["cf"], cx.misc_lane, writes=[cx.c_b])
    P.dma("pool", cx.cb[:], cx.d["cb"], cx.misc_lane, writes=[cx.c_b])
    P.dma("sp", cx.flag[:], cx.d["flag"], cx.misc_lane, writes=[cx.c_b])
    cx.cfv = lambda k, m=128, n=128: cx.cf[0:m, CF[k]:CF[k] + n]
    cx.cbv = lambda k, m=128, n=128: cx.cb[0:m, CB[k]:CB[k] + n]


class Stage:
    def __init__(self, cx, name, shape, dt, n=2):
        self.t = cx.sb(name, [128, n] + shape, dt)
        self.n = n
        self.b = [Buf(f"{name}{i}") for i in range(n)]
        self.l = [cx.P.lane(name) for i in range(n)]
        self.i = 0

    def get(self):
        a = self.i % self.n
        self.i += 1
        return a, self.b[a], self.l[a]


def even_mixer(cx, j=0):
    P = cx.P
    nc = cx.nc
    d = cx.d
    esm = contextlib.ExitStack()
    sb = lambda name, shape, dt: esm.enter_context(nc.sbuf_tensor(name, shape, dt))
    cb = cx.cbv
    cf = cx.cfv
    G2 = cx.groups
    merged = cx.xb
    merged_b = cx.xb_b
    lf = sb("ev_lf", [128, 9, 8], F32)
    lf_b = Buf("lf")
    sm = sb("ev_small", [128, 1024], F32)
    sm_b = Buf("sm")
    cum_loc = sm[:, 0:72].rearrange("p (i h) -> p i h", h=8)
    cum_cache = sm[:, 72:136].rearrange("p (i h) -> p i h", h=8)
    ck_rem = sm[:, 136:200].rearrange("p (i h) -> p i h", h=8)
    AT = sm[:, 200:208]
    G_loc = sm[:, 232:296].rearrange("p (i h) -> p i h", h=8)
    cfl = sm[:, 296:360].rearrange("p (i h) -> p i h", h=8)
    bfb = sm[:, 688:696]
    ng = sm[:, 696:704]
    lbt = sb("ev_lb", [128, 2, 1024], F32)
    lb_b = Buf("lb")

    P.op("dve", lambda e: e.memset(lf[:], 0.0), writes=[lf_b])
    P.op("dve", lambda e: e.memset(sm[:], 0.0), writes=[sm_b])
    P.dma("sp", bfb, d["fox_bf"], cx.L(16), writes=[sm_b])
    P.dma("sp", ng, d["hgrn_ng"], cx.L(16), writes=[sm_b])
    P.dma("sp", cfl, d["cflogf"], cx.L(16), writes=[sm_b])
    P.dma("sp", lbt[:, 0:2, :], d["lb_logits"], cx.L(17), writes=[lb_b])
    P.op("dve", lambda e: e.tensor_tensor(lbt[:, 0, :], lbt[:, 0, :], lbt[:, 1, :], ALU.subtract), reads=[lb_b], writes=[lb_b])
    P.op("act", lambda e: e.activation(lbt[:, 0, :], lbt[:, 0, :], AF.Sigmoid), reads=[lb_b], writes=[lb_b])
    P.op("dve", lambda e: e.tensor_scalar(lbt[:, 1, :], lbt[:, 0, :], -1.0, 1.0, ALU.mult, ALU.add), reads=[lb_b], writes=[lb_b])
    lb_bc = lambda h: lbt[:, 0, h * 128:(h + 1) * 128]
    oml_bc = lambda h: lbt[:, 1, h * 128:(h + 1) * 128]

    es1 = contextlib.ExitStack()
    es1.enter_context(ring(cx))
    sb_save = cx.sb
    cx.sb = lambda name, shape, dt: es1.enter_context(nc.sbuf_tensor(name, shape, dt))
    sfm = Stage(cx, "e1_sfm", [NT], F32)
    bfm = Stage(cx, "e1_bfm", [NT], BF16)
    stm = Stage(cx, "e1_stm", [9, 128], F32)
    btm = Stage(cx, "e1_btm", [9, 128], BF16)
    P.op("pool", lambda e: e.memset(stm.t[:], 0.0), writes=stm.b)
    P.op("pool", lambda e: e.memset(btm.t[:], 0.0), writes=btm.b)
    W = d["ev_win"]
    order = []
    for h in range(8):
        order.append(("ka", h, h))
    for h in range(8):
        order.append(("va", h, 8 + h))
    order.append(("fa", 0, 56))
    for h in range(8):
        order.append(("fb", h, 32 + h))
    for h in range(8):
        order.append(("ib", h, 40 + h))
    for h in range(8):
        order.append(("qb", h, 24 + h))
    for h in range(8):
        order.append(("gb", h, 48 + h))
    for h in range(8):
        order.append(("qa", h, 16 + h))
    srcs = [((d["ev_wfa"], 128) if k == "fa" else (W[c], 2048)) for (k, h, c) in order]
    ws = WStream(cx, srcs, la=8)
    TY = {"qb": 0, "fb": 1, "ib": 2}
    for oi, (kind, h, c) in enumerate(order):
        w, w_b = ws.get(oi)
        if kind == "ka":
            a, fb_, fl = sfm.get()
            a2, bb_, bl = bfm.get()

            def cons(ti, t0, n, ps, ps_b):
                P.op("act", lambda e: e.copy(sfm.t[:, a, t0:t0 + n], ps[:, 0:n]), reads=[ps_b], writes=[fb_])
                P.op("dve", lambda e: e.tensor_copy(bfm.t[:, a2, t0:t0 + n], ps[:, 0:n]), reads=[ps_b], writes=[bb_])
            proj_fm(cx, w, w_b, cx.xb, cx.xb_b, TILES_F, cons)
            P.dma("sp", d["fox_kT"][h], sfm.t[:, a, :], fl, reads=[fb_])
            P.dma("sp", d["ka_s"][h], bfm.t[:, a2, :], bl, reads=[bb_])
            P.dma("sp", d["xg_key_in"][h * 128:(h + 1) * 128, :], bfm.t[:, a2, 0:1024], bl, reads=[bb_])
        elif kind == "va":
            a, fb_, fl = stm.get()
            a2, bb_, bl = btm.get()

            def cons(ci, t0, m, ps, ps_b):
                P.op("act", lambda e: e.copy(stm.t[0:m, a, ci, :], ps[0:m, 0:128]), reads=[ps_b], writes=[fb_])
                P.op("dve", lambda e: e.tensor_copy(btm.t[0:m, a2, ci, :], ps[0:m, 0:128]), reads=[ps_b], writes=[bb_])
            proj_tm(cx, w, w_b, cx.xb, cx.xb_b, TCH, cons)
            cs = slice(h * 128, (h + 1) * 128)
            P.dma("sp", d["fox_v"][0:1024, cs].rearrange("(i p) f -> p i f", p=128), stm.t[:, a, 0:8, :], fl, reads=[fb_])
            P.dma("sp", d["fox_v"][1024:1040, cs], stm.t[0:16, a, 8, :], fl, reads=[fb_])
            P.dma("sp", d["va_s"][0:1024, cs].rearrange("(i p) f -> p i f", p=128), btm.t[:, a2, 0:8, :], bl, reads=[bb_])
            P.dma("sp", d["va_s"][1024:1040, cs], btm.t[0:16, a2, 8, :], bl, reads=[bb_])
            P.dma("sp", d["xg_val_in"][0:1024, cs].rearrange("(i p) f -> p i f", p=128), btm.t[:, a2, 0:8, :], bl, reads=[bb_])
        elif kind == "qa":
            a2, bb_, bl = bfm.get()

            def cons(ti, t0, n, ps, ps_b):
                P.op("dve", lambda e: e.tensor_scalar(bfm.t[:, a2, t0:t0 + n], ps[:, 0:n], SC128, None, ALU.mult), reads=[ps_b], writes=[bb_])
            proj_fm(cx, w, w_b, cx.xb, cx.xb_b, TILES_F, cons)
            P.dma("sp", d["qa_s"][h], bfm.t[:, a2, :], bl, reads=[bb_])
        elif kind == "gb":
            a2, bb_, bl = bfm.get()

            def cons(ti, t0, n, ps, ps_b):
                P.op("act", lambda e: e.activation(bfm.t[:, a2, t0:t0 + n], ps[:, 0:n], AF.Silu), reads=[ps_b], writes=[bb_])
            proj_fm(cx, w, w_b, cx.xb, cx.xb_b, TILES_F, cons)
            P.dma("sp", d["hg_s"][h], bfm.t[:, a2, :], bl, reads=[bb_])
        elif kind in TY:
            a, fb_, fl = stm.get()

            def cons(ci, t0, m, ps, ps_b):
                P.op("act", lambda e: e.copy(stm.t[0:m, a, ci, :], ps[0:m, 0:128]), reads=[ps_b], writes=[fb_])
            proj_tm(cx, w, w_b, cx.xb, cx.xb_b, TCH, cons)
            P.dma("sp", d["hraw_s"][h, :, :, TY[kind], :].rearrange("i p f -> p i f"), stm.t[:, a, :, :], fl, reads=[fb_])
        elif kind == "fa":
            def cons(ci, t0, m, ps, ps_b):
                P.op("dve", lambda e: e.tensor_tensor(lf[0:m, ci, :], ps[0:m, 0:8], bfb[0:m, :], ALU.add), reads=[ps_b, sm_b], writes=[lf_b])
            proj_tm(cx, w, w_b, cx.xb, cx.xb_b, TCH, cons, ncols=8)
            P.op("act", lambda e: e.activation(lf[:], lf[:], AF.Exp, scale=-1.0), reads=[lf_b], writes=[lf_b])
            P.op("act", lambda e: e.activation(lf[:], lf[:], AF.Ln, bias=1.0), reads=[lf_b], writes=[lf_b])
            P.op("dve", lambda e: e.tensor_scalar(lf[:], lf[:], -1.0, None, ALU.mult), reads=[lf_b], writes=[lf_b])
            P.dma("sp", d["fox_logf"][0:1024, :].rearrange("(i p) h -> p i h", p=128), lf[:, 0:8, :], cx.L(18), reads=[lf_b])
            P.dma("sp", d["fox_logf"][1024:1040, :], lf[0:16, 8, :], cx.L(18), reads=[lf_b])
            cp, cp_b = cx.psB.get()
            for i in range(8):
                for i2 in range(i):
                    P.op("pe", lambda e, i=i, i2=i2: e.matmul(cp[:, i * 8:(i + 1) * 8], cf("ones"), lf[:, i2, :], start=(i2 == 0), stop=False), reads=[lf_b, cx.c_b], writes=[cp_b])
                P.op("pe", lambda e, i=i: e.matmul(cp[:, i * 8:(i + 1) * 8], cf("triinc"), lf[:, i, :], start=(i == 0), stop=True), reads=[lf_b, cx.c_b], writes=[cp_b])
            P.op("dve", lambda e: e.tensor_copy(sm[:, 0:64], cp[:, 0:64]), reads=[cp_b], writes=[sm_b])
            cp2, cp2_b = cx.psB.get()
            for i in range(8):
                for i2 in range(i):
                    P.op("pe", lambda e, i=i, i2=i2: e.matmul(cp2[:, i * 8:(i + 1) * 8], cf("ones"), cfl[:, i2, :], start=(i2 == 0), stop=False), reads=[sm_b, cx.c_b], writes=[cp2_b])
                P.op("pe", lambda e, i=i: e.matmul(cp2[:, i * 8:(i + 1) * 8], cf("triinc"), cfl[:, i, :], start=(i == 0), stop=True), reads=[sm_b, cx.c_b], writes=[cp2_b])
            for i2 in range(8):
                P.op("pe", lambda e, i2=i2: e.matmul(cp2[0:16, 64:72], cf("ones", 128, 16), cfl[:, i2, :], start=(i2 == 0), stop=False), reads=[sm_b, cx.c_b], writes=[cp2_b])
            P.op("pe", lambda e: e.matmul(cp2[0:16, 64:72], cf("triinc", 16, 16), lf[0:16, 8, :], start=False, stop=True), reads=[lf_b, cx.c_b], writes=[cp2_b])
            P.op("dve", lambda e: e.tensor_copy(sm[:, 72:136], cp2[:, 0:64]), reads=[cp2_b], writes=[sm_b])
            P.op("dve", lambda e: e.tensor_copy(sm[0:16, 64:72], cp2[0:16, 64:72]), reads=[cp2_b], writes=[sm_b])
            P.dma("sp", d["xg_c_in"], sm[:, 0:64], cx.L(19), reads=[sm_b])
    es1.close()
    cx.sb = sb_save
    P.barrier()
    if getattr(cx, "stop", "") == "e1":
        return
    P.coll("AllGather", [d["xg_key_in_t"].ap().opt()], [d["xg_key_out_t"].ap().opt()], G2, cx.cc_lane, writes=[cx.xgo_b])
    P.coll("AllGather", [d["xg_val_in_t"].ap().opt()], [d["xg_val_out_t"].ap().opt()], G2, cx.cc_lane, writes=[cx.xgo_b])
    P.coll("AllGather", [d["xg_c_in_t"].ap().opt()], [d["xg_c_out_t"].ap().opt()], G2, cx.cc_lane, writes=[cx.xgco_b])
    hgrn_pass(cx, True, lb_bc, oml_bc, lb_b, ng, sm_b, merged, merged_b)
    P.barrier()
    P.coll("AllGather", [d["xg_s_in_t"].ap().opt()], [d["xg_s_out_t"].ap().opt()], G2, cx.cc_lane, writes=[cx.xgso_b])
    P.dma("sp", sm[:, 136:200], d["xg_c_out"][0:128, :], cx.L(19), reads=[cx.xgco_b], writes=[sm_b])
    tp, tp_b = cx.psB.get()
    P.op("pe", lambda e: e.matmul(tp[:, 0:8], cf("sel127"), ck_rem[:, 7, :], start=True, stop=True), reads=[sm_b, cx.c_b], writes=[tp_b])
    P.op("dve", lambda e: e.tensor_scalar(AT, tp[:, 0:8], cx.flag[:, 0:1], None, ALU.mult), reads=[tp_b, cx.c_b], writes=[sm_b])
    for i in range(8):
        P.op("dve", lambda e, i=i: e.tensor_tensor(G_loc[:, i, :], cum_loc[:, i, :], AT, ALU.add), reads=[sm_b], writes=[sm_b])
    ft = sb("ev_ft", [128, 1200], F32)
    ft_b = sm_b
    fb_loc = ft[:, 0:512].rearrange("p (h j i) -> p h j i", h=8, j=8)
    fb_rem = ft[:, 512:1024].rearrange("p (h j i) -> p h j i", h=8, j=8)
    fb_cache = ft[:, 1024:1088].rearrange("p (h i) -> p h i", h=8)
    fb_sloc = ft[:, 1088:1096]
    cref = ft[:, 1100:1172].rearrange("p (j h) -> p j h", h=8)
    P.op("dve", lambda e: e.memset(ft[:], 0.0), writes=[sm_b])
    tp, tp_b = cx.psB.get()
    for jq in range(8):
        P.op("pe", lambda e, jq=jq: e.matmul(tp[:, jq * 8:(jq + 1) * 8], cf("sel127"), G_loc[:, jq, :], start=True, stop=True), reads=[sm_b, cx.c_b], writes=[tp_b])
    P.op("pe", lambda e: e.matmul(tp[:, 64:72], cf("sel15", 16, 128), cum_loc[0:16, 8, :], start=True, stop=True), reads=[sm_b, cx.c_b], writes=[tp_b])
    P.op("dve", lambda e: e.tensor_copy(ft[:, 1100:1172], tp[:, 0:72]), reads=[tp_b], writes=[sm_b])
    for h in range(8):
        for jq in range(8):
            ni = jq + 1
            P.op("dve", lambda e, h=h, jq=jq, ni=ni: e.tensor_scalar(fb_loc[:, h, jq, 0:ni], G_loc[:, 0:ni, h], cref[:, jq, h:h + 1], -1.0, ALU.subtract, ALU.mult), reads=[sm_b], writes=[sm_b])
            P.op("dve", lambda e, h=h, jq=jq: e.tensor_scalar(fb_rem[:, h, jq, :], ck_rem[:, :, h], cref[:, jq, h:h + 1], -1.0, ALU.subtract, ALU.mult), reads=[sm_b], writes=[sm_b])
            P.op("dve", lambda e, h=h, jq=jq: e.tensor_scalar(fb_rem[:, h, jq, :], fb_rem[:, h, jq, :], cx.flag[:, 1:2], None, ALU.add), reads=[sm_b, cx.c_b], writes=[sm_b])
        P.op("dve", lambda e, h=h: e.tensor_scalar(fb_cache[:, h, :], cum_cache[:, :, h], cref[:, 8, h:h + 1], -1.0, ALU.subtract, ALU.mult), reads=[sm_b], writes=[sm_b])
        P.op("dve", lambda e, h=h: e.tensor_scalar(fb_sloc[0:16, h:h + 1], cum_loc[0:16, 8, h:h + 1], cref[0:16, 8, h:h + 1], -1.0, ALU.subtract, ALU.mult), reads=[sm_b], writes=[sm_b])
    P.barrier()
    if getattr(cx, "stop", "") == "x1":
        return
    tabs = dict(fb_loc=fb_loc, fb_rem=fb_rem, fb_cache=fb_cache, fb_sloc=fb_sloc, sm_b=sm_b)
    if getattr(cx, "stop", "") == "h1":
        return
    fox_heads(cx, tabs, merged, merged_b)
    P.barrier()
    if getattr(cx, "stop", "") == "fox":
        return
    hgrn_pass(cx, False, lb_bc, oml_bc, lb_b, ng, sm_b, merged, merged_b)
    P.barrier()
    esm.close()
    out_proj_residual(cx, d["ev_wout"], merged, merged_b)


def hgrn_pass(cx, state_only, lb_bc, oml_bc, lb_b, ng, sm_b, merged, merged_b):
    P = cx.P
    nc = cx.nc
    d = cx.d
    cb = cx.cbv
    cf = cx.cfv
    sfx = "1" if state_only else "2"
    with contextlib.ExitStack() as es:
        sbl = lambda name, shape, dt: es.enter_context(nc.sbuf_tensor(name + sfx, shape, dt))
        raw = sbl("hg_raw", [128, 9, 3, 128], F32); raw_b = Buf()
        f_ = sbl("hg_f", [128, 9, 128], F32); f_b = Buf()
        lg = sbl("hg_lg", [128, 9, 128], F32); lg_b = Buf()
        erb = sbl("hg_erb", [128, 9, 128], F32); erb_b = Buf()
        Kh = sbl("hg_Kh", [128, 9, 128], BF16); Kh_b = Buf()
        ib = sbl("hg_ib", [128, 9, 128], BF16); ib_b = Buf()
        S = sbl("hg_S", [128, 128], F32); S_b = Buf()
        Sb = sbl("hg_Sb", [128, 128], BF16); Sb_b = Buf()
        el = sbl("hg_el", [128, 16], F32); el_b = Buf()
        if not state_only:
            eb = sbl("hg_eb", [128, 9, 128], F32); eb_b = Buf()
            enb = sbl("hg_enb", [128, 9, 128], F32); enb_b = Buf()
            qs = lg; qs_b = lg_b
            Qt = sbl("hg_Qt", [128, 9, 128], BF16); Qt_b = Buf()
            Kt = sbl("hg_Kt", [128, 9, 128], BF16); Kt_b = Buf()
            QtT = sbl("hg_QtT", [128, 9, 128], BF16); QtT_b = Buf()
            KtT = sbl("hg_KtT", [128, 9, 128], BF16); KtT_b = Buf()
            sc = sbl("hg_sc", [128, 9, 128], BF16); sc_b = Buf()
            Sball = sbl("hg_Sball", [128, 9, 128], BF16); Sball_b = Buf()
            ob = sbl("hg_ob", [128, NT], F32); ob_b = Buf()
            gt = sbl("hg_g", [128, NT], BF16); gt_b = Buf()
            sq = sbl("hg_sq", [128, 512], BF16); sq_b = Buf()
            rs = sbl("hg_rs", [128, 512], F32); rs_b = Buf()
            P.op("pool", lambda e: e.memset(sc[:], 0.0), writes=[sc_b])
        l_raw, l_S, l_sin, l_g = cx.L(14), cx.L(15), cx.L(16), cx.L(17)
        GR = [(0, 4), (4, 8), (8, 9)]
        nblk = 8 if state_only else 9
        for h in range(8):
            P.dma("sp", raw[:], d["hraw_s"][h].rearrange("i p t f -> p i t f"), l_raw, writes=[raw_b])
            if not state_only:
                P.dma("sp", gt[:], d["hg_s"][h], l_g, writes=[gt_b])
            P.op("act", lambda e: e.activation(f_[:], raw[:, :, 1, :], AF.Sigmoid), reads=[raw_b], writes=[f_b])
            for bi in range(9):
                P.op("dve", lambda e, bi=bi: e.tensor_tensor(f_[:, bi, :], f_[:, bi, :], oml_bc(h), ALU.mult), reads=[f_b, lb_b], writes=[f_b])
                P.op("dve", lambda e, bi=bi: e.tensor_tensor(f_[:, bi, :], f_[:, bi, :], lb_bc(h), ALU.add), reads=[f_b, lb_b], writes=[f_b])
            P.op("act", lambda e: e.activation(lg[:], f_[:], AF.Ln), reads=[f_b], writes=[lg_b])
            P.op("dve", lambda e: e.tensor_scalar(f_[:], f_[:], -1.0, 1.0, ALU.mult, ALU.add), reads=[f_b], writes=[f_b])
            P.op("pool", lambda e: e.tensor_copy(ib[:], raw[:, :, 2, :]), reads=[raw_b], writes=[ib_b])
            tp, tp_b = cx.psB.get()
            for (g0, g1) in GR:
                rp, rp_b = cx.psA.get()
                if not state_only:
                    bp, bp_b = cx.psA.get()
                for bi in range(g0, g1):
                    m = 128 if bi < 8 else 16
                    c0 = (bi - g0) * 128
                    P.op("pe", lambda e, bi=bi, m=m, c0=c0: e.matmul(rp[0:m, c0:c0 + 128], cf("trigt", m, m), lg[0:m, bi, :], start=True, stop=True), reads=[lg_b, cx.c_b], writes=[rp_b])
                    if not state_only:
                        P.op("pe", lambda e, bi=bi, m=m, c0=c0: e.matmul(bp[0:m, c0:c0 + 128], cf("triinc", m, m), lg[0:m, bi, :], start=True, stop=True), reads=[lg_b, cx.c_b], writes=[bp_b])
                    P.op("pe", lambda e, bi=bi, m=m: e.matmul(tp[:, bi:bi + 1], lg[0:m, bi, :], cf("ones", m, 1), start=True, stop=True), reads=[lg_b, cx.c_b], writes=[tp_b])
                ncol = (g1 - g0) * 128
                P.op("act", lambda e, g0=g0, g1=g1, ncol=ncol: e.activation(erb[:, g0:g1, :], rp[:, 0:ncol].rearrange("p (i f) -> p i f", f=128), AF.Exp), reads=[rp_b], writes=[erb_b])
                if not state_only:
                    P.op("act", lambda e, g0=g0, g1=g1, ncol=ncol: e.activation(eb[:, g0:g1, :], bp[:, 0:ncol].rearrange("p (i f) -> p i f", f=128), AF.Exp), reads=[bp_b], writes=[eb_b])
                    P.op("act", lambda e, g0=g0, g1=g1, ncol=ncol: e.activation(enb[:, g0:g1, :], bp[:, 0:ncol].rearrange("p (i f) -> p i f", f=128), AF.Exp, scale=-1.0), reads=[bp_b], writes=[enb_b])
            P.op("act", lambda e: e.activation(el[:, 0:9], tp[:, 0:9], AF.Exp), reads=[tp_b], writes=[el_b])
            if not state_only:
                P.op("act", lambda e: e.activation(qs[:], raw[:, :, 0, :], AF.Silu), reads=[raw_b], writes=[qs_b])
            P.op("pool", lambda e: e.tensor_tensor(Kh[:], f_[:], erb[:], ALU.mult), reads=[f_b, erb_b], writes=[Kh_b])
            if not state_only:
                P.op("dve", lambda e: e.tensor_tensor(Qt[:], qs[:], eb[:], ALU.mult), reads=[qs_b, eb_b], writes=[Qt_b])
                P.op("dve", lambda e: e.tensor_tensor(Kt[:], f_[:], enb[:], ALU.mult), reads=[f_b, enb_b], writes=[Kt_b])
                for (g0, g1) in GR:
                    ncol = (g1 - g0) * 128
                    qp, qp_b = cx.psA.get()
                    kp, kp_b = cx.psA.get()
                    for bi in range(g0, g1):
                        m = 128 if bi < 8 else 16
                        c0 = (bi - g0) * 128
                        P.op("pe", lambda e, bi=bi, m=m, c0=c0: e.matmul(qp[:, c0:c0 + m], Qt[0:m, bi, :], cb("ident", m, m), start=True, stop=True), reads=[Qt_b, cx.c_b], writes=[qp_b])
                        P.op("pe", lambda e, bi=bi, m=m, c0=c0: e.matmul(kp[:, c0:c0 + m], Kt[0:m, bi, :], cb("ident", m, m), start=True, stop=True), reads=[Kt_b, cx.c_b], writes=[kp_b])
                    if g0 < 8:
                        P.op("act", lambda e, g0=g0, g1=g1, ncol=ncol: e.copy(QtT[:, g0:g1, :], qp[:, 0:ncol].rearrange("p (i f) -> p i f", f=128)), reads=[qp_b], writes=[QtT_b])
                        P.op("dve", lambda e, g0=g0, g1=g1, ncol=ncol: e.tensor_copy(KtT[:, g0:g1, :], kp[:, 0:ncol].rearrange("p (i f) -> p i f", f=128)), reads=[kp_b], writes=[KtT_b])
                    else:
                        P.op("act", lambda e: e.copy(QtT[:, 8, 0:16], qp[:, 0:16]), reads=[qp_b], writes=[QtT_b])
                        P.op("dve", lambda e: e.tensor_copy(KtT[:, 8, 0:16], kp[:, 0:16]), reads=[kp_b], writes=[KtT_b])
                for (g0, g1) in GR:
                    sp_, sp_b = cx.psA.get()
                    for bi in range(g0, g1):
                        m = 128 if bi < 8 else 16
                        c0 = (bi - g0) * 128
                        P.op("pe", lambda e, bi=bi, m=m, c0=c0: e.matmul(sp_[0:m, c0:c0 + m], KtT[:, bi, 0:m], QtT[:, bi, 0:m], start=True, stop=True), reads=[KtT_b, QtT_b], writes=[sp_b])
                    for bi in range(g0, g1):
                        m = 128 if bi < 8 else 16
                        c0 = (bi - g0) * 128
                        P.op("dve", lambda e, bi=bi, m=m, c0=c0: e.tensor_tensor(sc[0:m, bi, 0:m], sp_[0:m, c0:c0 + m], cb("triinc", m, m), ALU.mult), reads=[sp_b, cx.c_b], writes=[sc_b])
            s2t = []
            for g3 in range(3):
                s2t.append(cx.psB.get())
            for bi in range(nblk):
                m = 128 if bi < 8 else 16
                s2, s2_b = s2t[bi // 4]
                c0 = (bi % 4) * 128
                P.op("pe", lambda e, bi=bi, m=m, c0=c0: e.matmul(s2[:, c0:c0 + 128], Kh[0:m, bi, :], ib[0:m, bi, :], start=True, stop=True), reads=[Kh_b, ib_b], writes=[s2_b])
            for bi in range(nblk):
                s2, s2_b = s2t[bi // 4]
                c0 = (bi % 4) * 128
                if bi == 0:
                    if state_only:
                        P.op("dve", lambda e: e.memset(S[:], 0.0), writes=[S_b])
                    else:
                        P.dma("sp", S[:], d["xg_s_out"][h * 128:(h + 1) * 128, :], l_sin, reads=[cx.xgso_b], writes=[S_b])
                        P.op("dve", lambda e: e.tensor_scalar(S[:], S[:], cx.flag[:, 0:1], None, ALU.mult), reads=[S_b, cx.c_b], writes=[S_b])
                if bi == 8:
                    P.dma("sp", S[:], d["hstate_in"][h], l_sin, writes=[S_b])
                if not state_only:
                    P.op("act", lambda e, bi=bi: e.copy(Sball[:, bi, :], S[:]), reads=[S_b], writes=[Sball_b])
                P.op("dve", lambda e, bi=bi, c0=c0: e.scalar_tensor_tensor(S[:], S[:], el[:, bi:bi + 1], s2[:, c0:c0 + 128], ALU.mult, ALU.add), reads=[S_b, el_b, s2_b], writes=[S_b])
                if bi == 7:
                    if state_only:
                        P.dma("sp", d["xg_s_in"][h * 128:(h + 1) * 128, :], S[:], l_S, reads=[S_b])
                    else:
                        P.dma("sp", d["hstate_p"][h], S[:], l_S, reads=[S_b])
                if bi == 8:
                    P.dma("sp", d["hstate_s"][h], S[:], l_S, reads=[S_b])
            if not state_only:
                for bi in range(nblk):
                    m = 128 if bi < 8 else 16
                    t0 = bi * 128
                    op_, op_b = cx.psA.get()
                    P.op("pe", lambda e, bi=bi, m=m: e.matmul(op_[:, 0:m], ib[0:m, bi, :], sc[0:m, bi, 0:m], start=True, stop=False), reads=[ib_b, sc_b], writes=[op_b])
                    P.op("pe", lambda e, bi=bi, m=m: e.matmul(op_[:, 0:m], Sball[:, bi, :], QtT[:, bi, 0:m], start=False, stop=True), reads=[Sball_b, QtT_b], writes=[op_b])
                    P.op("act", lambda e, m=m, t0=t0: e.copy(ob[:, t0:t0 + m], op_[:, 0:m]), reads=[op_b], writes=[ob_b])
            if state_only:
                continue
            for ti, (t0, n) in enumerate(TILES):
                P.op("pool", lambda e, t0=t0, n=n: e.tensor_tensor(sq[:, 0:n], ob[:, t0:t0 + n], ob[:, t0:t0 + n], ALU.mult), reads=[ob_b], writes=[sq_b])
                mp, mp_b = cx.psA.get()
                P.op("pe", lambda e, n=n: e.matmul(mp[:, 0:n], cb("inv128"), sq[:, 0:n], start=True, stop=True), reads=[sq_b, cx.c_b], writes=[mp_b])
                P.op("dve", lambda e, n=n: e.tensor_scalar(rs[:, 0:n], mp[:, 0:n], RMS_EPS, None, ALU.add), reads=[mp_b], writes=[rs_b])
                P.op("act", lambda e, n=n: e.activation(rs[:, 0:n], rs[:, 0:n], AF.Sqrt), reads=[rs_b], writes=[rs_b])
                P.op("dve", lambda e, n=n: e.reciprocal(rs[:, 0:n], rs[:, 0:n]), reads=[rs_b], writes=[rs_b])
                P.op("dve", lambda e, t0=t0, n=n: e.tensor_tensor(rs[:, 0:n], ob[:, t0:t0 + n], rs[:, 0:n], ALU.mult), reads=[ob_b, rs_b], writes=[rs_b])
                P.op("dve", lambda e, t0=t0, n=n: e.scalar_tensor_tensor(merged[:, 8 + h, t0:t0 + n], rs[:, 0:n], ng[:, h:h + 1], gt[:, t0:t0 + n], ALU.mult, ALU.mult),
                     reads=[rs_b, sm_b, gt_b], writes=[merged_b[8 + h][ti]])


def fox_heads(cx, tabs, merged, merged_b):
    P = cx.P
    nc = cx.nc
    d = cx.d
    cb = cx.cbv
    fb_loc, fb_rem, fb_cache, fb_sloc, sm_b = tabs["fb_loc"], tabs["fb_rem"], tabs["fb_cache"], tabs["fb_sloc"], tabs["sm_b"]
    K = 2
    with contextlib.ExitStack() as es:
        sbl = lambda name, shape, dt: es.enter_context(nc.sbuf_tensor(name, shape, dt))
        NS = 3
        qT = sbl("fx_qT", [128, NS, NT], BF16)
        kT = sbl("fx_kT", [128, NS, NT], BF16)
        vl = sbl("fx_vl", [128, NS, 9, 128], BF16)
        kTr = sbl("fx_kTr", [128, NS, 1024], BF16)
        vr = sbl("fx_vr", [128, NS, 8, 128], BF16)
        kTc = sbl("fx_kTc", [128, NS, 1024], BF16)
        vc = sbl("fx_vc", [128, NS, 8, 128], BF16)
        in_b = [[Buf() for _ in range(7)] for _ in range(NS)]
        pp = sbl("fx_p", [128, K, 2, 512], BF16)
        pp_b = [[Buf(), Buf()] for _ in range(K)]
        rd = sbl("fx_rd", [128, K, 512], F32)
        rd_b = [Buf() for _ in range(K)]
        spools = [PsPool(cx.ps[4 * k + 2:4 * k + 4]) for k in range(K)]
        loaded = set()

        def load_head(h):
            if h in loaded:
                return
            loaded.add(h)
            s = h % NS
            ib_ = in_b[s]
            L = lambda i: cx.lanes2_[s * 7 + i]
            cs = slice(h * 128, (h + 1) * 128)
            P.dma("sp", qT[:, s, :], d["qa_s"][h], L(0), writes=[ib_[0]])
            P.dma("sp", kT[:, s, :], d["ka_s"][h], L(1), writes=[ib_[1]])
            P.dma("sp", vl[:, s, 0:8, :], d["va_s"][0:1024, cs].rearrange("(i p) f -> p i f", p=128), L(2), writes=[ib_[2]])
            P.dma("sp", vl[0:16, s, 8, :], d["va_s"][1024:1040, cs], L(2), writes=[ib_[2]])
            P.dma("sp", kTr[:, s, :], d["xg_key_out"][h * 128:(h + 1) * 128, :], L(3), reads=[cx.xgo_b], writes=[ib_[3]])
            P.dma("sp", vr[:, s, :, :], d["xg_val_out"][0:1024, cs].rearrange("(i p) f -> p i f", p=128), L(4), reads=[cx.xgo_b], writes=[ib_[4]])
            P.dma("pool", kTc[:, s, :], d["cfkT"][h], L(5), writes=[ib_[5]])
            P.dma("pool", vc[:, s, :, :], d["cfv"][:, cs].rearrange("(i p) f -> p i f", p=128), L(6), writes=[ib_[6]])

        def chain(h, ti, sl):
            load_head(h)
            if h + 1 < 8:
                load_head(h + 1)
            s = h % NS
            ib_ = in_b[s]
            t0, n = TILES[ti]
            if ti < 2:
                chunks = [("rem", i) for i in range(8)] + [("loc", i) for i in range(4 * ti + 4)]
            else:
                chunks = [("cache", i) for i in range(8)] + [("sloc", 8)]
            den, den_b = cx.ps[4 * sl]
            oT, oT_b = cx.ps[4 * sl + 1]
            spool = spools[sl]
            nch = len(chunks)

            def info(ci):
                kind, i = chunks[ci]
                m, c0 = 128, 0
                if kind == "rem":
                    kap, kb_, vap, vb_ = kTr[:, s, i * 128:(i + 1) * 128], ib_[3], vr[:, s, i, :], ib_[4]
                elif kind == "loc":
                    kap, kb_, vap, vb_ = kT[:, s, i * 128:(i + 1) * 128], ib_[1], vl[:, s, i, :], ib_[2]
                    c0 = max(0, (i - 4 * ti) * 128)
                elif kind == "cache":
                    kap, kb_, vap, vb_ = kTc[:, s, i * 128:(i + 1) * 128], ib_[5], vc[:, s, i, :], ib_[6]
                else:
                    m = 16
                    kap, kb_, vap, vb_ = kT[:, s, 1024:1040], ib_[1], vl[0:16, s, 8, :], ib_[2]
                return kind, i, kap, kb_, vap, vb_, m, c0

            sps = {}

            def fst(ci):
                kind, i, kap, kb_, vap, vb_, m, c0 = info(ci)
                sp_, sp_b = spool.get()
                sps[ci] = (sp_, sp_b)
                P.op("pe", lambda e: e.matmul(sp_[0:m, c0:n], kap, qT[:, s, t0 + c0:t0 + n], start=True, stop=True), reads=[kb_, ib_[0]], writes=[sp_b])

            def est(ci):
                kind, i, kap, kb_, vap, vb_, m, c0 = info(ci)
                sp_, sp_b = sps.pop(ci)
                a = ci % 2
                pb = pp_b[sl][a]
                if ti < 2:
                    for sq in range(c0 // 128, 4):
                        jq = 4 * ti + sq
                        cc = slice(sq * 128, (sq + 1) * 128)
                        bias = fb_rem[:, h, jq, i:i + 1] if kind == "rem" else fb_loc[:, h, jq, i:i + 1]
                        P.op("act", lambda e: e.activation(pp[:, sl, a, cc], sp_[:, cc], AF.Exp, bias=bias), reads=[sp_b, sm_b], writes=[pb])
                        if kind == "loc" and i == jq:
                            P.op("pool", lambda e: e.tensor_tensor(pp[:, sl, a, cc], pp[:, sl, a, cc], cb("triinc"), ALU.mult), reads=[pb, cx.c_b], writes=[pb])
                else:
                    bias = fb_cache[:, h, i:i + 1] if kind == "cache" else fb_sloc[0:16, h:h + 1]
                    P.op("act", lambda e: e.activation(pp[0:m, sl, a, 0:n], sp_[0:m, 0:n], AF.Exp, bias=bias), reads=[sp_b, sm_b], writes=[pb])
                    if kind == "sloc":
                        P.op("pool", lambda e: e.tensor_tensor(pp[0:16, sl, a, 0:16], pp[0:16, sl, a, 0:16], cb("triinc", 16, 16), ALU.mult), reads=[pb, cx.c_b], writes=[pb])

            def gst(ci):
                kind, i, kap, kb_, vap, vb_, m, c0 = info(ci)
                a = ci % 2
                pb = pp_b[sl][a]
                P.op("pe", lambda e: e.matmul(den[:, c0:n], cb("ones", m, 128), pp[0:m, sl, a, c0:n], start=(ci == 0), stop=(ci == nch - 1)), reads=[pb, cx.c_b], writes=[den_b])
                P.op("pe", lambda e: e.matmul(oT[:, c0:n], vap, pp[0:m, sl, a, c0:n], start=(ci == 0), stop=(ci == nch - 1)), reads=[pb, vb_], writes=[oT_b])

            fst(0)
            yield
            for ci in range(nch):
                if ci + 1 < nch:
                    fst(ci + 1)
                est(ci)
                yield
                gst(ci)
            yield
            P.op("dve", lambda e: e.reciprocal(rd[:, sl, 0:n], den[:, 0:n]), reads=[den_b], writes=[rd_b[sl]])
            P.op("dve", lambda e: e.tensor_tensor(merged[:, h, t0:t0 + n], oT[:, 0:n], rd[:, sl, 0:n], ALU.mult), reads=[oT_b, rd_b[sl]], writes=[merged_b[h][ti]])

        facs = []
        for h in range(8):
            for ti in (1, 0, 2):
                facs.append(lambda sl, h=h, ti=ti: chain(h, ti, sl))
        run_chains(facs, K)


def tile_cols(W):
    K, N = W.shape
    kc = K // 128
    nch = N // 128
    return np.ascontiguousarray(W.reshape(kc, 128, nch, 128).transpose(2, 1, 0, 3).reshape(nch, 128, kc * 128))


IN_SPECS = {
    "xT": ([KC, 128, NT], F32), "lng": ([128, 8, KC], F32), "lnb": ([128, 8, KC], F32),
    "cf": ([128, CF_N], F32), "cb": ([128, CB_N], F32), "flag": ([128, 2], F32),
    "ev_win": ([56, 128, 2048], F32), "ev_wfa": ([128, 128], F32), "ev_wout": ([16, 128, 2048], F32),
    "fox_bf": ([128, 8], F32), "hgrn_ng": ([128, 8], F32), "cflogf": ([128, 8, 8], F32), "lb_logits": ([128, 2, 1024], F32),
    "cfkT": ([8, 128, 1024], F32), "cfv": ([1024, 1024], F32), "hstate_in": ([8, 128, 128], F32),
    "od_win": ([48, 128, 2048], F32), "od_wout": ([16, 128, 2048], F32),
    "cskT": ([16, 128, 1024], F32), "csv": ([1024, 2048], F32),
    "memT": ([KC, 128, 256], F32),
}
for _l in range(2):
    for _f in (1, 2):
        for _n in ("wg", "wu", "wd"):
            IN_SPECS[f"{_n}{_l}{_f}"] = ([NG, 128, 2048], F32)
    IN_SPECS[f"xwq{_l}"] = ([16, 128, 2048], F32)
    IN_SPECS[f"xwkv{_l}"] = ([32, 128, 2048], F32)
    IN_SPECS[f"xwo{_l}"] = ([16, 128, 2048], F32)
    IN_SPECS[f"cmkT{_l}"] = ([16, 128, 256], F32)
    IN_SPECS[f"cmv{_l}"] = ([256, 2048], F32)
OUT_SPECS = {
    "yT": ([KC, 128, NT], F32), "fox_kT": ([8, 128, NT], F32), "fox_v": ([NT, 1024], F32), "fox_logf": ([NT, 8], F32),
    "hstate_p": ([8, 128, 128], F32), "hstate_s": ([8, 128, 128], F32),
    "sb_kT": ([16, 128, NT], F32), "sb_v": ([NT, 2048], F32),
    "mem_kT0": ([16, 128, 256], F32), "mem_v0": ([256, 2048], F32), "mem_kT1": ([16, 128, 256], F32), "mem_v1": ([256, 2048], F32),
}
INT_SPECS = {
    "ka_s": ([8, 128, NT], BF16), "va_s": ([NT, 1024], BF16), "qa_s": ([8, 128, NT], BF16), "hg_s": ([8, 128, NT], BF16),
    "hraw_s": ([8, 9, 128, 3, 128], F32),
    "xg_key_in": ([1024, 1024], BF16), "xg_key_out": ([2048, 1024], BF16),
    "xg_val_in": ([1024, 1024], BF16), "xg_val_out": ([2048, 1024], BF16),
    "xg_c_in": ([128, 64], F32), "xg_c_out": ([256, 64], F32),
    "xg_s_in": ([1024, 128], F32), "xg_s_out": ([2048, 128], F32),
    "sq_s": ([16, 128, NT], BF16), "sk_s": ([16, 128, NT], BF16), "sv_s": ([NT, 2048], BF16),
    "xg2k0_in": ([1024, 1024], BF16), "xg2k0_out": ([2048, 1024], BF16),
    "xg2k1_in": ([1024, 1024], BF16), "xg2k1_out": ([2048, 1024], BF16),
    "xg2v0_in": ([1024, 1024], BF16), "xg2v0_out": ([2048, 1024], BF16),
    "xg2v1_in": ([1024, 1024], BF16), "xg2v1_out": ([2048, 1024], BF16),
}


def declare(cx, nc, ins=None, outs=None):
    cx.d = {}
    for k, (shape, dt) in IN_SPECS.items():
        if ins is None or k in ins:
            cx.d[k] = nc.dram_tensor(k, shape, dt, kind="ExternalInput").ap()
    for k, (shape, dt) in OUT_SPECS.items():
        if outs is None or k in outs:
            cx.d[k] = nc.dram_tensor(k, shape, dt, kind="ExternalOutput").ap()
    for k, (shape, dt) in INT_SPECS.items():
        t = nc.dram_tensor(k, shape, dt)
        cx.d[k + "_t"] = t
        cx.d[k] = t.ap()


def host_shared(I):
    S = {}
    S["lng"] = np.ascontiguousarray(I["ln_g"].reshape(8, KC, 128).transpose(2, 0, 1))
    S["lnb"] = np.ascontiguousarray(I["ln_b"].reshape(8, KC, 128).transpose(2, 0, 1))
    S["cf"], S["cb"] = host_consts()
    W = I["ev_w_in"][0]
    o = {"qa": 0, "ka": 1024, "va": 2048, "fa": 3072, "qb": 3080, "fb": 4104, "ib": 5128, "gb": 6152}
    cat = np.concatenate([W[:, o[k]:o[k] + 1024] for k in ("ka", "va", "qa", "qb", "fb", "ib", "gb")], axis=1)
    S["ev_win"] = tile_cols(cat)
    S["ev_wfa"] = np.ascontiguousarray(W[:, 3072:3080].reshape(KC, 128, 8).transpose(1, 0, 2).reshape(128, 128))
    S["ev_wout"] = tile_cols(I["ev_w_out"][0])
    S["fox_bf"] = np.ascontiguousarray(np.broadcast_to(I["fox_b_f"][0][None, :], (128, 8)))
    S["hgrn_ng"] = np.ascontiguousarray(I["hgrn_norm_g"][0].reshape(8, 128).T)
    S["lb_logits"] = np.ascontiguousarray(np.broadcast_to(I["hgrn_lb_logits"][None, :, :], (128, 2, 1024)))
    S["od_win"] = tile_cols(I["od_w_in"][0])
    S["od_wout"] = tile_cols(I["od_w_out"][0])
    for l in range(2):
        for f in (1, 2):
            S[f"wg{l}{f}"] = tile_cols(I[f"ffn{f}_w_gate"][l])
            S[f"wu{l}{f}"] = tile_cols(I[f"ffn{f}_w_up"][l])
            S[f"wd{l}{f}"] = np.ascontiguousarray(I[f"ffn{f}_w_down"][l].reshape(NG, 128, 2048))
        S[f"xwq{l}"] = tile_cols(I["x_w_q"][l])
        S[f"xwkv{l}"] = tile_cols(I["x_w_kv"][l])
        S[f"xwo{l}"] = tile_cols(I["x_w_o"][l])
    return S


def host_core(I, c):
    b, hf = c // 2, c % 2
    C = {}
    x = np.concatenate([I["x_prompt"][b, hf * 1024:(hf + 1) * 1024], I["x_sample"][c]], axis=0)
    C["xT"] = np.ascontiguousarray(x.T.reshape(KC, 128, NT))
    fl = np.zeros((128, 2), np.float32)
    fl[:, 0] = hf
    fl[:, 1] = (hf - 1) * BIG
    C["flag"] = fl
    C["cflogf"] = np.ascontiguousarray(I["cache_fox_logf"][0, c].reshape(8, 128, 8).transpose(1, 0, 2))
    C["cfkT"] = np.ascontiguousarray(I["cache_fox_k"][0, c].transpose(1, 2, 0))
    C["cfv"] = np.ascontiguousarray(I["cache_fox_v"][0, c].reshape(1024, 1024))
    C["hstate_in"] = np.ascontiguousarray(I["state_hgrn"][0, c])
    C["cskT"] = np.ascontiguousarray(I["cache_sb_k"][0, c].transpose(1, 2, 0))
    C["csv"] = np.ascontiguousarray(I["cache_sb_v"][0, c].reshape(1024, 2048))
    C["memT"] = np.ascontiguousarray(I["mem_prompt"][b].T.reshape(KC, 128, 256))
    for l in range(2):
        C[f"cmkT{l}"] = np.ascontiguousarray(I["cache_mem_k"][l, c].reshape(256, 2048).T.reshape(16, 128, 256))
        C[f"cmv{l}"] = np.ascontiguousarray(I["cache_mem_v"][l, c].reshape(256, 2048))
    return C


def cross_attn(cx, l):
    P = cx.P
    nc = cx.nc
    d = cx.d
    cb = cx.cbv
    with contextlib.ExitStack() as es:
        sbl = lambda name, shape, dt: es.enter_context(nc.sbuf_tensor(uid(cx, name), shape, dt))
        qx = sbl("xa_qx", [128, KC, NT], BF16)
        qx_b = [[Buf() for _ in range(3)] for _ in range(KC)]
        mkT = sbl("xa_mkT", [128, KC, 256], BF16); mkT_b = Buf()
        mv = sbl("xa_mv", [128, 2, 2048], BF16); mv_b = Buf()
        with contextlib.ExitStack() as es2:
            sb2 = lambda name, shape, dt: es2.enter_context(nc.sbuf_tensor(uid(cx, name), shape, dt))
            es2.enter_context(ring(cx, 6))
            memb = sb2("xa_memb", [128, KC, 256], BF16)
            memb_b = [[Buf()] for _ in range(KC)]
            sb_save = cx.sb
            cx.sb = lambda name, shape, dt: es2.enter_context(nc.sbuf_tensor(name, shape, dt))
            sk = Stage(cx, uid(cx, "xa_sk"), [256], F32)
            sv = Stage(cx, uid(cx, "xa_sv"), [2, 128], F32)
            cx.sb = sb_save
            lm = cx.L(0)
            P.dma("pool", memb[:], d["memT"].rearrange("k p m -> p k m"), lm, writes=[b[0] for b in memb_b])
            W = d[f"xwkv{l}"]
            srcs = []
            for c in range(16):
                srcs += [(W[c], 2048), (W[16 + c], 2048), (d[f"xwq{l}"][c], 2048)]
            ws = WStream(cx, srcs, la=4)
            for c in range(16):
                w, w_b = ws.get(3 * c)
                a, fb_, fl = sk.get()

                def cons(ti, t0, n, ps, ps_b):
                    P.op("act", lambda e: e.copy(sk.t[:, a, :], ps[:, 0:256]), reads=[ps_b], writes=[fb_])
                    P.op("dve", lambda e: e.tensor_copy(mkT[:, c, :], ps[:, 0:256]), reads=[ps_b], writes=[mkT_b])
                proj_fm(cx, w, w_b, memb, memb_b, [(0, 256)], cons)
                P.dma("sp", d[f"mem_kT{l}"][c], sk.t[:, a, :], fl, reads=[fb_])
                w, w_b = ws.get(3 * c + 1)
                a, fb_, fl = sv.get()

                def cons(ci, t0, m, ps, ps_b):
                    P.op("act", lambda e: e.copy(sv.t[:, a, ci, :], ps[:, 0:128]), reads=[ps_b], writes=[fb_])
                    P.op("dve", lambda e: e.tensor_copy(mv[:, ci, c * 128:(c + 1) * 128], ps[:, 0:128]), reads=[ps_b], writes=[mv_b])
                proj_tm(cx, w, w_b, memb, memb_b, [(0, 128, 0), (128, 128, 0)], cons)
                P.dma("sp", d[f"mem_v{l}"][:, c * 128:(c + 1) * 128].rearrange("(i p) f -> p i f", p=128), sv.t[:, a, :, :], fl, reads=[fb_])
                w, w_b = ws.get(3 * c + 2)

                def cons(ti, t0, n, ps, ps_b):
                    P.op("dve", lambda e: e.tensor_scalar(qx[:, c, t0:t0 + n], ps[:, 0:n], SC512, None, ALU.mult), reads=[ps_b], writes=[qx_b[c][ti]])
                proj_fm(cx, w, w_b, cx.xb, cx.xb_b, TILES_F, cons)
            P.barrier()
        cmk = sbl("xa_cmk", [128, 2, 4, 256], BF16)
        cmv = sbl("xa_cmv", [128, 2, 2, 512], BF16)
        cm_b = [[Buf(), Buf()] for _ in range(2)]
        pp2 = sbl("xa_pp", [128, 2, 2, 512], BF16)
        pp_b = [Buf(), Buf()]
        rd = sbl("xa_rd", [128, 512], F32); rd_b = Buf()
        pi = 0
        for h in range(4):
            s = h % 2
            P.dma("pool", cmk[:, s, :, :], d[f"cmkT{l}"][4 * h:4 * h + 4].rearrange("k p m -> p k m"), cx.L(1 + 2 * s), writes=[cm_b[s][0]])
            P.dma("pool", cmv[:, s, :, :], d[f"cmv{l}"][:, h * 512:(h + 1) * 512].rearrange("(i p) f -> p i f", p=128), cx.L(2 + 2 * s), writes=[cm_b[s][1]])
            for ti, (t0, n) in enumerate(TILES):
                a = pi % 2
                pi += 1
                for mc in range(2):
                    sp_, sp_b = cx.psA.get()
                    for dc in range(4):
                        if ti < 2:
                            kap, kb_ = mkT[:, 4 * h + dc, mc * 128:(mc + 1) * 128], mkT_b
                        else:
                            kap, kb_ = cmk[:, s, dc, mc * 128:(mc + 1) * 128], cm_b[s][0]
                        P.op("pe", lambda e: e.matmul(sp_[:, 0:n], kap, qx[:, 4 * h + dc, t0:t0 + n], start=(dc == 0), stop=(dc == 3)), reads=[kb_, qx_b[4 * h + dc][ti]], writes=[sp_b])
                    P.op("act", lambda e: e.activation(pp2[:, a, mc, 0:n], sp_[:, 0:n], AF.Exp), reads=[sp_b], writes=[pp_b[a]])
                den, den_b = cx.psB.get()
                for mc in range(2):
                    P.op("pe", lambda e: e.matmul(den[:, 0:n], cb("ones"), pp2[:, a, mc, 0:n], start=(mc == 0), stop=(mc == 1)), reads=[pp_b[a], cx.c_b], writes=[den_b])
                P.op("dve", lambda e: e.reciprocal(rd[:, 0:n], den[:, 0:n]), reads=[den_b], writes=[rd_b])
                for dc in range(4):
                    o_, o_b = cx.psB.get()
                    for mc in range(2):
                        if ti < 2:
                            vap, vb_ = mv[:, mc, (4 * h + dc) * 128:(4 * h + dc + 1) * 128], mv_b
                        else:
                            vap, vb_ = cmv[:, s, mc, dc * 128:(dc + 1) * 128], cm_b[s][1]
                        P.op("pe", lambda e: e.matmul(o_[:, 0:n], vap, pp2[:, a, mc, 0:n], start=(mc == 0), stop=(mc == 1)), reads=[vb_, pp_b[a]], writes=[o_b])
                    P.op("dve", lambda e: e.tensor_tensor(cx.xb[:, 4 * h + dc, t0:t0 + n], o_[:, 0:n], rd[:, 0:n], ALU.mult), reads=[o_b, rd_b], writes=[cx.xb_b[4 * h + dc][ti]])
        P.barrier()
    out_proj_residual(cx, d[f"xwo{l}"], cx.xb, cx.xb_b)


def odd_mixer(cx):
    P = cx.P
    nc = cx.nc
    d = cx.d
    cb = cx.cbv
    G2 = cx.groups
    with contextlib.ExitStack() as es1:
        sb_save = cx.sb
        cx.sb = lambda name, shape, dt: es1.enter_context(nc.sbuf_tensor(name, shape, dt))
        es1.enter_context(ring(cx))
        sfm = Stage(cx, "o1_sfm", [NT], F32)
        bfm = Stage(cx, "o1_bfm", [NT], BF16)
        stm = Stage(cx, "o1_stm", [9, 128], F32)
        btm = Stage(cx, "o1_btm", [9, 128], BF16)
        cx.sb = sb_save
        W = d["od_win"]
        order = [("k", h, 16 + h) for h in range(16)] + [("v", h, 32 + h) for h in range(16)] + [("q", h, h) for h in range(16)]
        ws = WStream(cx, [(W[c], 2048) for (_, _, c) in order], la=8)
        for oi, (kind, h, c) in enumerate(order):
            w, w_b = ws.get(oi)
            g, hh = h // 8, h % 8
            if kind == "k":
                a, fb_, fl = sfm.get()
                a2, bb_, bl = bfm.get()

                def cons(ti, t0, n, ps, ps_b):
                    P.op("act", lambda e: e.copy(sfm.t[:, a, t0:t0 + n], ps[:, 0:n]), reads=[ps_b], writes=[fb_])
                    P.op("dve", lambda e: e.tensor_copy(bfm.t[:, a2, t0:t0 + n], ps[:, 0:n]), reads=[ps_b], writes=[bb_])
                proj_fm(cx, w, w_b, cx.xb, cx.xb_b, TILES_F, cons)
                P.dma("sp", d["sb_kT"][h], sfm.t[:, a, :], fl, reads=[fb_])
                P.dma("sp", d["sk_s"][h], bfm.t[:, a2, :], bl, reads=[bb_])
                P.dma("sp", d[f"xg2k{g}_in"][hh * 128:(hh + 1) * 128, :], bfm.t[:, a2, 0:1024], bl, reads=[bb_])
            elif kind == "v":
                a, fb_, fl = stm.get()
                a2, bb_, bl = btm.get()

                def cons(ci, t0, m, ps, ps_b):
                    P.op("act", lambda e: e.copy(stm.t[0:m, a, ci, :], ps[0:m, 0:128]), reads=[ps_b], writes=[fb_])
                    P.op("dve", lambda e: e.tensor_copy(btm.t[0:m, a2, ci, :], ps[0:m, 0:128]), reads=[ps_b], writes=[bb_])
                proj_tm(cx, w, w_b, cx.xb, cx.xb_b, TCH, cons)
                cs = slice(h * 128, (h + 1) * 128)
                cs2 = slice(hh * 128, (hh + 1) * 128)
                P.dma("sp", d["sb_v"][0:1024, cs].rearrange("(i p) f -> p i f", p=128), stm.t[:, a, 0:8, :], fl, reads=[fb_])
                P.dma("sp", d["sb_v"][1024:1040, cs], stm.t[0:16, a, 8, :], fl, reads=[fb_])
                P.dma("sp", d["sv_s"][0:1024, cs].rearrange("(i p) f -> p i f", p=128), btm.t[:, a2, 0:8, :], bl, reads=[bb_])
                P.dma("sp", d["sv_s"][1024:1040, cs], btm.t[0:16, a2, 8, :], bl, reads=[bb_])
                P.dma("sp", d[f"xg2v{g}_in"][0:1024, cs2].rearrange("(i p) f -> p i f", p=128), btm.t[:, a2, 0:8, :], bl, reads=[bb_])
            else:
                a2, bb_, bl = bfm.get()

                def cons(ti, t0, n, ps, ps_b):
                    P.op("dve", lambda e: e.tensor_scalar(bfm.t[:, a2, t0:t0 + n], ps[:, 0:n], SC128, None, ALU.mult), reads=[ps_b], writes=[bb_])
                proj_fm(cx, w, w_b, cx.xb, cx.xb_b, TILES_F, cons)
                P.dma("sp", d["sq_s"][h], bfm.t[:, a2, :], bl, reads=[bb_])
        P.barrier()
    for g in range(2):
        P.coll("AllGather", [d[f"xg2k{g}_in_t"].ap().opt()], [d[f"xg2k{g}_out_t"].ap().opt()], G2, cx.cc_lane, writes=[cx.xg2ko_b])
        P.coll("AllGather", [d[f"xg2v{g}_in_t"].ap().opt()], [d[f"xg2v{g}_out_t"].ap().opt()], G2, cx.cc_lane, writes=[cx.xg2vo_b])
    K = 2
    with contextlib.ExitStack() as es:
        sbl = lambda name, shape, dt: es.enter_context(nc.sbuf_tensor(name, shape, dt))
        NS = 3
        qT = sbl("sb_qT", [128, NS, NT], BF16)
        kT = sbl("sb_kT_", [128, NS, NT], BF16)
        vl = sbl("sb_vl", [128, NS, 9, 128], BF16)
        kTr = sbl("sb_kTr", [128, NS, 1024], BF16)
        vr = sbl("sb_vr", [128, NS, 8, 128], BF16)
        kTc = sbl("sb_kTc", [128, NS, 1024], BF16)
        vc = sbl("sb_vc", [128, NS, 8, 128], BF16)
        in_b = [[Buf() for _ in range(7)] for _ in range(NS)]
        e1 = sbl("sb_e1", [128, K, 2, 512], F32); e1_b = [[Buf(), Buf()] for _ in range(K)]
        lp = sbl("sb_lp", [128, K, 2, 512], BF16); lp_b = [[Buf(), Buf()] for _ in range(K)]
        xx = sbl("sb_xx", [128, K, 512], F32); xx_b = [Buf() for _ in range(K)]
        ww = sbl("sb_ww", [128, K, 512], BF16); ww_b = [Buf() for _ in range(K)]
        zpools = [PsPool(cx.ps[4 * k + 2:4 * k + 4]) for k in range(K)]
        loaded = set()

        def load_head(h):
            if h in loaded:
                return
            loaded.add(h)
            s = h % NS
            g, hh = h // 8, h % 8
            ib_ = in_b[s]
            L = lambda i: cx.lanes2_[s * 7 + i]
            cs = slice(h * 128, (h + 1) * 128)
            cs2 = slice(hh * 128, (hh + 1) * 128)
            P.dma("sp", qT[:, s, :], d["sq_s"][h], L(0), writes=[ib_[0]])
            P.dma("sp", kT[:, s, :], d["sk_s"][h], L(1), writes=[ib_[1]])
            P.dma("sp", vl[:, s, 0:8, :], d["sv_s"][0:1024, cs].rearrange("(i p) f -> p i f", p=128), L(2), writes=[ib_[2]])
            P.dma("sp", vl[0:16, s, 8, :], d["sv_s"][1024:1040, cs], L(2), writes=[ib_[2]])
            P.dma("sp", kTr[:, s, :], d[f"xg2k{g}_out"][hh * 128:(hh + 1) * 128, :], L(3), reads=[cx.xg2ko_b], writes=[ib_[3]])
            P.dma("sp", vr[:, s, :, :], d[f"xg2v{g}_out"][0:1024, cs2].rearrange("(i p) f -> p i f", p=128), L(4), reads=[cx.xg2vo_b], writes=[ib_[4]])
            P.dma("pool", kTc[:, s, :], d["cskT"][h], L(5), writes=[ib_[5]])
            P.dma("pool", vc[:, s, :, :], d["csv"][:, cs].rearrange("(i p) f -> p i f", p=128), L(6), writes=[ib_[6]])

        def chain(h, ti, sl):
            load_head(h)
            if h + 1 < 16:
                load_head(h + 1)
            s = h % NS
            ib_ = in_b[s]
            t0, n = TILES[ti]
            if ti < 2:
                chunks = [("loc", i) for i in range(4 * ti + 3, -1, -1)] + [("rem", i) for i in range(7, -1, -1)]
            else:
                chunks = [("sloc", 8)] + [("cache", i) for i in range(7, -1, -1)]
            oT, oT_b = cx.ps[4 * sl]
            A, A_b = cx.ps[4 * sl + 1]
            zpool = zpools[sl]
            P.op("pe", lambda e: e.matmul(oT[:, 0:n], cb("zeros"), qT[:, s, t0:t0 + n], start=True, stop=False), reads=[cx.c_b, ib_[0]], writes=[oT_b])
            P.op("pe", lambda e: e.matmul(A[:, 0:n], cb("zeros"), qT[:, s, t0:t0 + n], start=True, stop=False), reads=[cx.c_b, ib_[0]], writes=[A_b])

            def info(ci):
                kind, i = chunks[ci]
                m, c0, diag, bias = 128, 0, False, 0.0
                if kind == "rem":
                    kap, kb_, vap, vb_ = kTr[:, s, i * 128:(i + 1) * 128], ib_[3], vr[:, s, i, :], ib_[4]
                    bias = cx.flag[:, 1:2]
                elif kind == "loc":
                    kap, kb_, vap, vb_ = kT[:, s, i * 128:(i + 1) * 128], ib_[1], vl[:, s, i, :], ib_[2]
                    c0 = max(0, (i - 4 * ti) * 128)
                    diag = i >= 4 * ti
                elif kind == "cache":
                    kap, kb_, vap, vb_ = kTc[:, s, i * 128:(i + 1) * 128], ib_[5], vc[:, s, i, :], ib_[6]
                else:
                    m = 16
                    kap, kb_, vap, vb_ = kT[:, s, 1024:1040], ib_[1], vl[0:16, s, 8, :], ib_[2]
                    diag = True
                return kap, kb_, vap, vb_, m, c0, diag, bias

            nch = len(chunks)
            zps = {}

            def f1(ci):
                kap, kb_, vap, vb_, m, c0, diag, bias = info(ci)
                zp, zp_b = zpool.get()
                zps[ci] = (zp, zp_b)
                P.op("pe", lambda e: e.matmul(zp[0:m, c0:n], kap, qT[:, s, t0 + c0:t0 + n], start=True, stop=True), reads=[kb_, ib_[0]], writes=[zp_b])

            def f2(ci):
                kap, kb_, vap, vb_, m, c0, diag, bias = info(ci)
                zp, zp_b = zps.pop(ci)
                a = ci % 2
                P.op("act", lambda e: e.activation(e1[0:m, sl, a, c0:n], zp[0:m, c0:n], AF.Exp), reads=[zp_b], writes=[e1_b[sl][a]])

            def f3(ci):
                kap, kb_, vap, vb_, m, c0, diag, bias = info(ci)
                a = ci % 2
                dm = min(m, 128)
                P.op("act", lambda e: e.activation(lp[0:m, sl, a, c0:n], e1[0:m, sl, a, c0:n], AF.Ln, bias=1.0), reads=[e1_b[sl][a]], writes=[lp_b[sl][a]])
                if diag:
                    P.op("pool", lambda e: e.tensor_tensor(lp[0:m, sl, a, c0:c0 + dm], lp[0:m, sl, a, c0:c0 + dm], cb("trilt", m, dm), ALU.mult), reads=[lp_b[sl][a], cx.c_b], writes=[lp_b[sl][a]])

            def b1(ci):
                kap, kb_, vap, vb_, m, c0, diag, bias = info(ci)
                a = ci % 2
                P.op("pe", lambda e: e.matmul(A[:, c0:n], cb("negtrige", m, 128), lp[0:m, sl, a, c0:n], start=False, stop=False), reads=[lp_b[sl][a], cx.c_b], writes=[A_b])

            def b2(ci):
                kap, kb_, vap, vb_, m, c0, diag, bias = info(ci)
                a = ci % 2
                P.op("act", lambda e: e.activation(xx[0:m, sl, c0:n], A[0:m, c0:n], AF.Exp, bias=bias), reads=[A_b, cx.c_b], writes=[xx_b[sl]])

            def b3(ci):
                kap, kb_, vap, vb_, m, c0, diag, bias = info(ci)
                a = ci % 2
                dm = min(m, 128)
                P.op("dve", lambda e: e.tensor_tensor(ww[0:m, sl, c0:n], e1[0:m, sl, a, c0:n], xx[0:m, sl, c0:n], ALU.mult), reads=[e1_b[sl][a], xx_b[sl]], writes=[ww_b[sl]])
                if diag:
                    P.op("pool", lambda e: e.tensor_tensor(ww[0:m, sl, c0:c0 + dm], ww[0:m, sl, c0:c0 + dm], cb("trilt", m, dm), ALU.mult), reads=[ww_b[sl], cx.c_b], writes=[ww_b[sl]])

            def b4(ci):
                kap, kb_, vap, vb_, m, c0, diag, bias = info(ci)
                a = ci % 2
                P.op("pe", lambda e: e.matmul(oT[:, c0:n], vap, ww[0:m, sl, c0:n], start=False, stop=(ci == nch - 1)), reads=[ww_b[sl], vb_], writes=[oT_b])
                P.op("pe", lambda e: e.matmul(A[:, c0:n], cb("negtrilt", m, 128), lp[0:m, sl, a, c0:n], start=False, stop=(ci == nch - 1)), reads=[lp_b[sl][a], ww_b[sl], cx.c_b], writes=[A_b])

            f1(0); yield
            f2(0); yield
            f3(0); yield
            for ci in range(nch):
                nx = ci + 1 < nch
                if nx:
                    f1(ci + 1)
                b1(ci); yield
                if nx:
                    f2(ci + 1)
                b2(ci); yield
                if nx:
                    f3(ci + 1)
                b3(ci); yield
                b4(ci)
            yield
            P.op("act", lambda e: e.copy(cx.xb[:, h, t0:t0 + n], oT[:, 0:n]), reads=[oT_b], writes=[cx.xb_b[h][ti]])

        facs = []
        for h in range(16):
            for ti in (1, 0, 2):
                facs.append(lambda sl, h=h, ti=ti: chain(h, ti, sl))
        run_chains(facs, K)
        P.barrier()
    out_proj_residual(cx, d["od_wout"], cx.xb, cx.xb_b)


def build_program(cx):
    nc = cx.nc
    P = cx.P
    d = cx.d
    setup(nc, cx)
    setup_consts2(cx)
    load_x(cx)
    P.barrier()
    for l in range(2):
        ffn(cx, d[f"wg{l}1"], d[f"wu{l}1"], d[f"wd{l}1"])
        layer_norm(cx, 4 * l + 0)
        if l == 0:
            even_mixer(cx)
        else:
            odd_mixer(cx)
        layer_norm(cx, 4 * l + 1)
        cross_attn(cx, l)
        layer_norm(cx, 4 * l + 2)
        ffn(cx, d[f"wg{l}2"], d[f"wu{l}2"], d[f"wd{l}2"])
        layer_norm(cx, 4 * l + 3, final=(l == 1))
    store_y(cx)
    P.barrier()


_NC_CACHE = {}


def get_nc(ncores=8):
    if ncores in _NC_CACHE:
        return _NC_CACHE[ncores]
    nc = bass.Bass("TRN2", target_bir_lowering=False)
    cx = Ctx()
    cx.nc = nc
    cx.P = Prog(nc)
    cx.groups = [[2 * i, 2 * i + 1] for i in range(ncores // 2)]
    declare(cx, nc)
    with contextlib.ExitStack() as es:
        cx.es = es
        build_program(cx)
    _NC_CACHE[ncores] = nc
    return nc


def kernel(**inputs):
    I = {k: np.asarray(v) for k, v in inputs.items()}
    ncores = 8
    nc = get_nc(ncores)
    S = host_shared(I)
    in_maps = []
    for c in range(ncores):
        C = host_core(I, c)
        C.update(S)
        in_maps.append({k: np.ascontiguousarray(C[k], dtype=np.float32) for k in IN_SPECS})
    res = run_bass_kernel_spmd(nc, in_maps, core_ids=list(range(ncores)))
    R = res.results
    f32 = np.float32
    y_p = np.zeros((4, 2048, 2048), f32); y_s = np.zeros((8, 16, 2048), f32)
    fk_p = np.zeros((1, 4, 2048, 8, 128), f32); fv_p = np.zeros((1, 4, 2048, 8, 128), f32); fl_p = np.zeros((1, 4, 2048, 8), f32)
    hs_p = np.zeros((1, 4, 8, 128, 128), f32)
    sk_p = np.zeros((1, 4, 2048, 16, 128), f32); sv_p = np.zeros((1, 4, 2048, 16, 128), f32)
    mk_p = np.zeros((2, 4, 256, 4, 512), f32); mv_p = np.zeros((2, 4, 256, 4, 512), f32)
    fk_s = np.zeros((1, 8, 16, 8, 128), f32); fv_s = np.zeros((1, 8, 16, 8, 128), f32); fl_s = np.zeros((1, 8, 16, 8), f32)
    hs_s = np.zeros((1, 8, 8, 128, 128), f32)
    sk_s = np.zeros((1, 8, 16, 16, 128), f32); sv_s = np.zeros((1, 8, 16, 16, 128), f32)
    for c in range(ncores):
        r = R[c]
        b, hf = c // 2, c % 2
        sl = slice(hf * 1024, (hf + 1) * 1024)
        y = np.asarray(r["yT"]).reshape(2048, NT).T
        y_p[b, sl] = y[:1024]; y_s[c] = y[1024:]
        k = np.asarray(r["fox_kT"]).transpose(2, 0, 1)
        fk_p[0, b, sl] = k[:1024]; fk_s[0, c] = k[1024:]
        v = np.asarray(r["fox_v"]).reshape(NT, 8, 128)
        fv_p[0, b, sl] = v[:1024]; fv_s[0, c] = v[1024:]
        lf = np.asarray(r["fox_logf"])
        fl_p[0, b, sl] = lf[:1024]; fl_s[0, c] = lf[1024:]
        if hf == 1:
            hs_p[0, b] = np.asarray(r["hstate_p"])
        hs_s[0, c] = np.asarray(r["hstate_s"])
        k = np.asarray(r["sb_kT"]).transpose(2, 0, 1)
        sk_p[0, b, sl] = k[:1024]; sk_s[0, c] = k[1024:]
        v = np.asarray(r["sb_v"]).reshape(NT, 16, 128)
        sv_p[0, b, sl] = v[:1024]; sv_s[0, c] = v[1024:]
        if hf == 0:
            for l in range(2):
                mk_p[l, b] = np.asarray(r[f"mem_kT{l}"]).reshape(2048, 256).T.reshape(256, 4, 512)
                mv_p[l, b] = np.asarray(r[f"mem_v{l}"]).reshape(256, 4, 512)
    return (y_p, y_s, fk_p, fv_p, fl_p, hs_p, sk_p, sv_p, mk_p, mv_p, fk_s, fv_s, fl_s, hs_s, sk_s, sv_s)
```

```python
from concourse.bass_utils import run_bass_kernel_spmd
import numpy as np
import concourse.bass as bass
import concourse.mybir as mybir

F32 = mybir.dt.float32
BF16 = mybir.dt.bfloat16
AF = mybir.ActivationFunctionType
ALU = mybir.AluOpType


class Buf:
    __slots__ = ("name", "w", "r", "x")

    def __init__(self, name="", x=False):
        self.name = name
        self.w = None
        self.r = []
        self.x = x


class Lane:
    __slots__ = ("sem", "cnt")

    def __init__(self, sem):
        self.sem = sem
        self.cnt = 0


class Prog:
    ROT = 30000

    def __init__(self, nc):
        self.nc = nc
        self.eng = {"pe": nc.tensor, "dve": nc.vector, "act": nc.scalar, "pool": nc.gpsimd, "sp": nc.sync}
        self.cnt = {e: 0 for e in self.eng}
        self.sem = {e: nc.alloc_semaphore(name=f"c_{e}_0") for e in self.eng}
        self.nrot = {e: 0 for e in self.eng}
        self.known = {e: {} for e in self.eng}
        self.lanes = []
        self.all_sems = list(self.sem.values())
        self.ninstr = 0

    def lane(self, name="lane"):
        s = self.nc.alloc_semaphore(name=f"{name}_{len(self.lanes)}")
        l = Lane(s)
        self.lanes.append(l)
        return l

    def _collect(self, e, reads, writes):
        need = {}

        def add(tok):
            if tok is None:
                return
            s, v = tok
            if e == "pe" and s is self.sem["pe"]:
                return
            k = id(s)
            if k not in need or need[k][1] < v:
                need[k] = (s, v)

        for b in reads:
            add(b.w)
            if b.x:
                for t in b.r:
                    if t[0] is not self.sem[e]:
                        add(t)
        for b in writes:
            add(b.w)
            for t in b.r:
                add(t)
        kn = self.known[e]
        out = []
        for k, (s, v) in need.items():
            if kn.get(k, 0) >= v:
                continue
            out.append((s, v))
            kn[k] = v
        return out

    def _waits(self, e, deps):
        for s, v in deps:
            self.eng[e].wait_ge(s, v)
            self.ninstr += 1

    def _rot(self, e):
        if self.cnt[e] >= self.ROT:
            self.nrot[e] += 1
            self.sem[e] = self.nc.alloc_semaphore(name=f"c_{e}_{self.nrot[e]}")
            self.cnt[e] = 0

    def op(self, e, fn, reads=(), writes=()):
        self._rot(e)
        deps = self._collect(e, reads, writes)
        self._waits(e, deps)
        ins = fn(self.eng[e])
        self.cnt[e] += 1
        tok = (self.sem[e], self.cnt[e])
        ins.then_inc(tok[0], 1)
        self.ninstr += 1
        for b in reads:
            b.r.append(tok)
        for b in writes:
            b.w = tok
            b.r = []
        return tok

    def dma(self, q, out, in_, lane, reads=(), writes=(), **kw):
        deps = self._collect(q, reads, writes)
        self._waits(q, deps)
        ins = self.eng[q].dma_start(out=out, in_=in_, **kw)
        lane.cnt += 16
        ins.then_inc(lane.sem, 16)
        self.ninstr += 1
        tok = (lane.sem, lane.cnt)
        for b in reads:
            b.r.append(tok)
        for b in writes:
            b.w = tok
            b.r = []
        return tok

    def coll(self, kind, ins, outs, groups, lane, reads=(), writes=()):
        if getattr(self, "nocoll", False):
            return None
        q = "pool"
        deps = self._collect(q, reads, writes)
        self._waits(q, deps)
        i = self.eng[q].collective_compute(kind, ALU.bypass, replica_groups=groups, ins=ins, outs=outs)
        lane.cnt += 1
        i.then_inc(lane.sem, 1)
        tok = (lane.sem, lane.cnt)
        for b in reads:
            b.r.append(tok)
        for b in writes:
            b.w = tok
            b.r = []
        return tok

    def barrier(self):
        toks = [(self.sem[e], self.cnt[e]) for e in self.eng if self.cnt[e] > 0]
        toks += [(l.sem, l.cnt) for l in self.lanes if l.cnt > 0]
        for e in self.eng:
            kn = self.known[e]
            for s, v in toks:
                if e == "pe" and s is self.sem["pe"]:
                    continue
                if kn.get(id(s), 0) >= v:
                    continue
                self.eng[e].wait_ge(s, v)
                kn[id(s)] = v
                self.ninstr += 1


class PsPool:
    def __init__(self, tiles):
        self.tiles = tiles
        self.i = 0

    def get(self):
        t = self.tiles[self.i % len(self.tiles)]
        self.i += 1
        return t


class Rot:
    def __init__(self, tiles):
        self.tiles = tiles
        self.i = 0

    def get(self):
        t = self.tiles[self.i % len(self.tiles)]
        self.i += 1
        return t


import contextlib

D = 2048
KC = 16
T = 1024
TS = 16
NT = T + TS
TILES = [(0, 512), (512, 512), (1024, 16)]
TILES_F = [(0, 352), (352, 344), (696, 344)]
DFF = 5504
NG = 43
ALPHA = 4.0 ** 0.25
LN_EPS = 1e-5
NB_W = 10


class Ctx:
    pass


def setup(nc, cx):
    P = cx.P
    es = cx.es
    sb = lambda name, shape, dt: es.enter_context(nc.sbuf_tensor(name, shape, dt))
    cx.sb = sb
    cx.xf = sb("s_xf", [128, KC, NT], F32)
    cx.xf_b = [[Buf(f"xf{k}_{t}") for t in range(3)] for k in range(KC)]
    cx.xb = sb("s_xb", [128, KC, NT], BF16)
    cx.xb_b = [[Buf(f"xb{k}_{t}") for t in range(3)] for k in range(KC)]
    cx.wring = None
    cx.uid_ = 0
    cx.lanes_ = [P.lane("g") for _ in range(20)]
    cx.lanes2_ = [P.lane("h") for _ in range(21)]
    cx.L = lambda i: cx.lanes_[i]
    cx.cc_lane = P.lane("cc")
    for nm in ["xgo_b", "xgco_b", "xgso_b", "xg2ko_b", "xg2vo_b"]:
        setattr(cx, nm, Buf(nm))
    cx.w_b = [Buf(f"w{i}") for i in range(NB_W)]
    cx.w_lane = [P.lane("wl") for i in range(NB_W)]
    cx.w_i = 0
    cx.ps = []
    for i in range(8):
        t = es.enter_context(nc.psum_tensor(f"ps{i}", [128, 512], F32))
        cx.ps.append((t, Buf(f"ps{i}", x=True)))
    cx.psA = PsPool(cx.ps[0:4])
    cx.psB = PsPool(cx.ps[4:8])
    cx.c_b = Buf("consts")
    cx.inv2048 = sb("inv2048", [128, 128], BF16)
    cx.ones_bf = sb("ones_bf", [128, 128], BF16)
    cx.ones_f = sb("ones_f", [128, 128], F32)
    cx.lng = sb("s_lng", [128, 8, KC], F32)
    cx.lnb = sb("s_lnb", [128, 8, KC], F32)
    cx.lnga = sb("s_lnga", [128, 8, KC], F32)
    cx.lnba = sb("s_lnba", [128, 8, KC], F32)
    P.op("pool", lambda e: e.memset(cx.inv2048[:], 1.0 / 2048.0), writes=[cx.c_b])
    P.op("pool", lambda e: e.memset(cx.ones_bf[:], 1.0), writes=[cx.c_b])
    P.op("pool", lambda e: e.memset(cx.ones_f[:], 1.0), writes=[cx.c_b])
    ll = P.lane("ln")
    cx.misc_lane = ll
    P.dma("sp", cx.lng[:], cx.d["lng"], ll, writes=[cx.c_b])
    P.dma("sp", cx.lnb[:], cx.d["lnb"], ll, writes=[cx.c_b])
    P.op("dve", lambda e: e.tensor_scalar(cx.lnga[:], cx.lng[:], ALPHA, None, ALU.mult), reads=[cx.c_b], writes=[cx.c_b])
    P.op("dve", lambda e: e.tensor_scalar(cx.lnba[:], cx.lnb[:], ALPHA, None, ALU.mult), reads=[cx.c_b], writes=[cx.c_b])


def load_w(cx, src_ap, ncols=2048):
    P = cx.P
    i = cx.w_i % cx.nslots
    cx.w_i += 1
    P.dma("pool", cx.wring[:, i, 0:ncols], src_ap, cx.w_lane[i], writes=[cx.w_b[i]])
    return cx.wring[:, i, :], cx.w_b[i]


def uid(cx, name):
    cx.uid_ += 1
    return f"{name}_{cx.uid_}"


@contextlib.contextmanager
def ring(cx, nslots=NB_W):
    with cx.nc.sbuf_tensor(uid(cx, "wring"), [128, nslots, 2048], BF16) as t:
        cx.wring = t
        cx.nslots = nslots
        cx.w_i = 0
        yield
    cx.wring = None


class WStream:
    def __init__(self, cx, srcs, la=6):
        self.cx = cx
        self.srcs = srcs
        self.slots = {}
        self.issued = 0
        self.la = la

    def get(self, k):
        while self.issued < len(self.srcs) and self.issued <= k + self.la:
            ap, ncols = self.srcs[self.issued]
            self.slots[self.issued] = load_w(self.cx, ap, ncols)
            self.issued += 1
        return self.slots.pop(k)


def load_x(cx):
    P = cx.P
    l = P.lane("xin")
    for kc in range(KC):
        P.dma("sp", cx.xf[:, kc, :], cx.d["xT"][kc], l, writes=cx.xf_b[kc])
    for kc in range(KC):
        for b in cx.xf_b[kc]:
            b.w = (l.sem, l.cnt)
    for kc in range(KC):
        for ti, (t0, n) in enumerate(TILES_F):
            P.op("act", lambda e, kc=kc, t0=t0, n=n: e.copy(cx.xb[:, kc, t0:t0 + n], cx.xf[:, kc, t0:t0 + n]),
                 reads=[cx.xf_b[kc][ti]], writes=[cx.xb_b[kc][ti]])
            P.op("dve", lambda e, kc=kc, t0=t0, n=n: e.tensor_scalar(cx.xf[:, kc, t0:t0 + n], cx.xf[:, kc, t0:t0 + n], ALPHA, None, ALU.mult),
                 reads=[cx.xf_b[kc][ti]], writes=[cx.xf_b[kc][ti]])


def ffn(cx, wg_d, wu_d, wd_d):
    P = cx.P
    nc = cx.nc
    with ring(cx), nc.sbuf_tensor(uid(cx, "ffn_h"), [128, 4, NT], BF16) as h, nc.sbuf_tensor(uid(cx, "ffn_s"), [128, 2, 512], F32) as stmp:
        h_b = [[Buf(f"h{i}_{t}") for t in range(3)] for i in range(4)]
        s_b = [Buf("s0"), Buf("s1")]
        s_i = 0
        srcs = []
        for g in range(NG):
            srcs += [(wg_d[g], 2048), (wu_d[g], 2048), (wd_d[g], 2048)]
        ws = WStream(cx, srcs, la=6)
        G = 2
        sgs = [list(range(a, min(a + G, NG))) for a in range(0, NG, G)]
        for si, sg in enumerate(sgs):
            wds = []
            for gi, g in enumerate(sg):
                hi = (si % 2) * 2 + gi
                wg, wg_b = ws.get(3 * g)
                wu, wu_b = ws.get(3 * g + 1)
                wd, wd_b = ws.get(3 * g + 2)
                wds.append((wd, wd_b, hi))
                for ti, (t0, n) in enumerate(TILES_F):
                    gp, gp_b = cx.psA.get()
                    up, up_b = cx.psA.get()
                    for kc in range(KC):
                        P.op("pe", lambda e, kc=kc, gp=gp, wg=wg, t0=t0, n=n: e.matmul(gp[:, 0:n], wg[:, kc * 128:(kc + 1) * 128], cx.xb[:, kc, t0:t0 + n], start=(kc == 0), stop=(kc == KC - 1)),
                             reads=[wg_b, cx.xb_b[kc][ti]], writes=[gp_b])
                    for kc in range(KC):
                        P.op("pe", lambda e, kc=kc, up=up, wu=wu, t0=t0, n=n: e.matmul(up[:, 0:n], wu[:, kc * 128:(kc + 1) * 128], cx.xb[:, kc, t0:t0 + n], start=(kc == 0), stop=(kc == KC - 1)),
                             reads=[wu_b, cx.xb_b[kc][ti]], writes=[up_b])
                    sj = s_i % 2
                    s_i += 1
                    P.op("act", lambda e, sj=sj, gp=gp, n=n: e.activation(stmp[:, sj, 0:n], gp[:, 0:n], AF.Silu), reads=[gp_b], writes=[s_b[sj]])
                    P.op("dve", lambda e, sj=sj, up=up, hi=hi, t0=t0, n=n: e.tensor_tensor(h[:, hi, t0:t0 + n], stmp[:, sj, 0:n], up[:, 0:n], ALU.mult),
                         reads=[s_b[sj], up_b], writes=[h_b[hi][ti]])
            for oc in range(KC):
                for ti, (t0, n) in enumerate(TILES_F):
                    dp, dp_b = cx.psB.get()
                    for j, (wd, wd_b, hi) in enumerate(wds):
                        P.op("pe", lambda e, dp=dp, wd=wd, hi=hi, oc=oc, t0=t0, n=n, j=j: e.matmul(dp[:, 0:n], wd[:, oc * 128:(oc + 1) * 128], h[:, hi, t0:t0 + n], start=(j == 0), stop=(j == len(wds) - 1)),
                             reads=[wd_b, h_b[hi][ti]], writes=[dp_b])
                    P.op("dve", lambda e, dp=dp, oc=oc, t0=t0, n=n: e.scalar_tensor_tensor(cx.xf[:, oc, t0:t0 + n], dp[:, 0:n], 0.5, cx.xf[:, oc, t0:t0 + n], ALU.mult, ALU.add),
                         reads=[dp_b, cx.xf_b[oc][ti]], writes=[cx.xf_b[oc][ti]])
        P.barrier()


def layer_norm(cx, li, final=False):
    P = cx.P
    nc = cx.nc
    with nc.sbuf_tensor(uid(cx, "ln_rb"), [128, 2, KC, 512], BF16) as rb, nc.sbuf_tensor(uid(cx, "ln_rsq"), [128, 2, KC, 512], BF16) as rsq, \
            nc.sbuf_tensor(uid(cx, "ln_st"), [128, 2, 4, 512], F32) as st, nc.sbuf_tensor(uid(cx, "ln_t"), [128, 4, 512], F32) as tt:
        rb_b = [[Buf() for _ in range(KC)] for _ in range(2)]
        rsq_b = [[Buf() for _ in range(KC)] for _ in range(2)]
        st_b = [[Buf() for _ in range(4)] for _ in range(2)]
        t_b = [Buf() for _ in range(4)]
        tc = [0]

        def stats(ti):
            t0, n = TILES_F[ti]
            u = ti % 2
            for kc in range(KC):
                if ti == 2:
                    P.op("pool", lambda e, kc=kc: e.tensor_copy(rb[:, u, kc, 0:n], cx.xf[:, kc, t0:t0 + n]), reads=[cx.xf_b[kc][ti]], writes=[rb_b[u][kc]])
                else:
                    P.op("act", lambda e, kc=kc: e.copy(rb[:, u, kc, 0:n], cx.xf[:, kc, t0:t0 + n]), reads=[cx.xf_b[kc][ti]], writes=[rb_b[u][kc]])
                P.op("dve" if kc % 3 else "pool", lambda e, kc=kc: e.tensor_tensor(rsq[:, u, kc, 0:n], cx.xf[:, kc, t0:t0 + n], cx.xf[:, kc, t0:t0 + n], ALU.mult), reads=[cx.xf_b[kc][ti]], writes=[rsq_b[u][kc]])
            mp, mp_b = cx.psA.get()
            ep, ep_b = cx.psA.get()
            for kc in range(KC):
                P.op("pe", lambda e, kc=kc: e.matmul(mp[:, 0:n], cx.inv2048[:], rb[:, u, kc, 0:n], start=(kc == 0), stop=(kc == KC - 1)), reads=[rb_b[u][kc], cx.c_b], writes=[mp_b])
            for kc in range(KC):
                P.op("pe", lambda e, kc=kc: e.matmul(ep[:, 0:n], cx.inv2048[:], rsq[:, u, kc, 0:n], start=(kc == 0), stop=(kc == KC - 1)), reads=[rsq_b[u][kc], cx.c_b], writes=[ep_b])
            sb_ = st_b[u]
            P.op("act", lambda e: e.copy(st[:, u, 0, 0:n], mp[:, 0:n]), reads=[mp_b], writes=[sb_[0]])
            P.op("dve", lambda e: e.tensor_tensor(st[:, u, 1, 0:n], st[:, u, 0, 0:n], st[:, u, 0, 0:n], ALU.mult), reads=[sb_[0]], writes=[sb_[1]])
            P.op("dve", lambda e: e.tensor_tensor(st[:, u, 1, 0:n], ep[:, 0:n], st[:, u, 1, 0:n], ALU.subtract), reads=[ep_b, sb_[1]], writes=[sb_[1]])
            P.op("dve", lambda e: e.tensor_scalar(st[:, u, 1, 0:n], st[:, u, 1, 0:n], LN_EPS, None, ALU.add), reads=[sb_[1]], writes=[sb_[1]])
            P.op("act", lambda e: e.activation(st[:, u, 2, 0:n], st[:, u, 1, 0:n], AF.Sqrt), reads=[sb_[1]], writes=[sb_[2]])
            P.op("dve", lambda e: e.reciprocal(st[:, u, 2, 0:n], st[:, u, 2, 0:n]), reads=[sb_[2]], writes=[sb_[2]])
            P.op("dve", lambda e: e.scalar_tensor_tensor(st[:, u, 3, 0:n], st[:, u, 0, 0:n], -1.0, st[:, u, 2, 0:n], ALU.mult, ALU.mult), reads=[sb_[0], sb_[2]], writes=[sb_[3]])

        def norm(ti):
            t0, n = TILES_F[ti]
            u = ti % 2
            sb_ = st_b[u]
            for kc in range(KC):
                a = tc[0] % 4
                tc[0] += 1
                P.op("dve", lambda e, kc=kc, a=a: e.tensor_tensor(tt[:, a, 0:n], cx.xf[:, kc, t0:t0 + n], st[:, u, 2, 0:n], ALU.mult), reads=[cx.xf_b[kc][ti], sb_[2]], writes=[t_b[a]])
                P.op("dve" if kc % 3 else "pool", lambda e, a=a: e.tensor_tensor(tt[:, a, 0:n], tt[:, a, 0:n], st[:, u, 3, 0:n], ALU.add), reads=[t_b[a], sb_[3]], writes=[t_b[a]])
                P.op("act", lambda e, kc=kc, a=a: e.activation(cx.xb[:, kc, t0:t0 + n], tt[:, a, 0:n], AF.Identity, bias=cx.lnb[:, li, kc:kc + 1], scale=cx.lng[:, li, kc:kc + 1]),
                     reads=[t_b[a], cx.c_b], writes=[cx.xb_b[kc][ti]])
                if final:
                    P.op("act", lambda e, kc=kc, a=a: e.activation(cx.xf[:, kc, t0:t0 + n], tt[:, a, 0:n], AF.Identity, bias=cx.lnb[:, li, kc:kc + 1], scale=cx.lng[:, li, kc:kc + 1]),
                         reads=[t_b[a], cx.c_b], writes=[cx.xf_b[kc][ti]])
                else:
                    P.op("act", lambda e, kc=kc, a=a: e.activation(cx.xf[:, kc, t0:t0 + n], tt[:, a, 0:n], AF.Identity, bias=cx.lnba[:, li, kc:kc + 1], scale=cx.lnga[:, li, kc:kc + 1]),
                         reads=[t_b[a], cx.c_b], writes=[cx.xf_b[kc][ti]])

        stats(0)
        stats(1)
        norm(0)
        stats(2)
        norm(1)
        norm(2)
        P.barrier()


def store_y(cx, scale=None):
    P = cx.P
    l = P.lane("yout")
    for kc in range(KC):
        P.dma("sp", cx.d["yT"][kc], cx.xf[:, kc, :], l, reads=cx.xf_b[kc])


def run_chains(factories, K):
    free = list(range(K))
    active = []
    it = iter(factories)
    done = False
    while True:
        while free and not done:
            f = next(it, None)
            if f is None:
                done = True
                break
            sl = free.pop(0)
            active.append((f(sl), sl))
        if not active:
            break
        for g, sl in list(active):
            try:
                next(g)
            except StopIteration:
                active.remove((g, sl))
                free.append(sl)


def proj_fm(cx, w, w_b, src, src_b, tiles, consumer, pool=None):
    P = cx.P
    pool = pool or cx.psA
    for ti, (t0, n) in enumerate(tiles):
        ps, ps_b = pool.get()
        for kc in range(KC):
            P.op("pe", lambda e, kc=kc, ps=ps, t0=t0, n=n: e.matmul(ps[:, 0:n], w[:, kc * 128:(kc + 1) * 128], src[:, kc, t0:t0 + n], start=(kc == 0), stop=(kc == KC - 1)),
                 reads=[w_b, src_b[kc][ti]], writes=[ps_b])
        consumer(ti, t0, n, ps, ps_b)


TCH = [(i * 128, 128, i // 4) for i in range(8)] + [(1024, 16, 2)]


def proj_tm(cx, w, w_b, src, src_b, tch, consumer, ncols=128, pool=None):
    P = cx.P
    pool = pool or cx.psA
    for ci, (t0, m, ti) in enumerate(tch):
        ps, ps_b = pool.get()
        for kc in range(KC):
            P.op("pe", lambda e, kc=kc, ps=ps, t0=t0, m=m: e.matmul(ps[0:m, 0:ncols], src[:, kc, t0:t0 + m], w[:, kc * ncols:(kc + 1) * ncols], start=(kc == 0), stop=(kc == KC - 1)),
                 reads=[w_b, src_b[kc][ti]], writes=[ps_b])
        consumer(ci, t0, m, ps, ps_b)


def out_proj_residual(cx, w_d, src, src_b):
    P = cx.P
    with ring(cx):
        ws = WStream(cx, [(w_d[c], 2048) for c in range(KC)], la=8)
        for oc in range(KC):
            w, w_b = ws.get(oc)

            def cons(ti, t0, n, ps, ps_b, oc=oc):
                P.op("dve", lambda e: e.tensor_tensor(cx.xf[:, oc, t0:t0 + n], ps[:, 0:n], cx.xf[:, oc, t0:t0 + n], ALU.add),
                     reads=[ps_b, cx.xf_b[oc][ti]], writes=[cx.xf_b[oc][ti]])
            proj_fm(cx, w, w_b, src, src_b, TILES_F, cons)
        P.barrier()


CF = {"ones": 0, "triinc": 128, "trigt": 256, "sel127": 384, "sel15": 512}
CF_N = 640
CB = {"ident": 0, "ones": 128, "inv2048": 256, "inv128": 384, "triinc": 512, "negtrige": 640, "trilt": 768, "zeros": 896, "negtrilt": 1024}
CB_N = 1152
BIG = 30000.0
SC128 = 128.0 ** -0.5
SC512 = 512.0 ** -0.5
RMS_EPS = 1e-6


def host_consts():
    import numpy as _np
    f = _np.zeros((128, CF_N), _np.float32)
    r = _np.arange(128)
    f[:, 0:128] = 1.0
    f[:, 128:256] = (r[:, None] <= r[None, :])
    f[:, 256:384] = (r[:, None] > r[None, :])
    f[127, 384:512] = 1.0
    f[15, 512:640] = 1.0
    b = _np.zeros((128, CB_N), _np.float32)
    b[:, 0:128] = _np.eye(128)
    b[:, 128:256] = 1.0
    b[:, 256:384] = 1.0 / 2048.0
    b[:, 384:512] = 1.0 / 128.0
    b[:, 512:640] = (r[:, None] <= r[None, :])
    b[:, 640:768] = -1.0 * (r[:, None] >= r[None, :])
    b[:, 768:896] = (r[:, None] < r[None, :])
    b[:, 1024:1152] = -1.0 * (r[:, None] < r[None, :])
    return f, b


def setup_consts2(cx):
    P = cx.P
    sb = cx.sb
    cx.cf = sb("s_cf", [128, CF_N], F32)
    cx.cb = sb("s_cb", [128, CB_N], BF16)
    cx.flag = sb("s_flag", [128, 2], F32)
    P.dma("sp", cx.cf[:], cx.d["cf"], cx.misc_lane, writes=[cx.c_b])
    P.dma("pool", cx.cb[:], cx.d["cb"], cx.misc_lane, writes=[cx.c_b])
    P.dma("sp", cx.flag[:], cx.d["flag"], cx.misc_lane, writes=[cx.c_b])
    cx.cfv = lambda k, m=128, n=128: cx.cf[0:m, CF[k]:CF[k] + n]
    cx.cbv = lambda k, m=128, n=128: cx.cb[0:m, CB[k]:CB[k] + n]


class Stage:
    def __init__(self, cx, name, shape, dt, n=2):
        self.t = cx.sb(name, [128, n] + shape, dt)
        self.n = n
        self.b = [Buf(f"{name}{i}") for i in range(n)]
        self.l = [cx.P.lane(name) for i in range(n)]
        self.i = 0

    def get(self):
        a = self.i % self.n
        self.i += 1
        return a, self.b[a], self.l[a]


def even_mixer(cx, j=0):
    P = cx.P
    nc = cx.nc
    d = cx.d
    esm = contextlib.ExitStack()
    sb = lambda name, shape, dt: esm.enter_context(nc.sbuf_tensor(name, shape, dt))
    cb = cx.cbv
    cf = cx.cfv
    G2 = cx.groups
    merged = cx.xb
    merged_b = cx.xb_b
    lf = sb("ev_lf", [128, 9, 8], F32)
    lf_b = Buf("lf")
    sm = sb("ev_small", [128, 1024], F32)
    sm_b = Buf("sm")
    cum_loc = sm[:, 0:72].rearrange("p (i h) -> p i h", h=8)
    cum_cache = sm[:, 72:136].rearrange("p (i h) -> p i h", h=8)
    ck_rem = sm[:, 136:200].rearrange("p (i h) -> p i h", h=8)
    AT = sm[:, 200:208]
    G_loc = sm[:, 232:296].rearrange("p (i h) -> p i h", h=8)
    cfl = sm[:, 296:360].rearrange("p (i h) -> p i h", h=8)
    bfb = sm[:, 688:696]
    ng = sm[:, 696:704]
    lbt = sb("ev_lb", [128, 2, 1024], F32)
    lb_b = Buf("lb")

    P.op("dve", lambda e: e.memset(lf[:], 0.0), writes=[lf_b])
    P.op("dve", lambda e: e.memset(sm[:], 0.0), writes=[sm_b])
    P.dma("sp", bfb, d["fox_bf"], cx.L(16), writes=[sm_b])
    P.dma("sp", ng, d["hgrn_ng"], cx.L(16), writes=[sm_b])
    P.dma("sp", cfl, d["cflogf"], cx.L(16), writes=[sm_b])
    P.dma("sp", lbt[:, 0:2, :], d["lb_logits"], cx.L(17), writes=[lb_b])
    P.op("dve", lambda e: e.tensor_tensor(lbt[:, 0, :], lbt[:, 0, :], lbt[:, 1, :], ALU.subtract), reads=[lb_b], writes=[lb_b])
    P.op("act", lambda e: e.activation(lbt[:, 0, :], lbt[:, 0, :], AF.Sigmoid), reads=[lb_b], writes=[lb_b])
    P.op("dve", lambda e: e.tensor_scalar(lbt[:, 1, :], lbt[:, 0, :], -1.0, 1.0, ALU.mult, ALU.add), reads=[lb_b], writes=[lb_b])
    lb_bc = lambda h: lbt[:, 0, h * 128:(h + 1) * 128]
    oml_bc = lambda h: lbt[:, 1, h * 128:(h + 1) * 128]

    es1 = contextlib.ExitStack()
    es1.enter_context(ring(cx))
    sb_save = cx.sb
    cx.sb = lambda name, shape, dt: es1.enter_context(nc.sbuf_tensor(name, shape, dt))
    sfm = Stage(cx, "e1_sfm", [NT], F32)
    bfm = Stage(cx, "e1_bfm", [NT], BF16)
    stm = Stage(cx, "e1_stm", [9, 128], F32)
    btm = Stage(cx, "e1_btm", [9, 128], BF16)
    P.op("pool", lambda e: e.memset(stm.t[:], 0.0), writes=stm.b)
    P.op("pool", lambda e: e.memset(btm.t[:], 0.0), writes=btm.b)
    W = d["ev_win"]
    order = []
    for h in range(8):
        order.append(("ka", h, h))
    for h in range(8):
        order.append(("va", h, 8 + h))
    order.append(("fa", 0, 56))
    for h in range(8):
        order.append(("fb", h, 32 + h))
    for h in range(8):
        order.append(("ib", h, 40 + h))
    for h in range(8):
        order.append(("qb", h, 24 + h))
    for h in range(8):
        order.append(("gb", h, 48 + h))
    for h in range(8):
        order.append(("qa", h, 16 + h))
    srcs = [((d["ev_wfa"], 128) if k == "fa" else (W[c], 2048)) for (k, h, c) in order]
    ws = WStream(cx, srcs, la=8)
    TY = {"qb": 0, "fb": 1, "ib": 2}
    for oi, (kind, h, c) in enumerate(order):
        w, w_b = ws.get(oi)
        if kind == "ka":
            a, fb_, fl = sfm.get()
            a2, bb_, bl = bfm.get()

            def cons(ti, t0, n, ps, ps_b):
                P.op("act", lambda e: e.copy(sfm.t[:, a, t0:t0 + n], ps[:, 0:n]), reads=[ps_b], writes=[fb_])
                P.op("dve", lambda e: e.tensor_copy(bfm.t[:, a2, t0:t0 + n], ps[:, 0:n]), reads=[ps_b], writes=[bb_])
            proj_fm(cx, w, w_b, cx.xb, cx.xb_b, TILES_F, cons)
            P.dma("sp", d["fox_kT"][h], sfm.t[:, a, :], fl, reads=[fb_])
            P.dma("sp", d["ka_s"][h], bfm.t[:, a2, :], bl, reads=[bb_])
            P.dma("sp", d["xg_key_in"][h * 128:(h + 1) * 128, :], bfm.t[:, a2, 0:1024], bl, reads=[bb_])
        elif kind == "va":
            a, fb_, fl = stm.get()
            a2, bb_, bl = btm.get()

            def cons(ci, t0, m, ps, ps_b):
                P.op("act", lambda e: e.copy(stm.t[0:m, a, ci, :], ps[0:m, 0:128]), reads=[ps_b], writes=[fb_])
                P.op("dve", lambda e: e.tensor_copy(btm.t[0:m, a2, ci, :], ps[0:m, 0:128]), reads=[ps_b], writes=[bb_])
            proj_tm(cx, w, w_b, cx.xb, cx.xb_b, TCH, cons)
            cs = slice(h * 128, (h + 1) * 128)
            P.dma("sp", d["fox_v"][0:1024, cs].rearrange("(i p) f -> p i f", p=128), stm.t[:, a, 0:8, :], fl, reads=[fb_])
            P.dma("sp", d["fox_v"][1024:1040, cs], stm.t[0:16, a, 8, :], fl, reads=[fb_])
            P.dma("sp", d["va_s"][0:1024, cs].rearrange("(i p) f -> p i f", p=128), btm.t[:, a2, 0:8, :], bl, reads=[bb_])
            P.dma("sp", d["va_s"][1024:1040, cs], btm.t[0:16, a2, 8, :], bl, reads=[bb_])
            P.dma("sp", d["xg_val_in"][0:1024, cs].rearrange("(i p) f -> p i f", p=128), btm.t[:, a2, 0:8, :], bl, reads=[bb_])
        elif kind == "qa":
            a2, bb_, bl = bfm.get()

            def cons(ti, t0, n, ps, ps_b):
                P.op("dve", lambda e: e.tensor_scalar(bfm.t[:, a2, t0:t0 + n], ps[:, 0:n], SC128, None, ALU.mult), reads=[ps_b], writes=[bb_])
            proj_fm(cx, w, w_b, cx.xb, cx.xb_b, TILES_F, cons)
            P.dma("sp", d["qa_s"][h], bfm.t[:, a2, :], bl, reads=[bb_])
        elif kind == "gb":
            a2, bb_, bl = bfm.get()

            def cons(ti, t0, n, ps, ps_b):
                P.op("act", lambda e: e.activation(bfm.t[:, a2, t0:t0 + n], ps[:, 0:n], AF.Silu), reads=[ps_b], writes=[bb_])
            proj_fm(cx, w, w_b, cx.xb, cx.xb_b, TILES_F, cons)
            P.dma("sp", d["hg_s"][h], bfm.t[:, a2, :], bl, reads=[bb_])
        elif kind in TY:
            a, fb_, fl = stm.get()

            def cons(ci, t0, m, ps, ps_b):
                P.op("act", lambda e: e.copy(stm.t[0:m, a, ci, :], ps[0:m, 0:128]), reads=[ps_b], writes=[fb_])
            proj_tm(cx, w, w_b, cx.xb, cx.xb_b, TCH, cons)
            P.dma("sp", d["hraw_s"][h, :, :, TY[kind], :].rearrange("i p f -> p i f"), stm.t[:, a, :, :], fl, reads=[fb_])
        elif kind == "fa":
            def cons(ci, t0, m, ps, ps_b):
                P.op("dve", lambda e: e.tensor_tensor(lf[0:m, ci, :], ps[0:m, 0:8], bfb[0:m, :], ALU.add), reads=[ps_b, sm_b], writes=[lf_b])
            proj_tm(cx, w, w_b, cx.xb, cx.xb_b, TCH, cons, ncols=8)
            P.op("act", lambda e: e.activation(lf[:], lf[:], AF.Exp, scale=-1.0), reads=[lf_b], writes=[lf_b])
            P.op("act", lambda e: e.activation(lf[:], lf[:], AF.Ln, bias=1.0), reads=[lf_b], writes=[lf_b])
            P.op("dve", lambda e: e.tensor_scalar(lf[:], lf[:], -1.0, None, ALU.mult), reads=[lf_b], writes=[lf_b])
            P.dma("sp", d["fox_logf"][0:1024, :].rearrange("(i p) h -> p i h", p=128), lf[:, 0:8, :], cx.L(18), reads=[lf_b])
            P.dma("sp", d["fox_logf"][1024:1040, :], lf[0:16, 8, :], cx.L(18), reads=[lf_b])
            cp, cp_b = cx.psB.get()
            for i in range(8):
                for i2 in range(i):
                    P.op("pe", lambda e, i=i, i2=i2: e.matmul(cp[:, i * 8:(i + 1) * 8], cf("ones"), lf[:, i2, :], start=(i2 == 0), stop=False), reads=[lf_b, cx.c_b], writes=[cp_b])
                P.op("pe", lambda e, i=i: e.matmul(cp[:, i * 8:(i + 1) * 8], cf("triinc"), lf[:, i, :], start=(i == 0), stop=True), reads=[lf_b, cx.c_b], writes=[cp_b])
            P.op("dve", lambda e: e.tensor_copy(sm[:, 0:64], cp[:, 0:64]), reads=[cp_b], writes=[sm_b])
            cp2, cp2_b = cx.psB.get()
            for i in range(8):
                for i2 in range(i):
                    P.op("pe", lambda e, i=i, i2=i2: e.matmul(cp2[:, i * 8:(i + 1) * 8], cf("ones"), cfl[:, i2, :], start=(i2 == 0), stop=False), reads=[sm_b, cx.c_b], writes=[cp2_b])
                P.op("pe", lambda e, i=i: e.matmul(cp2[:, i * 8:(i + 1) * 8], cf("triinc"), cfl[:, i, :], start=(i == 0), stop=True), reads=[sm_b, cx.c_b], writes=[cp2_b])
            for i2 in range(8):
                P.op("pe", lambda e, i2=i2: e.matmul(cp2[0:16, 64:72], cf("ones", 128, 16), cfl[:, i2, :], start=(i2 == 0), stop=False), reads=[sm_b, cx.c_b], writes=[cp2_b])
            P.op("pe", lambda e: e.matmul(cp2[0:16, 64:72], cf("triinc", 16, 16), lf[0:16, 8, :], start=False, stop=True), reads=[lf_b, cx.c_b], writes=[cp2_b])
            P.op("dve", lambda e: e.tensor_copy(sm[:, 72:136], cp2[:, 0:64]), reads=[cp2_b], writes=[sm_b])
            P.op("dve", lambda e: e.tensor_copy(sm[0:16, 64:72], cp2[0:16, 64:72]), reads=[cp2_b], writes=[sm_b])
            P.dma("sp", d["xg_c_in"], sm[:, 0:64], cx.L(19), reads=[sm_b])
    es1.close()
    cx.sb = sb_save
    P.barrier()
    if getattr(cx, "stop", "") == "e1":
        return
    P.coll("AllGather", [d["xg_key_in_t"].ap().opt()], [d["xg_key_out_t"].ap().opt()], G2, cx.cc_lane, writes=[cx.xgo_b])
    P.coll("AllGather", [d["xg_val_in_t"].ap().opt()], [d["xg_val_out_t"].ap().opt()], G2, cx.cc_lane, writes=[cx.xgo_b])
    P.coll("AllGather", [d["xg_c_in_t"].ap().opt()], [d["xg_c_out_t"].ap().opt()], G2, cx.cc_lane, writes=[cx.xgco_b])
    hgrn_pass(cx, True, lb_bc, oml_bc, lb_b, ng, sm_b, merged, merged_b)
    P.barrier()
    P.coll("AllGather", [d["xg_s_in_t"].ap().opt()], [d["xg_s_out_t"].ap().opt()], G2, cx.cc_lane, writes=[cx.xgso_b])
    P.dma("sp", sm[:, 136:200], d["xg_c_out"][0:128, :], cx.L(19), reads=[cx.xgco_b], writes=[sm_b])
    tp, tp_b = cx.psB.get()
    P.op("pe", lambda e: e.matmul(tp[:, 0:8], cf("sel127"), ck_rem[:, 7, :], start=True, stop=True), reads=[sm_b, cx.c_b], writes=[tp_b])
    P.op("dve", lambda e: e.tensor_scalar(AT, tp[:, 0:8], cx.flag[:, 0:1], None, ALU.mult), reads=[tp_b, cx.c_b], writes=[sm_b])
    for i in range(8):
        P.op("dve", lambda e, i=i: e.tensor_tensor(G_loc[:, i, :], cum_loc[:, i, :], AT, ALU.add), reads=[sm_b], writes=[sm_b])
    ft = sb("ev_ft", [128, 1200], F32)
    ft_b = sm_b
    fb_loc = ft[:, 0:512].rearrange("p (h j i) -> p h j i", h=8, j=8)
    fb_rem = ft[:, 512:1024].rearrange("p (h j i) -> p h j i", h=8, j=8)
    fb_cache = ft[:, 1024:1088].rearrange("p (h i) -> p h i", h=8)
    fb_sloc = ft[:, 1088:1096]
    cref = ft[:, 1100:1172].rearrange("p (j h) -> p j h", h=8)
    P.op("dve", lambda e: e.memset(ft[:], 0.0), writes=[sm_b])
    tp, tp_b = cx.psB.get()
    for jq in range(8):
        P.op("pe", lambda e, jq=jq: e.matmul(tp[:, jq * 8:(jq + 1) * 8], cf("sel127"), G_loc[:, jq, :], start=True, stop=True), reads=[sm_b, cx.c_b], writes=[tp_b])
    P.op("pe", lambda e: e.matmul(tp[:, 64:72], cf("sel15", 16, 128), cum_loc[0:16, 8, :], start=True, stop=True), reads=[sm_b, cx.c_b], writes=[tp_b])
    P.op("dve", lambda e: e.tensor_copy(ft[:, 1100:1172], tp[:, 0:72]), reads=[tp_b], writes=[sm_b])
    for h in range(8):
        for jq in range(8):
            ni = jq + 1
            tb1, tb2 = Buf(), Buf()
            P.op("dve", lambda e, h=h, jq=jq, ni=ni: e.tensor_scalar(fb_loc[:, h, jq, 0:ni], G_loc[:, 0:ni, h], cref[:, jq, h:h + 1], -1.0, ALU.subtract, ALU.mult), reads=[sm_b], writes=[tb1])
            P.op("pool", lambda e, h=h, jq=jq: e.tensor_scalar(fb_rem[:, h, jq, :], ck_rem[:, :, h], cref[:, jq, h:h + 1], -1.0, ALU.subtract, ALU.mult), reads=[sm_b], writes=[tb2])
            P.op("pool", lambda e, h=h, jq=jq: e.tensor_scalar(fb_rem[:, h, jq, :], fb_rem[:, h, jq, :], cx.flag[:, 1:2], None, ALU.add), reads=[tb2, cx.c_b], writes=[tb2])
        tb3, tb4 = Buf(), Buf()
        P.op("dve", lambda e, h=h: e.tensor_scalar(fb_cache[:, h, :], cum_cache[:, :, h], cref[:, 8, h:h + 1], -1.0, ALU.subtract, ALU.mult), reads=[sm_b], writes=[tb3])
        P.op("dve", lambda e, h=h: e.tensor_scalar(fb_sloc[0:16, h:h + 1], cum_loc[0:16, 8, h:h + 1], cref[0:16, 8, h:h + 1], -1.0, ALU.subtract, ALU.mult), reads=[sm_b], writes=[tb4])
    P.barrier()
    if getattr(cx, "stop", "") == "x1":
        return
    tabs = dict(fb_loc=fb_loc, fb_rem=fb_rem, fb_cache=fb_cache, fb_sloc=fb_sloc, sm_b=sm_b)
    if getattr(cx, "stop", "") == "h1":
        return
    fox_heads(cx, tabs, merged, merged_b)
    P.barrier()
    if getattr(cx, "stop", "") == "fox":
        return
    hgrn_pass(cx, False, lb_bc, oml_bc, lb_b, ng, sm_b, merged, merged_b)
    P.barrier()
    esm.close()
    out_proj_residual(cx, d["ev_wout"], merged, merged_b)


def hgrn_pass(cx, state_only, lb_bc, oml_bc, lb_b, ng, sm_b, merged, merged_b):
    P = cx.P
    nc = cx.nc
    d = cx.d
    cb = cx.cbv
    cf = cx.cfv
    sfx = "1" if state_only else "2"
    with contextlib.ExitStack() as es:
        sbl = lambda name, shape, dt: es.enter_context(nc.sbuf_tensor(name + sfx, shape, dt))
        raw = sbl("hg_raw", [128, 9, 3, 128], F32); raw_b = Buf()
        f_ = sbl("hg_f", [128, 9, 128], F32); f_b = Buf()
        lg = sbl("hg_lg", [128, 9, 128], F32); lg_b = Buf()
        erb = sbl("hg_erb", [128, 9, 128], F32); erb_b = Buf()
        Kh = sbl("hg_Kh", [128, 9, 128], BF16); Kh_b = Buf()
        ib = sbl("hg_ib", [128, 9, 128], BF16); ib_b = Buf()
        S = sbl("hg_S", [128, 128], F32); S_b = Buf()
        Sb = sbl("hg_Sb", [128, 128], BF16); Sb_b = Buf()
        el = sbl("hg_el", [128, 16], F32); el_b = Buf()
        if not state_only:
            eb = sbl("hg_eb", [128, 9, 128], F32); eb_b = Buf()
            enb = sbl("hg_enb", [128, 9, 128], F32); enb_b = Buf()
            qs = lg; qs_b = lg_b
            Qt = sbl("hg_Qt", [128, 9, 128], BF16); Qt_b = Buf()
            Kt = sbl("hg_Kt", [128, 9, 128], BF16); Kt_b = Buf()
            QtT = sbl("hg_QtT", [128, 9, 128], BF16); QtT_b = Buf()
            KtT = sbl("hg_KtT", [128, 9, 128], BF16); KtT_b = Buf()
            sc = sbl("hg_sc", [128, 9, 128], BF16); sc_b = Buf()
            Sball = sbl("hg_Sball", [128, 9, 128], BF16); Sball_b = Buf()
            ob = sbl("hg_ob", [128, NT], F32); ob_b = Buf()
            gt = sbl("hg_g", [128, NT], BF16); gt_b = Buf()
            sq = sbl("hg_sq", [128, 512], BF16); sq_b = Buf()
            rs = sbl("hg_rs", [128, 512], F32); rs_b = Buf()
            P.op("pool", lambda e: e.memset(sc[:], 0.0), writes=[sc_b])
        l_raw, l_S, l_sin, l_g = cx.L(14), cx.L(15), cx.L(16), cx.L(17)
        GR = [(0, 4), (4, 8), (8, 9)]
        nblk = 8 if state_only else 9
        for h in range(8):
            P.dma("sp", raw[:], d["hraw_s"][h].rearrange("i p t f -> p i t f"), l_raw, writes=[raw_b])
            if not state_only:
                P.dma("sp", gt[:], d["hg_s"][h], l_g, writes=[gt_b])
            P.op("act", lambda e: e.activation(f_[:], raw[:, :, 1, :], AF.Sigmoid), reads=[raw_b], writes=[f_b])
            for bi in range(9):
                P.op("dve", lambda e, bi=bi: e.tensor_tensor(f_[:, bi, :], f_[:, bi, :], oml_bc(h), ALU.mult), reads=[f_b, lb_b], writes=[f_b])
                P.op("dve", lambda e, bi=bi: e.tensor_tensor(f_[:, bi, :], f_[:, bi, :], lb_bc(h), ALU.add), reads=[f_b, lb_b], writes=[f_b])
            P.op("act", lambda e: e.activation(lg[:], f_[:], AF.Ln), reads=[f_b], writes=[lg_b])
            P.op("dve", lambda e: e.tensor_scalar(f_[:], f_[:], -1.0, 1.0, ALU.mult, ALU.add), reads=[f_b], writes=[f_b])
            P.op("pool", lambda e: e.tensor_copy(ib[:], raw[:, :, 2, :]), reads=[raw_b], writes=[ib_b])
            tp, tp_b = cx.psB.get()
            for (g0, g1) in GR:
                rp, rp_b = cx.psA.get()
                if not state_only:
                    bp, bp_b = cx.psA.get()
                for bi in range(g0, g1):
                    m = 128 if bi < 8 else 16
                    c0 = (bi - g0) * 128
                    P.op("pe", lambda e, bi=bi, m=m, c0=c0: e.matmul(rp[0:m, c0:c0 + 128], cf("trigt", m, m), lg[0:m, bi, :], start=True, stop=True), reads=[lg_b, cx.c_b], writes=[rp_b])
                    if not state_only:
                        P.op("pe", lambda e, bi=bi, m=m, c0=c0: e.matmul(bp[0:m, c0:c0 + 128], cf("triinc", m, m), lg[0:m, bi, :], start=True, stop=True), reads=[lg_b, cx.c_b], writes=[bp_b])
                    P.op("pe", lambda e, bi=bi, m=m: e.matmul(tp[:, bi:bi + 1], lg[0:m, bi, :], cf("ones", m, 1), start=True, stop=True), reads=[lg_b, cx.c_b], writes=[tp_b])
                ncol = (g1 - g0) * 128
                P.op("act", lambda e, g0=g0, g1=g1, ncol=ncol: e.activation(erb[:, g0:g1, :], rp[:, 0:ncol].rearrange("p (i f) -> p i f", f=128), AF.Exp), reads=[rp_b], writes=[erb_b])
                if not state_only:
                    P.op("act", lambda e, g0=g0, g1=g1, ncol=ncol: e.activation(eb[:, g0:g1, :], bp[:, 0:ncol].rearrange("p (i f) -> p i f", f=128), AF.Exp), reads=[bp_b], writes=[eb_b])
                    P.op("act", lambda e, g0=g0, g1=g1, ncol=ncol: e.activation(enb[:, g0:g1, :], bp[:, 0:ncol].rearrange("p (i f) -> p i f", f=128), AF.Exp, scale=-1.0), reads=[bp_b], writes=[enb_b])
            P.op("act", lambda e: e.activation(el[:, 0:9], tp[:, 0:9], AF.Exp), reads=[tp_b], writes=[el_b])
            if not state_only:
                P.op("act", lambda e: e.activation(qs[:], raw[:, :, 0, :], AF.Silu), reads=[raw_b], writes=[qs_b])
            P.op("pool", lambda e: e.tensor_tensor(Kh[:], f_[:], erb[:], ALU.mult), reads=[f_b, erb_b], writes=[Kh_b])
            if not state_only:
                P.op("dve", lambda e: e.tensor_tensor(Qt[:], qs[:], eb[:], ALU.mult), reads=[qs_b, eb_b], writes=[Qt_b])
                P.op("dve", lambda e: e.tensor_tensor(Kt[:], f_[:], enb[:], ALU.mult), reads=[f_b, enb_b], writes=[Kt_b])
                for (g0, g1) in GR:
                    ncol = (g1 - g0) * 128
                    qp, qp_b = cx.psA.get()
                    kp, kp_b = cx.psA.get()
                    for bi in range(g0, g1):
                        m = 128 if bi < 8 else 16
                        c0 = (bi - g0) * 128
                        P.op("pe", lambda e, bi=bi, m=m, c0=c0: e.matmul(qp[:, c0:c0 + m], Qt[0:m, bi, :], cb("ident", m, m), start=True, stop=True), reads=[Qt_b, cx.c_b], writes=[qp_b])
                        P.op("pe", lambda e, bi=bi, m=m, c0=c0: e.matmul(kp[:, c0:c0 + m], Kt[0:m, bi, :], cb("ident", m, m), start=True, stop=True), reads=[Kt_b, cx.c_b], writes=[kp_b])
                    if g0 < 8:
                        P.op("act", lambda e, g0=g0, g1=g1, ncol=ncol: e.copy(QtT[:, g0:g1, :], qp[:, 0:ncol].rearrange("p (i f) -> p i f", f=128)), reads=[qp_b], writes=[QtT_b])
                        P.op("dve", lambda e, g0=g0, g1=g1, ncol=ncol: e.tensor_copy(KtT[:, g0:g1, :], kp[:, 0:ncol].rearrange("p (i f) -> p i f", f=128)), reads=[kp_b], writes=[KtT_b])
                    else:
                        P.op("act", lambda e: e.copy(QtT[:, 8, 0:16], qp[:, 0:16]), reads=[qp_b], writes=[QtT_b])
                        P.op("dve", lambda e: e.tensor_copy(KtT[:, 8, 0:16], kp[:, 0:16]), reads=[kp_b], writes=[KtT_b])
                for (g0, g1) in GR:
                    sp_, sp_b = cx.psA.get()
                    for bi in range(g0, g1):
                        m = 128 if bi < 8 else 16
                        c0 = (bi - g0) * 128
                        P.op("pe", lambda e, bi=bi, m=m, c0=c0: e.matmul(sp_[0:m, c0:c0 + m], KtT[:, bi, 0:m], QtT[:, bi, 0:m], start=True, stop=True), reads=[KtT_b, QtT_b], writes=[sp_b])
                    for bi in range(g0, g1):
                        m = 128 if bi < 8 else 16
                        c0 = (bi - g0) * 128
                        P.op("dve", lambda e, bi=bi, m=m, c0=c0: e.tensor_tensor(sc[0:m, bi, 0:m], sp_[0:m, c0:c0 + m], cb("triinc", m, m), ALU.mult), reads=[sp_b, cx.c_b], writes=[sc_b])
            s2t = []
            for g3 in range(3):
                s2t.append(cx.psB.get())
            for bi in range(nblk):
                m = 128 if bi < 8 else 16
                s2, s2_b = s2t[bi // 4]
                c0 = (bi % 4) * 128
                P.op("pe", lambda e, bi=bi, m=m, c0=c0: e.matmul(s2[:, c0:c0 + 128], Kh[0:m, bi, :], ib[0:m, bi, :], start=True, stop=True), reads=[Kh_b, ib_b], writes=[s2_b])
            for bi in range(nblk):
                s2, s2_b = s2t[bi // 4]
                c0 = (bi % 4) * 128
                if bi == 0:
                    if state_only:
                        P.op("dve", lambda e: e.memset(S[:], 0.0), writes=[S_b])
                    else:
                        P.dma("sp", S[:], d["xg_s_out"][h * 128:(h + 1) * 128, :], l_sin, reads=[cx.xgso_b], writes=[S_b])
                        P.op("dve", lambda e: e.tensor_scalar(S[:], S[:], cx.flag[:, 0:1], None, ALU.mult), reads=[S_b, cx.c_b], writes=[S_b])
                if bi == 8:
                    P.dma("sp", S[:], d["hstate_in"][h], l_sin, writes=[S_b])
                if not state_only:
                    P.op("act", lambda e, bi=bi: e.copy(Sball[:, bi, :], S[:]), reads=[S_b], writes=[Sball_b])
                P.op("dve", lambda e, bi=bi, c0=c0: e.scalar_tensor_tensor(S[:], S[:], el[:, bi:bi + 1], s2[:, c0:c0 + 128], ALU.mult, ALU.add), reads=[S_b, el_b, s2_b], writes=[S_b])
                if bi == 7:
                    if state_only:
                        P.dma("sp", d["xg_s_in"][h * 128:(h + 1) * 128, :], S[:], l_S, reads=[S_b])
                    else:
                        P.dma("sp", d["hstate_p"][h], S[:], l_S, reads=[S_b])
                if bi == 8:
                    P.dma("sp", d["hstate_s"][h], S[:], l_S, reads=[S_b])
            if not state_only:
                for bi in range(nblk):
                    m = 128 if bi < 8 else 16
                    t0 = bi * 128
                    op_, op_b = cx.psA.get()
                    P.op("pe", lambda e, bi=bi, m=m: e.matmul(op_[:, 0:m], ib[0:m, bi, :], sc[0:m, bi, 0:m], start=True, stop=False), reads=[ib_b, sc_b], writes=[op_b])
                    P.op("pe", lambda e, bi=bi, m=m: e.matmul(op_[:, 0:m], Sball[:, bi, :], QtT[:, bi, 0:m], start=False, stop=True), reads=[Sball_b, QtT_b], writes=[op_b])
                    P.op("act", lambda e, m=m, t0=t0: e.copy(ob[:, t0:t0 + m], op_[:, 0:m]), reads=[op_b], writes=[ob_b])
            if state_only:
                continue
            for ti, (t0, n) in enumerate(TILES):
                P.op("pool", lambda e, t0=t0, n=n: e.tensor_tensor(sq[:, 0:n], ob[:, t0:t0 + n], ob[:, t0:t0 + n], ALU.mult), reads=[ob_b], writes=[sq_b])
                mp, mp_b = cx.psA.get()
                P.op("pe", lambda e, n=n: e.matmul(mp[:, 0:n], cb("inv128"), sq[:, 0:n], start=True, stop=True), reads=[sq_b, cx.c_b], writes=[mp_b])
                P.op("dve", lambda e, n=n: e.tensor_scalar(rs[:, 0:n], mp[:, 0:n], RMS_EPS, None, ALU.add), reads=[mp_b], writes=[rs_b])
                P.op("act", lambda e, n=n: e.activation(rs[:, 0:n], rs[:, 0:n], AF.Sqrt), reads=[rs_b], writes=[rs_b])
                P.op("dve", lambda e, n=n: e.reciprocal(rs[:, 0:n], rs[:, 0:n]), reads=[rs_b], writes=[rs_b])
                P.op("dve", lambda e, t0=t0, n=n: e.tensor_tensor(rs[:, 0:n], ob[:, t0:t0 + n], rs[:, 0:n], ALU.mult), reads=[ob_b, rs_b], writes=[rs_b])
                P.op("dve", lambda e, t0=t0, n=n: e.scalar_tensor_tensor(merged[:, 8 + h, t0:t0 + n], rs[:, 0:n], ng[:, h:h + 1], gt[:, t0:t0 + n], ALU.mult, ALU.mult),
                     reads=[rs_b, sm_b, gt_b], writes=[merged_b[8 + h][ti]])


def fox_heads(cx, tabs, merged, merged_b):
    P = cx.P
    nc = cx.nc
    d = cx.d
    cb = cx.cbv
    fb_loc, fb_rem, fb_cache, fb_sloc, sm_b = tabs["fb_loc"], tabs["fb_rem"], tabs["fb_cache"], tabs["fb_sloc"], tabs["sm_b"]
    K = 2
    with contextlib.ExitStack() as es:
        sbl = lambda name, shape, dt: es.enter_context(nc.sbuf_tensor(name, shape, dt))
        NS = 3
        qT = sbl("fx_qT", [128, NS, NT], BF16)
        kT = sbl("fx_kT", [128, NS, NT], BF16)
        vl = sbl("fx_vl", [128, NS, 9, 128], BF16)
        kTr = sbl("fx_kTr", [128, NS, 1024], BF16)
        vr = sbl("fx_vr", [128, NS, 8, 128], BF16)
        kTc = sbl("fx_kTc", [128, NS, 1024], BF16)
        vc = sbl("fx_vc", [128, NS, 8, 128], BF16)
        in_b = [[Buf() for _ in range(7)] for _ in range(NS)]
        pp = sbl("fx_p", [128, K, 2, 512], BF16)
        pp_b = [[Buf(), Buf()] for _ in range(K)]
        rd = sbl("fx_rd", [128, K, 512], F32)
        rd_b = [Buf() for _ in range(K)]
        spools = [PsPool(cx.ps[4 * k + 2:4 * k + 4]) for k in range(K)]
        loaded = set()

        def load_head(h):
            if h in loaded:
                return
            loaded.add(h)
            s = h % NS
            ib_ = in_b[s]
            L = lambda i: cx.lanes2_[s * 7 + i]
            cs = slice(h * 128, (h + 1) * 128)
            P.dma("sp", qT[:, s, :], d["qa_s"][h], L(0), writes=[ib_[0]])
            P.dma("sp", kT[:, s, :], d["ka_s"][h], L(1), writes=[ib_[1]])
            P.dma("sp", vl[:, s, 0:8, :], d["va_s"][0:1024, cs].rearrange("(i p) f -> p i f", p=128), L(2), writes=[ib_[2]])
            P.dma("sp", vl[0:16, s, 8, :], d["va_s"][1024:1040, cs], L(2), writes=[ib_[2]])
            P.dma("sp", kTr[:, s, :], d["xg_key_out"][h * 128:(h + 1) * 128, :], L(3), reads=[cx.xgo_b], writes=[ib_[3]])
            P.dma("sp", vr[:, s, :, :], d["xg_val_out"][0:1024, cs].rearrange("(i p) f -> p i f", p=128), L(4), reads=[cx.xgo_b], writes=[ib_[4]])
            P.dma("pool", kTc[:, s, :], d["cfkT"][h], L(5), writes=[ib_[5]])
            P.dma("pool", vc[:, s, :, :], d["cfv"][:, cs].rearrange("(i p) f -> p i f", p=128), L(6), writes=[ib_[6]])

        def chain(h, ti, sl):
            load_head(h)
            if h + 1 < 8:
                load_head(h + 1)
            s = h % NS
            ib_ = in_b[s]
            t0, n = TILES[ti]
            if ti < 2:
                chunks = [("rem", i) for i in range(8)] + [("loc", i) for i in range(4 * ti + 4)]
            else:
                chunks = [("cache", i) for i in range(8)] + [("sloc", 8)]
            den, den_b = cx.ps[4 * sl]
            oT, oT_b = cx.ps[4 * sl + 1]
            spool = spools[sl]
            nch = len(chunks)

            def info(ci):
                kind, i = chunks[ci]
                m, c0 = 128, 0
                if kind == "rem":
                    kap, kb_, vap, vb_ = kTr[:, s, i * 128:(i + 1) * 128], ib_[3], vr[:, s, i, :], ib_[4]
                elif kind == "loc":
                    kap, kb_, vap, vb_ = kT[:, s, i * 128:(i + 1) * 128], ib_[1], vl[:, s, i, :], ib_[2]
                    c0 = max(0, (i - 4 * ti) * 128)
                elif kind == "cache":
                    kap, kb_, vap, vb_ = kTc[:, s, i * 128:(i + 1) * 128], ib_[5], vc[:, s, i, :], ib_[6]
                else:
                    m = 16
                    kap, kb_, vap, vb_ = kT[:, s, 1024:1040], ib_[1], vl[0:16, s, 8, :], ib_[2]
                return kind, i, kap, kb_, vap, vb_, m, c0

            sps = {}

            def fst(ci):
                kind, i, kap, kb_, vap, vb_, m, c0 = info(ci)
                sp_, sp_b = spool.get()
                sps[ci] = (sp_, sp_b)
                P.op("pe", lambda e: e.matmul(sp_[0:m, c0:n], kap, qT[:, s, t0 + c0:t0 + n], start=True, stop=True), reads=[kb_, ib_[0]], writes=[sp_b])

            def est(ci):
                kind, i, kap, kb_, vap, vb_, m, c0 = info(ci)
                sp_, sp_b = sps.pop(ci)
                a = ci % 2
                pb = pp_b[sl][a]
                if ti < 2:
                    for sq in range(c0 // 128, 4):
                        jq = 4 * ti + sq
                        cc = slice(sq * 128, (sq + 1) * 128)
                        bias = fb_rem[:, h, jq, i:i + 1] if kind == "rem" else fb_loc[:, h, jq, i:i + 1]
                        P.op("act", lambda e: e.activation(pp[:, sl, a, cc], sp_[:, cc], AF.Exp, bias=bias), reads=[sp_b, sm_b], writes=[pb])
                        if kind == "loc" and i == jq:
                            P.op("pool", lambda e: e.tensor_tensor(pp[:, sl, a, cc], pp[:, sl, a, cc], cb("triinc"), ALU.mult), reads=[pb, cx.c_b], writes=[pb])
                else:
                    bias = fb_cache[:, h, i:i + 1] if kind == "cache" else fb_sloc[0:16, h:h + 1]
                    P.op("act", lambda e: e.activation(pp[0:m, sl, a, 0:n], sp_[0:m, 0:n], AF.Exp, bias=bias), reads=[sp_b, sm_b], writes=[pb])
                    if kind == "sloc":
                        P.op("pool", lambda e: e.tensor_tensor(pp[0:16, sl, a, 0:16], pp[0:16, sl, a, 0:16], cb("triinc", 16, 16), ALU.mult), reads=[pb, cx.c_b], writes=[pb])

            def gst(ci):
                kind, i, kap, kb_, vap, vb_, m, c0 = info(ci)
                a = ci % 2
                pb = pp_b[sl][a]
                P.op("pe", lambda e: e.matmul(den[:, c0:n], cb("ones", m, 128), pp[0:m, sl, a, c0:n], start=(ci == 0), stop=(ci == nch - 1)), reads=[pb, cx.c_b], writes=[den_b])
                P.op("pe", lambda e: e.matmul(oT[:, c0:n], vap, pp[0:m, sl, a, c0:n], start=(ci == 0), stop=(ci == nch - 1)), reads=[pb, vb_], writes=[oT_b])

            fst(0)
            yield
            for ci in range(nch):
                if ci + 1 < nch:
                    fst(ci + 1)
                est(ci)
                yield
                gst(ci)
            yield
            P.op("dve", lambda e: e.reciprocal(rd[:, sl, 0:n], den[:, 0:n]), reads=[den_b], writes=[rd_b[sl]])
            P.op("dve", lambda e: e.tensor_tensor(merged[:, h, t0:t0 + n], oT[:, 0:n], rd[:, sl, 0:n], ALU.mult), reads=[oT_b, rd_b[sl]], writes=[merged_b[h][ti]])

        facs = []
        for h in range(8):
            for ti in (1, 0, 2):
                facs.append(lambda sl, h=h, ti=ti: chain(h, ti, sl))
        run_chains(facs, K)


def tile_cols(W):
    K, N = W.shape
    kc = K // 128
    nch = N // 128
    return np.ascontiguousarray(W.reshape(kc, 128, nch, 128).transpose(2, 1, 0, 3).reshape(nch, 128, kc * 128))


IN_SPECS = {
    "xT": ([KC, 128, NT], F32), "lng": ([128, 8, KC], F32), "lnb": ([128, 8, KC], F32),
    "cf": ([128, CF_N], F32), "cb": ([128, CB_N], F32), "flag": ([128, 2], F32),
    "ev_win": ([56, 128, 2048], F32), "ev_wfa": ([128, 128], F32), "ev_wout": ([16, 128, 2048], F32),
    "fox_bf": ([128, 8], F32), "hgrn_ng": ([128, 8], F32), "cflogf": ([128, 8, 8], F32), "lb_logits": ([128, 2, 1024], F32),
    "cfkT": ([8, 128, 1024], F32), "cfv": ([1024, 1024], F32), "hstate_in": ([8, 128, 128], F32),
    "od_win": ([48, 128, 2048], F32), "od_wout": ([16, 128, 2048], F32),
    "cskT": ([16, 128, 1024], F32), "csv": ([1024, 2048], F32),
    "memT": ([KC, 128, 256], F32),
}
for _l in range(2):
    for _f in (1, 2):
        for _n in ("wg", "wu", "wd"):
            IN_SPECS[f"{_n}{_l}{_f}"] = ([NG, 128, 2048], F32)
    IN_SPECS[f"xwq{_l}"] = ([16, 128, 2048], F32)
    IN_SPECS[f"xwkv{_l}"] = ([32, 128, 2048], F32)
    IN_SPECS[f"xwo{_l}"] = ([16, 128, 2048], F32)
    IN_SPECS[f"cmkT{_l}"] = ([16, 128, 256], F32)
    IN_SPECS[f"cmv{_l}"] = ([256, 2048], F32)
OUT_SPECS = {
    "yT": ([KC, 128, NT], F32), "fox_kT": ([8, 128, NT], F32), "fox_v": ([NT, 1024], F32), "fox_logf": ([NT, 8], F32),
    "hstate_p": ([8, 128, 128], F32), "hstate_s": ([8, 128, 128], F32),
    "sb_kT": ([16, 128, NT], F32), "sb_v": ([NT, 2048], F32),
    "mem_kT0": ([16, 128, 256], F32), "mem_v0": ([256, 2048], F32), "mem_kT1": ([16, 128, 256], F32), "mem_v1": ([256, 2048], F32),
}
INT_SPECS = {
    "ka_s": ([8, 128, NT], BF16), "va_s": ([NT, 1024], BF16), "qa_s": ([8, 128, NT], BF16), "hg_s": ([8, 128, NT], BF16),
    "hraw_s": ([8, 9, 128, 3, 128], F32),
    "xg_key_in": ([1024, 1024], BF16), "xg_key_out": ([2048, 1024], BF16),
    "xg_val_in": ([1024, 1024], BF16), "xg_val_out": ([2048, 1024], BF16),
    "xg_c_in": ([128, 64], F32), "xg_c_out": ([256, 64], F32),
    "xg_s_in": ([1024, 128], F32), "xg_s_out": ([2048, 128], F32),
    "sq_s": ([16, 128, NT], BF16), "sk_s": ([16, 128, NT], BF16), "sv_s": ([NT, 2048], BF16),
    "xg2k0_in": ([1024, 1024], BF16), "xg2k0_out": ([2048, 1024], BF16),
    "xg2k1_in": ([1024, 1024], BF16), "xg2k1_out": ([2048, 1024], BF16),
    "xg2v0_in": ([1024, 1024], BF16), "xg2v0_out": ([2048, 1024], BF16),
    "xg2v1_in": ([1024, 1024], BF16), "xg2v1_out": ([2048, 1024], BF16),
}


def declare(cx, nc, ins=None, outs=None):
    cx.d = {}
    for k, (shape, dt) in IN_SPECS.items():
        if ins is None or k in ins:
            cx.d[k] = nc.dram_tensor(k, shape, dt, kind="ExternalInput").ap()
    for k, (shape, dt) in OUT_SPECS.items():
        if outs is None or k in outs:
            cx.d[k] = nc.dram_tensor(k, shape, dt, kind="ExternalOutput").ap()
    for k, (shape, dt) in INT_SPECS.items():
        t = nc.dram_tensor(k, shape, dt)
        cx.d[k + "_t"] = t
        cx.d[k] = t.ap()


def host_shared(I):
    S = {}
    S["lng"] = np.ascontiguousarray(I["ln_g"].reshape(8, KC, 128).transpose(2, 0, 1))
    S["lnb"] = np.ascontiguousarray(I["ln_b"].reshape(8, KC, 128).transpose(2, 0, 1))
    S["cf"], S["cb"] = host_consts()
    W = I["ev_w_in"][0]
    o = {"qa": 0, "ka": 1024, "va": 2048, "fa": 3072, "qb": 3080, "fb": 4104, "ib": 5128, "gb": 6152}
    cat = np.concatenate([W[:, o[k]:o[k] + 1024] for k in ("ka", "va", "qa", "qb", "fb", "ib", "gb")], axis=1)
    S["ev_win"] = tile_cols(cat)
    S["ev_wfa"] = np.ascontiguousarray(W[:, 3072:3080].reshape(KC, 128, 8).transpose(1, 0, 2).reshape(128, 128))
    S["ev_wout"] = tile_cols(I["ev_w_out"][0])
    S["fox_bf"] = np.ascontiguousarray(np.broadcast_to(I["fox_b_f"][0][None, :], (128, 8)))
    S["hgrn_ng"] = np.ascontiguousarray(I["hgrn_norm_g"][0].reshape(8, 128).T)
    S["lb_logits"] = np.ascontiguousarray(np.broadcast_to(I["hgrn_lb_logits"][None, :, :], (128, 2, 1024)))
    S["od_win"] = tile_cols(I["od_w_in"][0])
    S["od_wout"] = tile_cols(I["od_w_out"][0])
    for l in range(2):
        for f in (1, 2):
            S[f"wg{l}{f}"] = tile_cols(I[f"ffn{f}_w_gate"][l])
            S[f"wu{l}{f}"] = tile_cols(I[f"ffn{f}_w_up"][l])
            S[f"wd{l}{f}"] = np.ascontiguousarray(I[f"ffn{f}_w_down"][l].reshape(NG, 128, 2048))
        S[f"xwq{l}"] = tile_cols(I["x_w_q"][l])
        S[f"xwkv{l}"] = tile_cols(I["x_w_kv"][l])
        S[f"xwo{l}"] = tile_cols(I["x_w_o"][l])
    return S


def host_core(I, c):
    b, hf = c // 2, c % 2
    C = {}
    x = np.concatenate([I["x_prompt"][b, hf * 1024:(hf + 1) * 1024], I["x_sample"][c]], axis=0)
    C["xT"] = np.ascontiguousarray(x.T.reshape(KC, 128, NT))
    fl = np.zeros((128, 2), np.float32)
    fl[:, 0] = hf
    fl[:, 1] = (hf - 1) * BIG
    C["flag"] = fl
    C["cflogf"] = np.ascontiguousarray(I["cache_fox_logf"][0, c].reshape(8, 128, 8).transpose(1, 0, 2))
    C["cfkT"] = np.ascontiguousarray(I["cache_fox_k"][0, c].transpose(1, 2, 0))
    C["cfv"] = np.ascontiguousarray(I["cache_fox_v"][0, c].reshape(1024, 1024))
    C["hstate_in"] = np.ascontiguousarray(I["state_hgrn"][0, c])
    C["cskT"] = np.ascontiguousarray(I["cache_sb_k"][0, c].transpose(1, 2, 0))
    C["csv"] = np.ascontiguousarray(I["cache_sb_v"][0, c].reshape(1024, 2048))
    C["memT"] = np.ascontiguousarray(I["mem_prompt"][b].T.reshape(KC, 128, 256))
    for l in range(2):
        C[f"cmkT{l}"] = np.ascontiguousarray(I["cache_mem_k"][l, c].reshape(256, 2048).T.reshape(16, 128, 256))
        C[f"cmv{l}"] = np.ascontiguousarray(I["cache_mem_v"][l, c].reshape(256, 2048))
    return C


def cross_attn(cx, l):
    P = cx.P
    nc = cx.nc
    d = cx.d
    cb = cx.cbv
    with contextlib.ExitStack() as es:
        sbl = lambda name, shape, dt: es.enter_context(nc.sbuf_tensor(uid(cx, name), shape, dt))
        qx = sbl("xa_qx", [128, KC, NT], BF16)
        qx_b = [[Buf() for _ in range(3)] for _ in range(KC)]
        mkT = sbl("xa_mkT", [128, KC, 256], BF16); mkT_b = Buf()
        mv = sbl("xa_mv", [128, 2, 2048], BF16); mv_b = Buf()
        with contextlib.ExitStack() as es2:
            sb2 = lambda name, shape, dt: es2.enter_context(nc.sbuf_tensor(uid(cx, name), shape, dt))
            es2.enter_context(ring(cx, 6))
            memb = sb2("xa_memb", [128, KC, 256], BF16)
            memb_b = [[Buf()] for _ in range(KC)]
            sb_save = cx.sb
            cx.sb = lambda name, shape, dt: es2.enter_context(nc.sbuf_tensor(name, shape, dt))
            sk = Stage(cx, uid(cx, "xa_sk"), [256], F32)
            sv = Stage(cx, uid(cx, "xa_sv"), [2, 128], F32)
            cx.sb = sb_save
            lm = cx.L(0)
            P.dma("pool", memb[:], d["memT"].rearrange("k p m -> p k m"), lm, writes=[b[0] for b in memb_b])
            W = d[f"xwkv{l}"]
            srcs = []
            for c in range(16):
                srcs += [(W[c], 2048), (W[16 + c], 2048), (d[f"xwq{l}"][c], 2048)]
            ws = WStream(cx, srcs, la=4)
            for c in range(16):
                w, w_b = ws.get(3 * c)
                a, fb_, fl = sk.get()

                def cons(ti, t0, n, ps, ps_b):
                    P.op("act", lambda e: e.copy(sk.t[:, a, :], ps[:, 0:256]), reads=[ps_b], writes=[fb_])
                    P.op("dve", lambda e: e.tensor_copy(mkT[:, c, :], ps[:, 0:256]), reads=[ps_b], writes=[mkT_b])
                proj_fm(cx, w, w_b, memb, memb_b, [(0, 256)], cons)
                P.dma("sp", d[f"mem_kT{l}"][c], sk.t[:, a, :], fl, reads=[fb_])
                w, w_b = ws.get(3 * c + 1)
                a, fb_, fl = sv.get()

                def cons(ci, t0, m, ps, ps_b):
                    P.op("act", lambda e: e.copy(sv.t[:, a, ci, :], ps[:, 0:128]), reads=[ps_b], writes=[fb_])
                    P.op("dve", lambda e: e.tensor_copy(mv[:, ci, c * 128:(c + 1) * 128], ps[:, 0:128]), reads=[ps_b], writes=[mv_b])
                proj_tm(cx, w, w_b, memb, memb_b, [(0, 128, 0), (128, 128, 0)], cons)
                P.dma("sp", d[f"mem_v{l}"][:, c * 128:(c + 1) * 128].rearrange("(i p) f -> p i f", p=128), sv.t[:, a, :, :], fl, reads=[fb_])
                w, w_b = ws.get(3 * c + 2)

                def cons(ti, t0, n, ps, ps_b):
                    P.op("dve", lambda e: e.tensor_scalar(qx[:, c, t0:t0 + n], ps[:, 0:n], SC512, None, ALU.mult), reads=[ps_b], writes=[qx_b[c][ti]])
                proj_fm(cx, w, w_b, cx.xb, cx.xb_b, TILES_F, cons)
            P.barrier()
        cmk = sbl("xa_cmk", [128, 2, 4, 256], BF16)
        cmv = sbl("xa_cmv", [128, 2, 2, 512], BF16)
        cm_b = [[Buf(), Buf()] for _ in range(2)]
        pp2 = sbl("xa_pp", [128, 2, 2, 512], BF16)
        pp_b = [Buf(), Buf()]
        rd = sbl("xa_rd", [128, 512], F32); rd_b = Buf()
        pi = 0
        for h in range(4):
            s = h % 2
            P.dma("pool", cmk[:, s, :, :], d[f"cmkT{l}"][4 * h:4 * h + 4].rearrange("k p m -> p k m"), cx.L(1 + 2 * s), writes=[cm_b[s][0]])
            P.dma("pool", cmv[:, s, :, :], d[f"cmv{l}"][:, h * 512:(h + 1) * 512].rearrange("(i p) f -> p i f", p=128), cx.L(2 + 2 * s), writes=[cm_b[s][1]])
            for ti, (t0, n) in enumerate(TILES):
                a = pi % 2
                pi += 1
                for mc in range(2):
                    sp_, sp_b = cx.psA.get()
                    for dc in range(4):
                        if ti < 2:
                            kap, kb_ = mkT[:, 4 * h + dc, mc * 128:(mc + 1) * 128], mkT_b
                        else:
                            kap, kb_ = cmk[:, s, dc, mc * 128:(mc + 1) * 128], cm_b[s][0]
                        P.op("pe", lambda e: e.matmul(sp_[:, 0:n], kap, qx[:, 4 * h + dc, t0:t0 + n], start=(dc == 0), stop=(dc == 3)), reads=[kb_, qx_b[4 * h + dc][ti]], writes=[sp_b])
                    P.op("act", lambda e: e.activation(pp2[:, a, mc, 0:n], sp_[:, 0:n], AF.Exp), reads=[sp_b], writes=[pp_b[a]])
                den, den_b = cx.psB.get()
                for mc in range(2):
                    P.op("pe", lambda e: e.matmul(den[:, 0:n], cb("ones"), pp2[:, a, mc, 0:n], start=(mc == 0), stop=(mc == 1)), reads=[pp_b[a], cx.c_b], writes=[den_b])
                P.op("dve", lambda e: e.reciprocal(rd[:, 0:n], den[:, 0:n]), reads=[den_b], writes=[rd_b])
                for dc in range(4):
                    o_, o_b = cx.psB.get()
                    for mc in range(2):
                        if ti < 2:
                            vap, vb_ = mv[:, mc, (4 * h + dc) * 128:(4 * h + dc + 1) * 128], mv_b
                        else:
                            vap, vb_ = cmv[:, s, mc, dc * 128:(dc + 1) * 128], cm_b[s][1]
                        P.op("pe", lambda e: e.matmul(o_[:, 0:n], vap, pp2[:, a, mc, 0:n], start=(mc == 0), stop=(mc == 1)), reads=[vb_, pp_b[a]], writes=[o_b])
                    P.op("dve", lambda e: e.tensor_tensor(cx.xb[:, 4 * h + dc, t0:t0 + n], o_[:, 0:n], rd[:, 0:n], ALU.mult), reads=[o_b, rd_b], writes=[cx.xb_b[4 * h + dc][ti]])
        P.barrier()
    out_proj_residual(cx, d[f"xwo{l}"], cx.xb, cx.xb_b)


def odd_mixer(cx):
    P = cx.P
    nc = cx.nc
    d = cx.d
    cb = cx.cbv
    G2 = cx.groups
    with contextlib.ExitStack() as es1:
        sb_save = cx.sb
        cx.sb = lambda name, shape, dt: es1.enter_context(nc.sbuf_tensor(name, shape, dt))
        es1.enter_context(ring(cx))
        sfm = Stage(cx, "o1_sfm", [NT], F32)
        bfm = Stage(cx, "o1_bfm", [NT], BF16)
        stm = Stage(cx, "o1_stm", [9, 128], F32)
        btm = Stage(cx, "o1_btm", [9, 128], BF16)
        cx.sb = sb_save
        W = d["od_win"]
        order = [("k", h, 16 + h) for h in range(16)] + [("v", h, 32 + h) for h in range(16)] + [("q", h, h) for h in range(16)]
        ws = WStream(cx, [(W[c], 2048) for (_, _, c) in order], la=8)
        for oi, (kind, h, c) in enumerate(order):
            w, w_b = ws.get(oi)
            g, hh = h // 8, h % 8
            if kind == "k":
                a, fb_, fl = sfm.get()
                a2, bb_, bl = bfm.get()

                def cons(ti, t0, n, ps, ps_b):
                    P.op("act", lambda e: e.copy(sfm.t[:, a, t0:t0 + n], ps[:, 0:n]), reads=[ps_b], writes=[fb_])
                    P.op("dve", lambda e: e.tensor_copy(bfm.t[:, a2, t0:t0 + n], ps[:, 0:n]), reads=[ps_b], writes=[bb_])
                proj_fm(cx, w, w_b, cx.xb, cx.xb_b, TILES_F, cons)
                P.dma("sp", d["sb_kT"][h], sfm.t[:, a, :], fl, reads=[fb_])
                P.dma("sp", d["sk_s"][h], bfm.t[:, a2, :], bl, reads=[bb_])
                P.dma("sp", d[f"xg2k{g}_in"][hh * 128:(hh + 1) * 128, :], bfm.t[:, a2, 0:1024], bl, reads=[bb_])
            elif kind == "v":
                a, fb_, fl = stm.get()
                a2, bb_, bl = btm.get()

                def cons(ci, t0, m, ps, ps_b):
                    P.op("act", lambda e: e.copy(stm.t[0:m, a, ci, :], ps[0:m, 0:128]), reads=[ps_b], writes=[fb_])
                    P.op("dve", lambda e: e.tensor_copy(btm.t[0:m, a2, ci, :], ps[0:m, 0:128]), reads=[ps_b], writes=[bb_])
                proj_tm(cx, w, w_b, cx.xb, cx.xb_b, TCH, cons)
                cs = slice(h * 128, (h + 1) * 128)
                cs2 = slice(hh * 128, (hh + 1) * 128)
                P.dma("sp", d["sb_v"][0:1024, cs].rearrange("(i p) f -> p i f", p=128), stm.t[:, a, 0:8, :], fl, reads=[fb_])
                P.dma("sp", d["sb_v"][1024:1040, cs], stm.t[0:16, a, 8, :], fl, reads=[fb_])
                P.dma("sp", d["sv_s"][0:1024, cs].rearrange("(i p) f -> p i f", p=128), btm.t[:, a2, 0:8, :], bl, reads=[bb_])
                P.dma("sp", d["sv_s"][1024:1040, cs], btm.t[0:16, a2, 8, :], bl, reads=[bb_])
                P.dma("sp", d[f"xg2v{g}_in"][0:1024, cs2].rearrange("(i p) f -> p i f", p=128), btm.t[:, a2, 0:8, :], bl, reads=[bb_])
            else:
                a2, bb_, bl = bfm.get()

                def cons(ti, t0, n, ps, ps_b):
                    P.op("dve", lambda e: e.tensor_scalar(bfm.t[:, a2, t0:t0 + n], ps[:, 0:n], SC128, None, ALU.mult), reads=[ps_b], writes=[bb_])
                proj_fm(cx, w, w_b, cx.xb, cx.xb_b, TILES_F, cons)
                P.dma("sp", d["sq_s"][h], bfm.t[:, a2, :], bl, reads=[bb_])
        P.barrier()
    for g in range(2):
        P.coll("AllGather", [d[f"xg2k{g}_in_t"].ap().opt()], [d[f"xg2k{g}_out_t"].ap().opt()], G2, cx.cc_lane, writes=[cx.xg2ko_b])
        P.coll("AllGather", [d[f"xg2v{g}_in_t"].ap().opt()], [d[f"xg2v{g}_out_t"].ap().opt()], G2, cx.cc_lane, writes=[cx.xg2vo_b])
    K = 2
    with contextlib.ExitStack() as es:
        sbl = lambda name, shape, dt: es.enter_context(nc.sbuf_tensor(name, shape, dt))
        NS = 3
        qT = sbl("sb_qT", [128, NS, NT], BF16)
        kT = sbl("sb_kT_", [128, NS, NT], BF16)
        vl = sbl("sb_vl", [128, NS, 9, 128], BF16)
        kTr = sbl("sb_kTr", [128, NS, 1024], BF16)
        vr = sbl("sb_vr", [128, NS, 8, 128], BF16)
        kTc = sbl("sb_kTc", [128, NS, 1024], BF16)
        vc = sbl("sb_vc", [128, NS, 8, 128], BF16)
        in_b = [[Buf() for _ in range(7)] for _ in range(NS)]
        e1 = sbl("sb_e1", [128, K, 2, 512], F32); e1_b = [[Buf(), Buf()] for _ in range(K)]
        lp = sbl("sb_lp", [128, K, 2, 512], BF16); lp_b = [[Buf(), Buf()] for _ in range(K)]
        xx = sbl("sb_xx", [128, K, 512], F32); xx_b = [Buf() for _ in range(K)]
        ww = sbl("sb_ww", [128, K, 512], BF16); ww_b = [Buf() for _ in range(K)]
        zpools = [PsPool(cx.ps[4 * k + 2:4 * k + 4]) for k in range(K)]
        loaded = set()

        def load_head(h):
            if h in loaded:
                return
            loaded.add(h)
            s = h % NS
            g, hh = h // 8, h % 8
            ib_ = in_b[s]
            L = lambda i: cx.lanes2_[s * 7 + i]
            cs = slice(h * 128, (h + 1) * 128)
            cs2 = slice(hh * 128, (hh + 1) * 128)
            P.dma("sp", qT[:, s, :], d["sq_s"][h], L(0), writes=[ib_[0]])
            P.dma("sp", kT[:, s, :], d["sk_s"][h], L(1), writes=[ib_[1]])
            P.dma("sp", vl[:, s, 0:8, :], d["sv_s"][0:1024, cs].rearrange("(i p) f -> p i f", p=128), L(2), writes=[ib_[2]])
            P.dma("sp", vl[0:16, s, 8, :], d["sv_s"][1024:1040, cs], L(2), writes=[ib_[2]])
            P.dma("sp", kTr[:, s, :], d[f"xg2k{g}_out"][hh * 128:(hh + 1) * 128, :], L(3), reads=[cx.xg2ko_b], writes=[ib_[3]])
            P.dma("sp", vr[:, s, :, :], d[f"xg2v{g}_out"][0:1024, cs2].rearrange("(i p) f -> p i f", p=128), L(4), reads=[cx.xg2vo_b], writes=[ib_[4]])
            P.dma("pool", kTc[:, s, :], d["cskT"][h], L(5), writes=[ib_[5]])
            P.dma("pool", vc[:, s, :, :], d["csv"][:, cs].rearrange("(i p) f -> p i f", p=128), L(6), writes=[ib_[6]])

        def chain(h, ti, sl):
            load_head(h)
            if h + 1 < 16:
                load_head(h + 1)
            s = h % NS
            ib_ = in_b[s]
            t0, n = TILES[ti]
            if ti < 2:
                chunks = [("loc", i) for i in range(4 * ti + 3, -1, -1)] + [("rem", i) for i in range(7, -1, -1)]
            else:
                chunks = [("sloc", 8)] + [("cache", i) for i in range(7, -1, -1)]
            oT, oT_b = cx.ps[4 * sl]
            A, A_b = cx.ps[4 * sl + 1]
            zpool = zpools[sl]
            P.op("pe", lambda e: e.matmul(oT[:, 0:n], cb("zeros"), qT[:, s, t0:t0 + n], start=True, stop=False), reads=[cx.c_b, ib_[0]], writes=[oT_b])
            P.op("pe", lambda e: e.matmul(A[:, 0:n], cb("zeros"), qT[:, s, t0:t0 + n], start=True, stop=False), reads=[cx.c_b, ib_[0]], writes=[A_b])

            def info(ci):
                kind, i = chunks[ci]
                m, c0, diag, bias = 128, 0, False, 0.0
                if kind == "rem":
                    kap, kb_, vap, vb_ = kTr[:, s, i * 128:(i + 1) * 128], ib_[3], vr[:, s, i, :], ib_[4]
                    bias = cx.flag[:, 1:2]
                elif kind == "loc":
                    kap, kb_, vap, vb_ = kT[:, s, i * 128:(i + 1) * 128], ib_[1], vl[:, s, i, :], ib_[2]
                    c0 = max(0, (i - 4 * ti) * 128)
                    diag = i >= 4 * ti
                elif kind == "cache":
                    kap, kb_, vap, vb_ = kTc[:, s, i * 128:(i + 1) * 128], ib_[5], vc[:, s, i, :], ib_[6]
                else:
                    m = 16
                    kap, kb_, vap, vb_ = kT[:, s, 1024:1040], ib_[1], vl[0:16, s, 8, :], ib_[2]
                    diag = True
                return kap, kb_, vap, vb_, m, c0, diag, bias

            nch = len(chunks)
            zps = {}

            def f1(ci):
                kap, kb_, vap, vb_, m, c0, diag, bias = info(ci)
                zp, zp_b = zpool.get()
                zps[ci] = (zp, zp_b)
                P.op("pe", lambda e: e.matmul(zp[0:m, c0:n], kap, qT[:, s, t0 + c0:t0 + n], start=True, stop=True), reads=[kb_, ib_[0]], writes=[zp_b])

            def f2(ci):
                kap, kb_, vap, vb_, m, c0, diag, bias = info(ci)
                zp, zp_b = zps.pop(ci)
                a = ci % 2
                P.op("act", lambda e: e.activation(e1[0:m, sl, a, c0:n], zp[0:m, c0:n], AF.Exp), reads=[zp_b], writes=[e1_b[sl][a]])

            def f3(ci):
                kap, kb_, vap, vb_, m, c0, diag, bias = info(ci)
                a = ci % 2
                dm = min(m, 128)
                P.op("act", lambda e: e.activation(lp[0:m, sl, a, c0:n], e1[0:m, sl, a, c0:n], AF.Ln, bias=1.0), reads=[e1_b[sl][a]], writes=[lp_b[sl][a]])
                if diag:
                    P.op("pool", lambda e: e.tensor_tensor(lp[0:m, sl, a, c0:c0 + dm], lp[0:m, sl, a, c0:c0 + dm], cb("trilt", m, dm), ALU.mult), reads=[lp_b[sl][a], cx.c_b], writes=[lp_b[sl][a]])

            def b1(ci):
                kap, kb_, vap, vb_, m, c0, diag, bias = info(ci)
                a = ci % 2
                P.op("pe", lambda e: e.matmul(A[:, c0:n], cb("negtrige", m, 128), lp[0:m, sl, a, c0:n], start=False, stop=False), reads=[lp_b[sl][a], cx.c_b], writes=[A_b])

            def b2(ci):
                kap, kb_, vap, vb_, m, c0, diag, bias = info(ci)
                a = ci % 2
                P.op("act", lambda e: e.activation(xx[0:m, sl, c0:n], A[0:m, c0:n], AF.Exp, bias=bias), reads=[A_b, cx.c_b], writes=[xx_b[sl]])

            def b3(ci):
                kap, kb_, vap, vb_, m, c0, diag, bias = info(ci)
                a = ci % 2
                dm = min(m, 128)
                P.op("dve", lambda e: e.tensor_tensor(ww[0:m, sl, c0:n], e1[0:m, sl, a, c0:n], xx[0:m, sl, c0:n], ALU.mult), reads=[e1_b[sl][a], xx_b[sl]], writes=[ww_b[sl]])
                if diag:
                    P.op("pool", lambda e: e.tensor_tensor(ww[0:m, sl, c0:c0 + dm], ww[0:m, sl, c0:c0 + dm], cb("trilt", m, dm), ALU.mult), reads=[ww_b[sl], cx.c_b], writes=[ww_b[sl]])

            def b4(ci):
                kap, kb_, vap, vb_, m, c0, diag, bias = info(ci)
                a = ci % 2
                P.op("pe", lambda e: e.matmul(oT[:, c0:n], vap, ww[0:m, sl, c0:n], start=False, stop=(ci == nch - 1)), reads=[ww_b[sl], vb_], writes=[oT_b])
                P.op("pe", lambda e: e.matmul(A[:, c0:n], cb("negtrilt", m, 128), lp[0:m, sl, a, c0:n], start=False, stop=(ci == nch - 1)), reads=[lp_b[sl][a], ww_b[sl], cx.c_b], writes=[A_b])

            f1(0); yield
            f2(0); yield
            f3(0); yield
            for ci in range(nch):
                nx = ci + 1 < nch
                if nx:
                    f1(ci + 1)
                b1(ci); yield
                if nx:
                    f2(ci + 1)
                b2(ci); yield
                if nx:
                    f3(ci + 1)
                b3(ci); yield
                b4(ci)
            yield
            P.op("act", lambda e: e.copy(cx.xb[:, h, t0:t0 + n], oT[:, 0:n]), reads=[oT_b], writes=[cx.xb_b[h][ti]])

        facs = []
        for h in range(16):
            for ti in (1, 0, 2):
                facs.append(lambda sl, h=h, ti=ti: chain(h, ti, sl))
        run_chains(facs, K)
        P.barrier()
    out_proj_residual(cx, d["od_wout"], cx.xb, cx.xb_b)


def build_program(cx):
    nc = cx.nc
    P = cx.P
    d = cx.d
    setup(nc, cx)
    setup_consts2(cx)
    load_x(cx)
    P.barrier()
    for l in range(2):
        ffn(cx, d[f"wg{l}1"], d[f"wu{l}1"], d[f"wd{l}1"])
        layer_norm(cx, 4 * l + 0)
        if l == 0:
            even_mixer(cx)
        else:
            odd_mixer(cx)
        layer_norm(cx, 4 * l + 1)
        cross_attn(cx, l)
        layer_norm(cx, 4 * l + 2)
        ffn(cx, d[f"wg{l}2"], d[f"wu{l}2"], d[f"wd{l}2"])
        layer_norm(cx, 4 * l + 3, final=(l == 1))
    store_y(cx)
    P.barrier()


_NC_CACHE = {}


def get_nc(ncores=8):
    if ncores in _NC_CACHE:
        return _NC_CACHE[ncores]
    nc = bass.Bass("TRN2", target_bir_lowering=False)
    cx = Ctx()
    cx.nc = nc
    cx.P = Prog(nc)
    cx.groups = [[2 * i, 2 * i + 1] for i in range(ncores // 2)]
    declare(cx, nc)
    with contextlib.ExitStack() as es:
        cx.es = es
        build_program(cx)
    _NC_CACHE[ncores] = nc
    return nc


def kernel(**inputs):
    I = {k: np.asarray(v) for k, v in inputs.items()}
    ncores = 8
    nc = get_nc(ncores)
    S = host_shared(I)
    in_maps = []
    for c in range(ncores):
        C = host_core(I, c)
        C.update(S)
        in_maps.append({k: np.ascontiguousarray(C[k], dtype=np.float32) for k in IN_SPECS})
    res = run_bass_kernel_spmd(nc, in_maps, core_ids=list(range(ncores)))
    R = res.results
    f32 = np.float32
    y_p = np.zeros((4, 2048, 2048), f32); y_s = np.zeros((8, 16, 2048), f32)
    fk_p = np.zeros((1, 4, 2048, 8, 128), f32); fv_p = np.zeros((1, 4, 2048, 8, 128), f32); fl_p = np.zeros((1, 4, 2048, 8), f32)
    hs_p = np.zeros((1, 4, 8, 128, 128), f32)
    sk_p = np.zeros((1, 4, 2048, 16, 128), f32); sv_p = np.zeros((1, 4, 2048, 16, 128), f32)
    mk_p = np.zeros((2, 4, 256, 4, 512), f32); mv_p = np.zeros((2, 4, 256, 4, 512), f32)
    fk_s = np.zeros((1, 8, 16, 8, 128), f32); fv_s = np.zeros((1, 8, 16, 8, 128), f32); fl_s = np.zeros((1, 8, 16, 8), f32)
    hs_s = np.zeros((1, 8, 8, 128, 128), f32)
    sk_s = np.zeros((1, 8, 16, 16, 128), f32); sv_s = np.zeros((1, 8, 16, 16, 128), f32)
    for c in range(ncores):
        r = R[c]
        b, hf = c // 2, c % 2
        sl = slice(hf * 1024, (hf + 1) * 1024)
        y = np.asarray(r["yT"]).reshape(2048, NT).T
        y_p[b, sl] = y[:1024]; y_s[c] = y[1024:]
        k = np.asarray(r["fox_kT"]).transpose(2, 0, 1)
        fk_p[0, b, sl] = k[:1024]; fk_s[0, c] = k[1024:]
        v = np.asarray(r["fox_v"]).reshape(NT, 8, 128)
        fv_p[0, b, sl] = v[:1024]; fv_s[0, c] = v[1024:]
        lf = np.asarray(r["fox_logf"])
        fl_p[0, b, sl] = lf[:1024]; fl_s[0, c] = lf[1024:]
        if hf == 1:
            hs_p[0, b] = np.asarray(r["hstate_p"])
        hs_s[0, c] = np.asarray(r["hstate_s"])
        k = np.asarray(r["sb_kT"]).transpose(2, 0, 1)
        sk_p[0, b, sl] = k[:1024]; sk_s[0, c] = k[1024:]
        v = np.asarray(r["sb_v"]).reshape(NT, 16, 128)
        sv_p[0, b, sl] = v[:1024]; sv_s[0, c] = v[1024:]
        if hf == 0:
            for l in range(2):
                mk_p[l, b] = np.asarray(r[f"mem_kT{l}"]).reshape(2048, 256).T.reshape(256, 4, 512)
                mv_p[l, b] = np.asarray(r[f"mem_v{l}"]).reshape(256, 4, 512)
    return (y_p, y_s, fk_p, fv_p, fl_p, hs_p, sk_p, sv_p, mk_p, mv_p, fk_s, fv_s, fl_s, hs_s, sk_s, sv_s)
```

```python
from concourse.bass_utils import run_bass_kernel_spmd
import numpy as np
import concourse.bass as bass
import concourse.mybir as mybir

F32 = mybir.dt.float32
BF16 = mybir.dt.bfloat16
AF = mybir.ActivationFunctionType
ALU = mybir.AluOpType


class Buf:
    __slots__ = ("name", "w", "r", "x")

    def __init__(self, name="", x=False):
        self.name = name
        self.w = None
        self.r = []
        self.x = x


class Lane:
    __slots__ = ("sem", "cnt")

    def __init__(self, sem):
        self.sem = sem
        self.cnt = 0


class Prog:
    ROT = 30000

    def __init__(self, nc):
        self.nc = nc
        self.eng = {"pe": nc.tensor, "dve": nc.vector, "act": nc.scalar, "pool": nc.gpsimd, "sp": nc.sync}
        self.cnt = {e: 0 for e in self.eng}
        self.sem = {e: nc.alloc_semaphore(name=f"c_{e}_0") for e in self.eng}
        self.nrot = {e: 0 for e in self.eng}
        self.known = {e: {} for e in self.eng}
        self.lanes = []
        self.all_sems = list(self.sem.values())
        self.ninstr = 0

    def lane(self, name="lane"):
        s = self.nc.alloc_semaphore(name=f"{name}_{len(self.lanes)}")
        l = Lane(s)
        self.lanes.append(l)
        return l

    def _collect(self, e, reads, writes):
        need = {}

        def add(tok):
            if tok is None:
                return
            s, v = tok
            if e == "pe" and s is self.sem["pe"]:
                return
            k = id(s)
            if k not in need or need[k][1] < v:
                need[k] = (s, v)

        for b in reads:
            add(b.w)
            if b.x:
                for t in b.r:
                    if t[0] is not self.sem[e]:
                        add(t)
        for b in writes:
            add(b.w)
            for t in b.r:
                add(t)
        kn = self.known[e]
        out = []
        for k, (s, v) in need.items():
            if kn.get(k, 0) >= v:
                continue
            out.append((s, v))
            kn[k] = v
        return out

    def _waits(self, e, deps):
        for s, v in deps:
            self.eng[e].wait_ge(s, v)
            self.ninstr += 1

    def _rot(self, e):
        if self.cnt[e] >= self.ROT:
            self.nrot[e] += 1
            self.sem[e] = self.nc.alloc_semaphore(name=f"c_{e}_{self.nrot[e]}")
            self.cnt[e] = 0

    def op(self, e, fn, reads=(), writes=()):
        self._rot(e)
        deps = self._collect(e, reads, writes)
        self._waits(e, deps)
        ins = fn(self.eng[e])
        self.cnt[e] += 1
        tok = (self.sem[e], self.cnt[e])
        ins.then_inc(tok[0], 1)
        self.ninstr += 1
        for b in reads:
            b.r.append(tok)
        for b in writes:
            b.w = tok
            b.r = []
        return tok

    def dma(self, q, out, in_, lane, reads=(), writes=(), **kw):
        deps = self._collect(q, reads, writes)
        self._waits(q, deps)
        ins = self.eng[q].dma_start(out=out, in_=in_, **kw)
        lane.cnt += 16
        ins.then_inc(lane.sem, 16)
        self.ninstr += 1
        tok = (lane.sem, lane.cnt)
        for b in reads:
            b.r.append(tok)
        for b in writes:
            b.w = tok
            b.r = []
        return tok

    def coll(self, kind, ins, outs, groups, lane, reads=(), writes=()):
        if getattr(self, "nocoll", False):
            return None
        q = "pool"
        deps = self._collect(q, reads, writes)
        self._waits(q, deps)
        i = self.eng[q].collective_compute(kind, ALU.bypass, replica_groups=groups, ins=ins, outs=outs)
        lane.cnt += 1
        i.then_inc(lane.sem, 1)
        tok = (lane.sem, lane.cnt)
        for b in reads:
            b.r.append(tok)
        for b in writes:
            b.w = tok
            b.r = []
        return tok

    def barrier(self):
        toks = [(self.sem[e], self.cnt[e]) for e in self.eng if self.cnt[e] > 0]
        toks += [(l.sem, l.cnt) for l in self.lanes if l.cnt > 0]
        for e in self.eng:
            kn = self.known[e]
            for s, v in toks:
                if e == "pe" and s is self.sem["pe"]:
                    continue
                if kn.get(id(s), 0) >= v:
                    continue
                self.eng[e].wait_ge(s, v)
                kn[id(s)] = v
                self.ninstr += 1


class PsPool:
    def __init__(self, tiles):
        self.tiles = tiles
        self.i = 0

    def get(self):
        t = self.tiles[self.i % len(self.tiles)]
        self.i += 1
        return t


class Rot:
    def __init__(self, tiles):
        self.tiles = tiles
        self.i = 0

    def get(self):
        t = self.tiles[self.i % len(self.tiles)]
        self.i += 1
        return t


import contextlib

D = 2048
KC = 16
T = 1024
TS = 16
NT = T + TS
TILES = [(0, 512), (512, 512), (1024, 16)]
TILES_F = [(0, 352), (352, 344), (696, 344)]
DFF = 5504
NG = 43
ALPHA = 4.0 ** 0.25
LN_EPS = 1e-5
NB_W = 10


class Ctx:
    pass


def setup(nc, cx):
    P = cx.P
    es = cx.es
    sb = lambda name, shape, dt: es.enter_context(nc.sbuf_tensor(name, shape, dt))
    cx.sb = sb
    cx.xf = sb("s_xf", [128, KC, NT], F32)
    cx.xf_b = [[Buf(f"xf{k}_{t}") for t in range(3)] for k in range(KC)]
    cx.xb = sb("s_xb", [128, KC, NT], BF16)
    cx.xb_b = [[Buf(f"xb{k}_{t}") for t in range(3)] for k in range(KC)]
    cx.wring = None
    cx.uid_ = 0
    cx.lanes_ = [P.lane("g") for _ in range(20)]
    cx.lanes2_ = [P.lane("h") for _ in range(21)]
    cx.L = lambda i: cx.lanes_[i]
    cx.cc_lane = P.lane("cc")
    for nm in ["xgo_b", "xgco_b", "xgso_b", "xg2ko_b", "xg2vo_b"]:
        setattr(cx, nm, Buf(nm))
    cx.w_b = [Buf(f"w{i}") for i in range(NB_W)]
    cx.w_lane = [P.lane("wl") for i in range(NB_W)]
    cx.w_i = 0
    cx.ps = []
    for i in range(8):
        t = es.enter_context(nc.psum_tensor(f"ps{i}", [128, 512], F32))
        cx.ps.append((t, Buf(f"ps{i}", x=True)))
    cx.psA = PsPool(cx.ps[0:4])
    cx.psB = PsPool(cx.ps[4:8])
    cx.c_b = Buf("consts")
    cx.inv2048 = sb("inv2048", [128, 128], BF16)
    cx.ones_bf = sb("ones_bf", [128, 128], BF16)
    cx.ones_f = sb("ones_f", [128, 128], F32)
    cx.lng = sb("s_lng", [128, 8, KC], F32)
    cx.lnb = sb("s_lnb", [128, 8, KC], F32)
    cx.lnga = sb("s_lnga", [128, 8, KC], F32)
    cx.lnba = sb("s_lnba", [128, 8, KC], F32)
    P.op("pool", lambda e: e.memset(cx.inv2048[:], 1.0 / 2048.0), writes=[cx.c_b])
    P.op("pool", lambda e: e.memset(cx.ones_bf[:], 1.0), writes=[cx.c_b])
    P.op("pool", lambda e: e.memset(cx.ones_f[:], 1.0), writes=[cx.c_b])
    ll = P.lane("ln")
    cx.misc_lane = ll
    P.dma("sp", cx.lng[:], cx.d["lng"], ll, writes=[cx.c_b])
    P.dma("sp", cx.lnb[:], cx.d["lnb"], ll, writes=[cx.c_b])
    P.op("dve", lambda e: e.tensor_scalar(cx.lnga[:], cx.lng[:], ALPHA, None, ALU.mult), reads=[cx.c_b], writes=[cx.c_b])
    P.op("dve", lambda e: e.tensor_scalar(cx.lnba[:], cx.lnb[:], ALPHA, None, ALU.mult), reads=[cx.c_b], writes=[cx.c_b])


def load_w(cx, src_ap, ncols=2048):
    P = cx.P
    i = cx.w_i % cx.nslots
    cx.w_i += 1
    P.dma("pool", cx.wring[:, i, 0:ncols], src_ap, cx.w_lane[i], writes=[cx.w_b[i]])
    return cx.wring[:, i, :], cx.w_b[i]


def uid(cx, name):
    cx.uid_ += 1
    return f"{name}_{cx.uid_}"


@contextlib.contextmanager
def ring(cx, nslots=NB_W):
    with cx.nc.sbuf_tensor(uid(cx, "wring"), [128, nslots, 2048], BF16) as t:
        cx.wring = t
        cx.nslots = nslots
        cx.w_i = 0
        yield
    cx.wring = None


class WStream:
    def __init__(self, cx, srcs, la=6):
        self.cx = cx
        self.srcs = srcs
        self.slots = {}
        self.issued = 0
        self.la = la

    def get(self, k):
        while self.issued < len(self.srcs) and self.issued <= k + self.la:
            ap, ncols = self.srcs[self.issued]
            self.slots[self.issued] = load_w(self.cx, ap, ncols)
            self.issued += 1
        return self.slots.pop(k)


def load_x(cx):
    P = cx.P
    l = P.lane("xin")
    for kc in range(KC):
        P.dma("sp", cx.xf[:, kc, :], cx.d["xT"][kc], l, writes=cx.xf_b[kc])
    for kc in range(KC):
        for b in cx.xf_b[kc]:
            b.w = (l.sem, l.cnt)
    for kc in range(KC):
        for ti, (t0, n) in enumerate(TILES_F):
            P.op("act", lambda e, kc=kc, t0=t0, n=n: e.copy(cx.xb[:, kc, t0:t0 + n], cx.xf[:, kc, t0:t0 + n]),
                 reads=[cx.xf_b[kc][ti]], writes=[cx.xb_b[kc][ti]])
            P.op("dve", lambda e, kc=kc, t0=t0, n=n: e.tensor_scalar(cx.xf[:, kc, t0:t0 + n], cx.xf[:, kc, t0:t0 + n], ALPHA, None, ALU.mult),
                 reads=[cx.xf_b[kc][ti]], writes=[cx.xf_b[kc][ti]])


def ffn(cx, wg_d, wu_d, wd_d):
    P = cx.P
    nc = cx.nc
    with ring(cx), nc.sbuf_tensor(uid(cx, "ffn_h"), [128, 4, NT], BF16) as h, nc.sbuf_tensor(uid(cx, "ffn_s"), [128, 2, 512], F32) as stmp:
        h_b = [[Buf(f"h{i}_{t}") for t in range(3)] for i in range(4)]
        s_b = [Buf("s0"), Buf("s1")]
        s_i = 0
        srcs = []
        for g in range(NG):
            srcs += [(wg_d[g], 2048), (wu_d[g], 2048), (wd_d[g], 2048)]
        ws = WStream(cx, srcs, la=6)
        G = 2
        sgs = [list(range(a, min(a + G, NG))) for a in range(0, NG, G)]
        for si, sg in enumerate(sgs):
            wds = []
            for gi, g in enumerate(sg):
                hi = (si % 2) * 2 + gi
                wg, wg_b = ws.get(3 * g)
                wu, wu_b = ws.get(3 * g + 1)
                wd, wd_b = ws.get(3 * g + 2)
                wds.append((wd, wd_b, hi))
                for ti, (t0, n) in enumerate(TILES_F):
                    gp, gp_b = cx.psA.get()
                    up, up_b = cx.psA.get()
                    for kc in range(KC):
                        P.op("pe", lambda e, kc=kc, gp=gp, wg=wg, t0=t0, n=n: e.matmul(gp[:, 0:n], wg[:, kc * 128:(kc + 1) * 128], cx.xb[:, kc, t0:t0 + n], start=(kc == 0), stop=(kc == KC - 1)),
                             reads=[wg_b, cx.xb_b[kc][ti]], writes=[gp_b])
                    for kc in range(KC):
                        P.op("pe", lambda e, kc=kc, up=up, wu=wu, t0=t0, n=n: e.matmul(up[:, 0:n], wu[:, kc * 128:(kc + 1) * 128], cx.xb[:, kc, t0:t0 + n], start=(kc == 0), stop=(kc == KC - 1)),
                             reads=[wu_b, cx.xb_b[kc][ti]], writes=[up_b])
                    sj = s_i % 2
                    s_i += 1
                    P.op("act", lambda e, sj=sj, gp=gp, n=n: e.activation(stmp[:, sj, 0:n], gp[:, 0:n], AF.Silu), reads=[gp_b], writes=[s_b[sj]])
                    P.op("dve", lambda e, sj=sj, up=up, hi=hi, t0=t0, n=n: e.tensor_tensor(h[:, hi, t0:t0 + n], stmp[:, sj, 0:n], up[:, 0:n], ALU.mult),
                         reads=[s_b[sj], up_b], writes=[h_b[hi][ti]])
            for oc in range(KC):
                for ti, (t0, n) in enumerate(TILES_F):
                    dp, dp_b = cx.psB.get()
                    for j, (wd, wd_b, hi) in enumerate(wds):
                        P.op("pe", lambda e, dp=dp, wd=wd, hi=hi, oc=oc, t0=t0, n=n, j=j: e.matmul(dp[:, 0:n], wd[:, oc * 128:(oc + 1) * 128], h[:, hi, t0:t0 + n], start=(j == 0), stop=(j == len(wds) - 1)),
                             reads=[wd_b, h_b[hi][ti]], writes=[dp_b])
                    P.op("dve", lambda e, dp=dp, oc=oc, t0=t0, n=n: e.scalar_tensor_tensor(cx.xf[:, oc, t0:t0 + n], dp[:, 0:n], 0.5, cx.xf[:, oc, t0:t0 + n], ALU.mult, ALU.add),
                         reads=[dp_b, cx.xf_b[oc][ti]], writes=[cx.xf_b[oc][ti]])
        P.barrier()


def layer_norm(cx, li, final=False):
    P = cx.P
    nc = cx.nc
    with nc.sbuf_tensor(uid(cx, "ln_rb"), [128, 2, KC, 512], BF16) as rb, nc.sbuf_tensor(uid(cx, "ln_rsq"), [128, 2, KC, 512], BF16) as rsq, \
            nc.sbuf_tensor(uid(cx, "ln_st"), [128, 3, 4, 512], F32) as st, nc.sbuf_tensor(uid(cx, "ln_t"), [128, 4, 512], F32) as tt:
        rb_b = [[Buf() for _ in range(KC)] for _ in range(2)]
        rsq_b = [[Buf() for _ in range(KC)] for _ in range(2)]
        st_b = [[Buf() for _ in range(4)] for _ in range(3)]
        t_b = [Buf() for _ in range(4)]
        tc = [0]

        def stats(ti):
            t0, n = TILES_F[ti]
            u = ti % 2
            for kc in range(KC):
                P.op("act", lambda e, kc=kc: e.copy(rb[:, u, kc, 0:n], cx.xf[:, kc, t0:t0 + n]), reads=[cx.xf_b[kc][ti]], writes=[rb_b[u][kc]])
                P.op("dve" if kc % 3 else "pool", lambda e, kc=kc: e.tensor_tensor(rsq[:, u, kc, 0:n], cx.xf[:, kc, t0:t0 + n], cx.xf[:, kc, t0:t0 + n], ALU.mult), reads=[cx.xf_b[kc][ti]], writes=[rsq_b[u][kc]])
            mp, mp_b = cx.psA.get()
            ep, ep_b = cx.psA.get()
            for kc in range(KC):
                P.op("pe", lambda e, kc=kc: e.matmul(mp[:, 0:n], cx.inv2048[:], rb[:, u, kc, 0:n], start=(kc == 0), stop=(kc == KC - 1)), reads=[rb_b[u][kc], cx.c_b], writes=[mp_b])
            for kc in range(KC):
                P.op("pe", lambda e, kc=kc: e.matmul(ep[:, 0:n], cx.inv2048[:], rsq[:, u, kc, 0:n], start=(kc == 0), stop=(kc == KC - 1)), reads=[rsq_b[u][kc], cx.c_b], writes=[ep_b])
            sb_ = st_b[ti]
            P.op("act", lambda e: e.copy(st[:, ti, 0, 0:n], mp[:, 0:n]), reads=[mp_b], writes=[sb_[0]])
            P.op("dve", lambda e: e.tensor_tensor(st[:, ti, 1, 0:n], st[:, ti, 0, 0:n], st[:, ti, 0, 0:n], ALU.mult), reads=[sb_[0]], writes=[sb_[1]])
            P.op("dve", lambda e: e.tensor_tensor(st[:, ti, 1, 0:n], ep[:, 0:n], st[:, ti, 1, 0:n], ALU.subtract), reads=[ep_b, sb_[1]], writes=[sb_[1]])
            P.op("dve", lambda e: e.tensor_scalar(st[:, ti, 1, 0:n], st[:, ti, 1, 0:n], LN_EPS, None, ALU.add), reads=[sb_[1]], writes=[sb_[1]])
            P.op("act", lambda e: e.activation(st[:, ti, 2, 0:n], st[:, ti, 1, 0:n], AF.Sqrt), reads=[sb_[1]], writes=[sb_[2]])
            P.op("dve", lambda e: e.reciprocal(st[:, ti, 2, 0:n], st[:, ti, 2, 0:n]), reads=[sb_[2]], writes=[sb_[2]])
            P.op("dve", lambda e: e.scalar_tensor_tensor(st[:, ti, 3, 0:n], st[:, ti, 0, 0:n], -1.0, st[:, ti, 2, 0:n], ALU.mult, ALU.mult), reads=[sb_[0], sb_[2]], writes=[sb_[3]])

        def norm(ti):
            t0, n = TILES_F[ti]
            sb_ = st_b[ti]
            for kc in range(KC):
                a = tc[0] % 4
                tc[0] += 1
                P.op("dve", lambda e, kc=kc, a=a: e.tensor_tensor(tt[:, a, 0:n], cx.xf[:, kc, t0:t0 + n], st[:, ti, 2, 0:n], ALU.mult), reads=[cx.xf_b[kc][ti], sb_[2]], writes=[t_b[a]])
                P.op("dve" if kc % 3 else "pool", lambda e, a=a: e.tensor_tensor(tt[:, a, 0:n], tt[:, a, 0:n], st[:, ti, 3, 0:n], ALU.add), reads=[t_b[a], sb_[3]], writes=[t_b[a]])
                P.op("act", lambda e, kc=kc, a=a: e.activation(cx.xb[:, kc, t0:t0 + n], tt[:, a, 0:n], AF.Identity, bias=cx.lnb[:, li, kc:kc + 1], scale=cx.lng[:, li, kc:kc + 1]),
                     reads=[t_b[a], cx.c_b], writes=[cx.xb_b[kc][ti]])
                if final:
                    P.op("act", lambda e, kc=kc, a=a: e.activation(cx.xf[:, kc, t0:t0 + n], tt[:, a, 0:n], AF.Identity, bias=cx.lnb[:, li, kc:kc + 1], scale=cx.lng[:, li, kc:kc + 1]),
                         reads=[t_b[a], cx.c_b], writes=[cx.xf_b[kc][ti]])
                else:
                    P.op("act", lambda e, kc=kc, a=a: e.activation(cx.xf[:, kc, t0:t0 + n], tt[:, a, 0:n], AF.Identity, bias=cx.lnba[:, li, kc:kc + 1], scale=cx.lnga[:, li, kc:kc + 1]),
                         reads=[t_b[a], cx.c_b], writes=[cx.xf_b[kc][ti]])

        stats(0)
        stats(1)
        stats(2)
        norm(0)
        norm(1)
        norm(2)
        P.barrier()


def store_y(cx, scale=None):
    P = cx.P
    l = P.lane("yout")
    for kc in range(KC):
        P.dma("sp", cx.d["yT"][kc], cx.xf[:, kc, :], l, reads=cx.xf_b[kc])


def run_chains(factories, K):
    free = list(range(K))
    active = []
    it = iter(factories)
    done = False
    while True:
        while free and not done:
            f = next(it, None)
            if f is None:
                done = True
                break
            sl = free.pop(0)
            active.append((f(sl), sl))
        if not active:
            break
        for g, sl in list(active):
            try:
                next(g)
            except StopIteration:
                active.remove((g, sl))
                free.append(sl)


def proj_fm(cx, w, w_b, src, src_b, tiles, consumer, pool=None):
    P = cx.P
    pool = pool or cx.psA
    for ti, (t0, n) in enumerate(tiles):
        ps, ps_b = pool.get()
        for kc in range(KC):
            P.op("pe", lambda e, kc=kc, ps=ps, t0=t0, n=n: e.matmul(ps[:, 0:n], w[:, kc * 128:(kc + 1) * 128], src[:, kc, t0:t0 + n], start=(kc == 0), stop=(kc == KC - 1)),
                 reads=[w_b, src_b[kc][ti]], writes=[ps_b])
        consumer(ti, t0, n, ps, ps_b)


TCH = [(i * 128, 128, i // 4) for i in range(8)] + [(1024, 16, 2)]


def proj_tm(cx, w, w_b, src, src_b, tch, consumer, ncols=128, pool=None):
    P = cx.P
    pool = pool or cx.psA
    for ci, (t0, m, ti) in enumerate(tch):
        ps, ps_b = pool.get()
        for kc in range(KC):
            P.op("pe", lambda e, kc=kc, ps=ps, t0=t0, m=m: e.matmul(ps[0:m, 0:ncols], src[:, kc, t0:t0 + m], w[:, kc * ncols:(kc + 1) * ncols], start=(kc == 0), stop=(kc == KC - 1)),
                 reads=[w_b, src_b[kc][ti]], writes=[ps_b])
        consumer(ci, t0, m, ps, ps_b)


def out_proj_residual(cx, w_d, src, src_b):
    P = cx.P
    with ring(cx):
        ws = WStream(cx, [(w_d[c], 2048) for c in range(KC)], la=8)
        for oc in range(KC):
            w, w_b = ws.get(oc)

            def cons(ti, t0, n, ps, ps_b, oc=oc):
                P.op("dve", lambda e: e.tensor_tensor(cx.xf[:, oc, t0:t0 + n], ps[:, 0:n], cx.xf[:, oc, t0:t0 + n], ALU.add),
                     reads=[ps_b, cx.xf_b[oc][ti]], writes=[cx.xf_b[oc][ti]])
            proj_fm(cx, w, w_b, src, src_b, TILES_F, cons)
        P.barrier()


CF = {"ones": 0, "triinc": 128, "trigt": 256, "sel127": 384, "sel15": 512}
CF_N = 640
CB = {"ident": 0, "ones": 128, "inv2048": 256, "inv128": 384, "triinc": 512, "negtrige": 640, "trilt": 768, "zeros": 896, "negtrilt": 1024}
CB_N = 1152
BIG = 30000.0
SC128 = 128.0 ** -0.5
SC512 = 512.0 ** -0.5
RMS_EPS = 1e-6


def host_consts():
    import numpy as _np
    f = _np.zeros((128, CF_N), _np.float32)
    r = _np.arange(128)
    f[:, 0:128] = 1.0
    f[:, 128:256] = (r[:, None] <= r[None, :])
    f[:, 256:384] = (r[:, None] > r[None, :])
    f[127, 384:512] = 1.0
    f[15, 512:640] = 1.0
    b = _np.zeros((128, CB_N), _np.float32)
    b[:, 0:128] = _np.eye(128)
    b[:, 128:256] = 1.0
    b[:, 256:384] = 1.0 / 2048.0
    b[:, 384:512] = 1.0 / 128.0
    b[:, 512:640] = (r[:, None] <= r[None, :])
    b[:, 640:768] = -1.0 * (r[:, None] >= r[None, :])
    b[:, 768:896] = (r[:, None] < r[None, :])
    b[:, 1024:1152] = -1.0 * (r[:, None] < r[None, :])
    return f, b


def setup_consts2(cx):
    P = cx.P
    sb = cx.sb
    cx.cf = sb("s_cf", [128, CF_N], F32)
    cx.cb = sb("s_cb", [128, CB_N], BF16)
    cx.flag = sb("s_flag", [128, 2], F32)
    P.dma("sp", cx.cf[:], cx.d["cf"], cx.misc_lane, writes=[cx.c_b])
    P.dma("pool", cx.cb[:], cx.d["cb"], cx.misc_lane, writes=[cx.c_b])
    P.dma("sp", cx.flag[:], cx.d["flag"], cx.misc_lane, writes=[cx.c_b])
    cx.cfv = lambda k, m=128, n=128: cx.cf[0:m, CF[k]:CF[k] + n]
    cx.cbv = lambda k, m=128, n=128: cx.cb[0:m, CB[k]:CB[k] + n]


class Stage:
    def __init__(self, cx, name, shape, dt, n=2):
        self.t = cx.sb(name, [128, n] + shape, dt)
        self.n = n
        self.b = [Buf(f"{name}{i}") for i in range(n)]
        self.l = [cx.P.lane(name) for i in range(n)]
        self.i = 0

    def get(self):
        a = self.i % self.n
        self.i += 1
        return a, self.b[a], self.l[a]


def even_mixer(cx, j=0):
    P = cx.P
    nc = cx.nc
    d = cx.d
    esm = contextlib.ExitStack()
    sb = lambda name, shape, dt: esm.enter_context(nc.sbuf_tensor(name, shape, dt))
    cb = cx.cbv
    cf = cx.cfv
    G2 = cx.groups
    merged = cx.xb
    merged_b = cx.xb_b
    lf = sb("ev_lf", [128, 9, 8], F32)
    lf_b = Buf("lf")
    sm = sb("ev_small", [128, 1024], F32)
    sm_b = Buf("sm")
    cum_loc = sm[:, 0:72].rearrange("p (i h) -> p i h", h=8)
    cum_cache = sm[:, 72:136].rearrange("p (i h) -> p i h", h=8)
    ck_rem = sm[:, 136:200].rearrange("p (i h) -> p i h", h=8)
    AT = sm[:, 200:208]
    G_loc = sm[:, 232:296].rearrange("p (i h) -> p i h", h=8)
    cfl = sm[:, 296:360].rearrange("p (i h) -> p i h", h=8)
    bfb = sm[:, 688:696]
    ng = sm[:, 696:704]
    lbt = sb("ev_lb", [128, 2, 1024], F32)
    lb_b = Buf("lb")

    P.op("dve", lambda e: e.memset(lf[:], 0.0), writes=[lf_b])
    P.op("dve", lambda e: e.memset(sm[:], 0.0), writes=[sm_b])
    P.dma("sp", bfb, d["fox_bf"], cx.L(16), writes=[sm_b])
    P.dma("sp", ng, d["hgrn_ng"], cx.L(16), writes=[sm_b])
    P.dma("sp", cfl, d["cflogf"], cx.L(16), writes=[sm_b])
    P.dma("sp", lbt[:, 0:2, :], d["lb_logits"], cx.L(17), writes=[lb_b])
    P.op("dve", lambda e: e.tensor_tensor(lbt[:, 0, :], lbt[:, 0, :], lbt[:, 1, :], ALU.subtract), reads=[lb_b], writes=[lb_b])
    P.op("act", lambda e: e.activation(lbt[:, 0, :], lbt[:, 0, :], AF.Sigmoid), reads=[lb_b], writes=[lb_b])
    P.op("dve", lambda e: e.tensor_scalar(lbt[:, 1, :], lbt[:, 0, :], -1.0, 1.0, ALU.mult, ALU.add), reads=[lb_b], writes=[lb_b])
    lb_bc = lambda h: lbt[:, 0, h * 128:(h + 1) * 128]
    oml_bc = lambda h: lbt[:, 1, h * 128:(h + 1) * 128]

    es1 = contextlib.ExitStack()
    es1.enter_context(ring(cx))
    sb_save = cx.sb
    cx.sb = lambda name, shape, dt: es1.enter_context(nc.sbuf_tensor(name, shape, dt))
    sfm = Stage(cx, "e1_sfm", [NT], F32)
    bfm = Stage(cx, "e1_bfm", [NT], BF16)
    stm = Stage(cx, "e1_stm", [9, 128], F32)
    btm = Stage(cx, "e1_btm", [9, 128], BF16)
    P.op("pool", lambda e: e.memset(stm.t[:], 0.0), writes=stm.b)
    P.op("pool", lambda e: e.memset(btm.t[:], 0.0), writes=btm.b)
    W = d["ev_win"]
    order = []
    for h in range(8):
        order.append(("ka", h, h))
    for h in range(8):
        order.append(("va", h, 8 + h))
    order.append(("fa", 0, 56))
    for h in range(8):
        order.append(("fb", h, 32 + h))
    for h in range(8):
        order.append(("ib", h, 40 + h))
    for h in range(8):
        order.append(("qb", h, 24 + h))
    for h in range(8):
        order.append(("gb", h, 48 + h))
    for h in range(8):
        order.append(("qa", h, 16 + h))
    srcs = [((d["ev_wfa"], 128) if k == "fa" else (W[c], 2048)) for (k, h, c) in order]
    ws = WStream(cx, srcs, la=8)
    TY = {"qb": 0, "fb": 1, "ib": 2}
    for oi, (kind, h, c) in enumerate(order):
        w, w_b = ws.get(oi)
        if kind == "ka":
            a, fb_, fl = sfm.get()
            a2, bb_, bl = bfm.get()

            def cons(ti, t0, n, ps, ps_b):
                P.op("act", lambda e: e.copy(sfm.t[:, a, t0:t0 + n], ps[:, 0:n]), reads=[ps_b], writes=[fb_])
                P.op("dve", lambda e: e.tensor_copy(bfm.t[:, a2, t0:t0 + n], ps[:, 0:n]), reads=[ps_b], writes=[bb_])
            proj_fm(cx, w, w_b, cx.xb, cx.xb_b, TILES_F, cons)
            P.dma("sp", d["fox_kT"][h], sfm.t[:, a, :], fl, reads=[fb_])
            P.dma("sp", d["ka_s"][h], bfm.t[:, a2, :], bl, reads=[bb_])
            P.dma("sp", d["xg_key_in"][h * 128:(h + 1) * 128, :], bfm.t[:, a2, 0:1024], bl, reads=[bb_])
        elif kind == "va":
            a, fb_, fl = stm.get()
            a2, bb_, bl = btm.get()

            def cons(ci, t0, m, ps, ps_b):
                P.op("act", lambda e: e.copy(stm.t[0:m, a, ci, :], ps[0:m, 0:128]), reads=[ps_b], writes=[fb_])
                P.op("dve", lambda e: e.tensor_copy(btm.t[0:m, a2, ci, :], ps[0:m, 0:128]), reads=[ps_b], writes=[bb_])
            proj_tm(cx, w, w_b, cx.xb, cx.xb_b, TCH, cons)
            cs = slice(h * 128, (h + 1) * 128)
            P.dma("sp", d["fox_v"][0:1024, cs].rearrange("(i p) f -> p i f", p=128), stm.t[:, a, 0:8, :], fl, reads=[fb_])
            P.dma("sp", d["fox_v"][1024:1040, cs], stm.t[0:16, a, 8, :], fl, reads=[fb_])
            P.dma("sp", d["va_s"][0:1024, cs].rearrange("(i p) f -> p i f", p=128), btm.t[:, a2, 0:8, :], bl, reads=[bb_])
            P.dma("sp", d["va_s"][1024:1040, cs], btm.t[0:16, a2, 8, :], bl, reads=[bb_])
            P.dma("sp", d["xg_val_in"][0:1024, cs].rearrange("(i p) f -> p i f", p=128), btm.t[:, a2, 0:8, :], bl, reads=[bb_])
        elif kind == "qa":
            a2, bb_, bl = bfm.get()

            def cons(ti, t0, n, ps, ps_b):
                P.op("dve", lambda e: e.tensor_scalar(bfm.t[:, a2, t0:t0 + n], ps[:, 0:n], SC128, None, ALU.mult), reads=[ps_b], writes=[bb_])
            proj_fm(cx, w, w_b, cx.xb, cx.xb_b, TILES_F, cons)
            P.dma("sp", d["qa_s"][h], bfm.t[:, a2, :], bl, reads=[bb_])
        elif kind == "gb":
            a2, bb_, bl = bfm.get()

            def cons(ti, t0, n, ps, ps_b):
                P.op("act", lambda e: e.activation(bfm.t[:, a2, t0:t0 + n], ps[:, 0:n], AF.Silu), reads=[ps_b], writes=[bb_])
            proj_fm(cx, w, w_b, cx.xb, cx.xb_b, TILES_F, cons)
            P.dma("sp", d["hg_s"][h], bfm.t[:, a2, :], bl, reads=[bb_])
        elif kind in TY:
            a, fb_, fl = stm.get()

            def cons(ci, t0, m, ps, ps_b):
                P.op("act", lambda e: e.copy(stm.t[0:m, a, ci, :], ps[0:m, 0:128]), reads=[ps_b], writes=[fb_])
            proj_tm(cx, w, w_b, cx.xb, cx.xb_b, TCH, cons)
            P.dma("sp", d["hraw_s"][h, :, :, TY[kind], :].rearrange("i p f -> p i f"), stm.t[:, a, :, :], fl, reads=[fb_])
        elif kind == "fa":
            def cons(ci, t0, m, ps, ps_b):
                P.op("dve", lambda e: e.tensor_tensor(lf[0:m, ci, :], ps[0:m, 0:8], bfb[0:m, :], ALU.add), reads=[ps_b, sm_b], writes=[lf_b])
            proj_tm(cx, w, w_b, cx.xb, cx.xb_b, TCH, cons, ncols=8)
            P.op("act", lambda e: e.activation(lf[:], lf[:], AF.Exp, scale=-1.0), reads=[lf_b], writes=[lf_b])
            P.op("act", lambda e: e.activation(lf[:], lf[:], AF.Ln, bias=1.0), reads=[lf_b], writes=[lf_b])
            P.op("dve", lambda e: e.tensor_scalar(lf[:], lf[:], -1.0, None, ALU.mult), reads=[lf_b], writes=[lf_b])
            P.dma("sp", d["fox_logf"][0:1024, :].rearrange("(i p) h -> p i h", p=128), lf[:, 0:8, :], cx.L(18), reads=[lf_b])
            P.dma("sp", d["fox_logf"][1024:1040, :], lf[0:16, 8, :], cx.L(18), reads=[lf_b])
            cp, cp_b = cx.psB.get()
            for i in range(8):
                for i2 in range(i):
                    P.op("pe", lambda e, i=i, i2=i2: e.matmul(cp[:, i * 8:(i + 1) * 8], cf("ones"), lf[:, i2, :], start=(i2 == 0), stop=False), reads=[lf_b, cx.c_b], writes=[cp_b])
                P.op("pe", lambda e, i=i: e.matmul(cp[:, i * 8:(i + 1) * 8], cf("triinc"), lf[:, i, :], start=(i == 0), stop=True), reads=[lf_b, cx.c_b], writes=[cp_b])
            P.op("dve", lambda e: e.tensor_copy(sm[:, 0:64], cp[:, 0:64]), reads=[cp_b], writes=[sm_b])
            cp2, cp2_b = cx.psB.get()
            for i in range(8):
                for i2 in range(i):
                    P.op("pe", lambda e, i=i, i2=i2: e.matmul(cp2[:, i * 8:(i + 1) * 8], cf("ones"), cfl[:, i2, :], start=(i2 == 0), stop=False), reads=[sm_b, cx.c_b], writes=[cp2_b])
                P.op("pe", lambda e, i=i: e.matmul(cp2[:, i * 8:(i + 1) * 8], cf("triinc"), cfl[:, i, :], start=(i == 0), stop=True), reads=[sm_b, cx.c_b], writes=[cp2_b])
            for i2 in range(8):
                P.op("pe", lambda e, i2=i2: e.matmul(cp2[0:16, 64:72], cf("ones", 128, 16), cfl[:, i2, :], start=(i2 == 0), stop=False), reads=[sm_b, cx.c_b], writes=[cp2_b])
            P.op("pe", lambda e: e.matmul(cp2[0:16, 64:72], cf("triinc", 16, 16), lf[0:16, 8, :], start=False, stop=True), reads=[lf_b, cx.c_b], writes=[cp2_b])
            P.op("dve", lambda e: e.tensor_copy(sm[:, 72:136], cp2[:, 0:64]), reads=[cp2_b], writes=[sm_b])
            P.op("dve", lambda e: e.tensor_copy(sm[0:16, 64:72], cp2[0:16, 64:72]), reads=[cp2_b], writes=[sm_b])
            P.dma("sp", d["xg_c_in"], sm[:, 0:64], cx.L(19), reads=[sm_b])
    es1.close()
    cx.sb = sb_save
    P.barrier()
    if getattr(cx, "stop", "") == "e1":
        return
    P.coll("AllGather", [d["xg_key_in_t"].ap().opt()], [d["xg_key_out_t"].ap().opt()], G2, cx.cc_lane, writes=[cx.xgo_b])
    P.coll("AllGather", [d["xg_val_in_t"].ap().opt()], [d["xg_val_out_t"].ap().opt()], G2, cx.cc_lane, writes=[cx.xgo_b])
    P.coll("AllGather", [d["xg_c_in_t"].ap().opt()], [d["xg_c_out_t"].ap().opt()], G2, cx.cc_lane, writes=[cx.xgco_b])
    hgrn_pass(cx, True, lb_bc, oml_bc, lb_b, ng, sm_b, merged, merged_b)
    P.barrier()
    P.coll("AllGather", [d["xg_s_in_t"].ap().opt()], [d["xg_s_out_t"].ap().opt()], G2, cx.cc_lane, writes=[cx.xgso_b])
    P.dma("sp", sm[:, 136:200], d["xg_c_out"][0:128, :], cx.L(19), reads=[cx.xgco_b], writes=[sm_b])
    tp, tp_b = cx.psB.get()
    P.op("pe", lambda e: e.matmul(tp[:, 0:8], cf("sel127"), ck_rem[:, 7, :], start=True, stop=True), reads=[sm_b, cx.c_b], writes=[tp_b])
    P.op("dve", lambda e: e.tensor_scalar(AT, tp[:, 0:8], cx.flag[:, 0:1], None, ALU.mult), reads=[tp_b, cx.c_b], writes=[sm_b])
    for i in range(8):
        P.op("dve", lambda e, i=i: e.tensor_tensor(G_loc[:, i, :], cum_loc[:, i, :], AT, ALU.add), reads=[sm_b], writes=[sm_b])
    ft = sb("ev_ft", [128, 1200], F32)
    ft_b = sm_b
    fb_loc = ft[:, 0:512].rearrange("p (h j i) -> p h j i", h=8, j=8)
    fb_rem = ft[:, 512:1024].rearrange("p (h j i) -> p h j i", h=8, j=8)
    fb_cache = ft[:, 1024:1088].rearrange("p (h i) -> p h i", h=8)
    fb_sloc = ft[:, 1088:1096]
    cref = ft[:, 1100:1172].rearrange("p (j h) -> p j h", h=8)
    P.op("dve", lambda e: e.memset(ft[:], 0.0), writes=[sm_b])
    tp, tp_b = cx.psB.get()
    for jq in range(8):
        P.op("pe", lambda e, jq=jq: e.matmul(tp[:, jq * 8:(jq + 1) * 8], cf("sel127"), G_loc[:, jq, :], start=True, stop=True), reads=[sm_b, cx.c_b], writes=[tp_b])
    P.op("pe", lambda e: e.matmul(tp[:, 64:72], cf("sel15", 16, 128), cum_loc[0:16, 8, :], start=True, stop=True), reads=[sm_b, cx.c_b], writes=[tp_b])
    P.op("dve", lambda e: e.tensor_copy(ft[:, 1100:1172], tp[:, 0:72]), reads=[tp_b], writes=[sm_b])
    for h in range(8):
        for jq in range(8):
            ni = jq + 1
            tb1, tb2 = Buf(), Buf()
            P.op("dve", lambda e, h=h, jq=jq, ni=ni: e.tensor_scalar(fb_loc[:, h, jq, 0:ni], G_loc[:, 0:ni, h], cref[:, jq, h:h + 1], -1.0, ALU.subtract, ALU.mult), reads=[sm_b], writes=[tb1])
            P.op("pool", lambda e, h=h, jq=jq: e.tensor_scalar(fb_rem[:, h, jq, :], ck_rem[:, :, h], cref[:, jq, h:h + 1], -1.0, ALU.subtract, ALU.mult), reads=[sm_b], writes=[tb2])
            P.op("pool", lambda e, h=h, jq=jq: e.tensor_scalar(fb_rem[:, h, jq, :], fb_rem[:, h, jq, :], cx.flag[:, 1:2], None, ALU.add), reads=[tb2, cx.c_b], writes=[tb2])
        tb3, tb4 = Buf(), Buf()
        P.op("dve", lambda e, h=h: e.tensor_scalar(fb_cache[:, h, :], cum_cache[:, :, h], cref[:, 8, h:h + 1], -1.0, ALU.subtract, ALU.mult), reads=[sm_b], writes=[tb3])
        P.op("dve", lambda e, h=h: e.tensor_scalar(fb_sloc[0:16, h:h + 1], cum_loc[0:16, 8, h:h + 1], cref[0:16, 8, h:h + 1], -1.0, ALU.subtract, ALU.mult), reads=[sm_b], writes=[tb4])
    P.barrier()
    if getattr(cx, "stop", "") == "x1":
        return
    tabs = dict(fb_loc=fb_loc, fb_rem=fb_rem, fb_cache=fb_cache, fb_sloc=fb_sloc, sm_b=sm_b)
    if getattr(cx, "stop", "") == "h1":
        return
    fox_heads(cx, tabs, merged, merged_b)
    P.barrier()
    if getattr(cx, "stop", "") == "fox":
        return
    hgrn_pass(cx, False, lb_bc, oml_bc, lb_b, ng, sm_b, merged, merged_b)
    P.barrier()
    esm.close()
    out_proj_residual(cx, d["ev_wout"], merged, merged_b)


def hgrn_pass(cx, state_only, lb_bc, oml_bc, lb_b, ng, sm_b, merged, merged_b):
    P = cx.P
    nc = cx.nc
    d = cx.d
    cb = cx.cbv
    cf = cx.cfv
    sfx = "1" if state_only else "2"
    with contextlib.ExitStack() as es:
        sbl = lambda name, shape, dt: es.enter_context(nc.sbuf_tensor(name + sfx, shape, dt))
        raw = sbl("hg_raw", [128, 9, 3, 128], F32); raw_b = Buf()
        f_ = sbl("hg_f", [128, 9, 128], F32); f_b = Buf()
        lg = sbl("hg_lg", [128, 9, 128], F32); lg_b = Buf()
        erb = sbl("hg_erb", [128, 9, 128], F32); erb_b = Buf()
        Kh = sbl("hg_Kh", [128, 9, 128], BF16); Kh_b = Buf()
        ib = sbl("hg_ib", [128, 9, 128], BF16); ib_b = Buf()
        S = sbl("hg_S", [128, 128], F32); S_b = Buf()
        Sb = sbl("hg_Sb", [128, 128], BF16); Sb_b = Buf()
        el = sbl("hg_el", [128, 16], F32); el_b = Buf()
        if not state_only:
            eb = sbl("hg_eb", [128, 9, 128], F32); eb_b = Buf()
            enb = sbl("hg_enb", [128, 9, 128], F32); enb_b = Buf()
            qs = lg; qs_b = lg_b
            Qt = sbl("hg_Qt", [128, 9, 128], BF16); Qt_b = Buf()
            Kt = sbl("hg_Kt", [128, 9, 128], BF16); Kt_b = Buf()
            QtT = sbl("hg_QtT", [128, 9, 128], BF16); QtT_b = Buf()
            KtT = sbl("hg_KtT", [128, 9, 128], BF16); KtT_b = Buf()
            sc = sbl("hg_sc", [128, 9, 128], BF16); sc_b = Buf()
            Sball = sbl("hg_Sball", [128, 9, 128], BF16); Sball_b = Buf()
            ob = sbl("hg_ob", [128, NT], F32); ob_b = Buf()
            gt = sbl("hg_g", [128, NT], BF16); gt_b = Buf()
            sq = sbl("hg_sq", [128, 512], BF16); sq_b = Buf()
            rs = sbl("hg_rs", [128, 512], F32); rs_b = Buf()
            P.op("pool", lambda e: e.memset(sc[:], 0.0), writes=[sc_b])
        l_raw, l_S, l_sin, l_g = cx.L(14), cx.L(15), cx.L(16), cx.L(17)
        GR = [(0, 4), (4, 8), (8, 9)]
        nblk = 8 if state_only else 9
        for h in range(8):
            P.dma("sp", raw[:], d["hraw_s"][h].rearrange("i p t f -> p i t f"), l_raw, writes=[raw_b])
            if not state_only:
                P.dma("sp", gt[:], d["hg_s"][h], l_g, writes=[gt_b])
            P.op("act", lambda e: e.activation(f_[:], raw[:, :, 1, :], AF.Sigmoid), reads=[raw_b], writes=[f_b])
            for bi in range(9):
                P.op("dve", lambda e, bi=bi: e.tensor_tensor(f_[:, bi, :], f_[:, bi, :], oml_bc(h), ALU.mult), reads=[f_b, lb_b], writes=[f_b])
                P.op("dve", lambda e, bi=bi: e.tensor_tensor(f_[:, bi, :], f_[:, bi, :], lb_bc(h), ALU.add), reads=[f_b, lb_b], writes=[f_b])
            P.op("act", lambda e: e.activation(lg[:], f_[:], AF.Ln), reads=[f_b], writes=[lg_b])
            P.op("dve", lambda e: e.tensor_scalar(f_[:], f_[:], -1.0, 1.0, ALU.mult, ALU.add), reads=[f_b], writes=[f_b])
            P.op("pool", lambda e: e.tensor_copy(ib[:], raw[:, :, 2, :]), reads=[raw_b], writes=[ib_b])
            tp, tp_b = cx.psB.get()
            for (g0, g1) in GR:
                rp, rp_b = cx.psA.get()
                if not state_only:
                    bp, bp_b = cx.psA.get()
                for bi in range(g0, g1):
                    m = 128 if bi < 8 else 16
                    c0 = (bi - g0) * 128
                    P.op("pe", lambda e, bi=bi, m=m, c0=c0: e.matmul(rp[0:m, c0:c0 + 128], cf("trigt", m, m), lg[0:m, bi, :], start=True, stop=True), reads=[lg_b, cx.c_b], writes=[rp_b])
                    if not state_only:
                        P.op("pe", lambda e, bi=bi, m=m, c0=c0: e.matmul(bp[0:m, c0:c0 + 128], cf("triinc", m, m), lg[0:m, bi, :], start=True, stop=True), reads=[lg_b, cx.c_b], writes=[bp_b])
                    P.op("pe", lambda e, bi=bi, m=m: e.matmul(tp[:, bi:bi + 1], lg[0:m, bi, :], cf("ones", m, 1), start=True, stop=True), reads=[lg_b, cx.c_b], writes=[tp_b])
                ncol = (g1 - g0) * 128
                P.op("act", lambda e, g0=g0, g1=g1, ncol=ncol: e.activation(erb[:, g0:g1, :], rp[:, 0:ncol].rearrange("p (i f) -> p i f", f=128), AF.Exp), reads=[rp_b], writes=[erb_b])
                if not state_only:
                    P.op("act", lambda e, g0=g0, g1=g1, ncol=ncol: e.activation(eb[:, g0:g1, :], bp[:, 0:ncol].rearrange("p (i f) -> p i f", f=128), AF.Exp), reads=[bp_b], writes=[eb_b])
                    P.op("act", lambda e, g0=g0, g1=g1, ncol=ncol: e.activation(enb[:, g0:g1, :], bp[:, 0:ncol].rearrange("p (i f) -> p i f", f=128), AF.Exp, scale=-1.0), reads=[bp_b], writes=[enb_b])
            P.op("act", lambda e: e.activation(el[:, 0:9], tp[:, 0:9], AF.Exp), reads=[tp_b], writes=[el_b])
            if not state_only:
                P.op("act", lambda e: e.activation(qs[:], raw[:, :, 0, :], AF.Silu), reads=[raw_b], writes=[qs_b])
            P.op("pool", lambda e: e.tensor_tensor(Kh[:], f_[:], erb[:], ALU.mult), reads=[f_b, erb_b], writes=[Kh_b])
            if not state_only:
                P.op("dve", lambda e: e.tensor_tensor(Qt[:], qs[:], eb[:], ALU.mult), reads=[qs_b, eb_b], writes=[Qt_b])
                P.op("dve", lambda e: e.tensor_tensor(Kt[:], f_[:], enb[:], ALU.mult), reads=[f_b, enb_b], writes=[Kt_b])
                for (g0, g1) in GR:
                    ncol = (g1 - g0) * 128
                    qp, qp_b = cx.psA.get()
                    kp, kp_b = cx.psA.get()
                    for bi in range(g0, g1):
                        m = 128 if bi < 8 else 16
                        c0 = (bi - g0) * 128
                        P.op("pe", lambda e, bi=bi, m=m, c0=c0: e.matmul(qp[:, c0:c0 + m], Qt[0:m, bi, :], cb("ident", m, m), start=True, stop=True), reads=[Qt_b, cx.c_b], writes=[qp_b])
                        P.op("pe", lambda e, bi=bi, m=m, c0=c0: e.matmul(kp[:, c0:c0 + m], Kt[0:m, bi, :], cb("ident", m, m), start=True, stop=True), reads=[Kt_b, cx.c_b], writes=[kp_b])
                    if g0 < 8:
                        P.op("act", lambda e, g0=g0, g1=g1, ncol=ncol: e.copy(QtT[:, g0:g1, :], qp[:, 0:ncol].rearrange("p (i f) -> p i f", f=128)), reads=[qp_b], writes=[QtT_b])
                        P.op("dve", lambda e, g0=g0, g1=g1, ncol=ncol: e.tensor_copy(KtT[:, g0:g1, :], kp[:, 0:ncol].rearrange("p (i f) -> p i f", f=128)), reads=[kp_b], writes=[KtT_b])
                    else:
                        P.op("act", lambda e: e.copy(QtT[:, 8, 0:16], qp[:, 0:16]), reads=[qp_b], writes=[QtT_b])
                        P.op("dve", lambda e: e.tensor_copy(KtT[:, 8, 0:16], kp[:, 0:16]), reads=[kp_b], writes=[KtT_b])
                for (g0, g1) in GR:
                    sp_, sp_b = cx.psA.get()
                    for bi in range(g0, g1):
                        m = 128 if bi < 8 else 16
                        c0 = (bi - g0) * 128
                        P.op("pe", lambda e, bi=bi, m=m, c0=c0: e.matmul(sp_[0:m, c0:c0 + m], KtT[:, bi, 0:m], QtT[:, bi, 0:m], start=True, stop=True), reads=[KtT_b, QtT_b], writes=[sp_b])
                    for bi in range(g0, g1):
                        m = 128 if bi < 8 else 16
                        c0 = (bi - g0) * 128
                        P.op("dve", lambda e, bi=bi, m=m, c0=c0: e.tensor_tensor(sc[0:m, bi, 0:m], sp_[0:m, c0:c0 + m], cb("triinc", m, m), ALU.mult), reads=[sp_b, cx.c_b], writes=[sc_b])
            s2t = []
            for g3 in range(3):
                s2t.append(cx.psB.get())
            for bi in range(nblk):
                m = 128 if bi < 8 else 16
                s2, s2_b = s2t[bi // 4]
                c0 = (bi % 4) * 128
                P.op("pe", lambda e, bi=bi, m=m, c0=c0: e.matmul(s2[:, c0:c0 + 128], Kh[0:m, bi, :], ib[0:m, bi, :], start=True, stop=True), reads=[Kh_b, ib_b], writes=[s2_b])
            for bi in range(nblk):
                s2, s2_b = s2t[bi // 4]
                c0 = (bi % 4) * 128
                if bi == 0:
                    if state_only:
                        P.op("dve", lambda e: e.memset(S[:], 0.0), writes=[S_b])
                    else:
                        P.dma("sp", S[:], d["xg_s_out"][h * 128:(h + 1) * 128, :], l_sin, reads=[cx.xgso_b], writes=[S_b])
                        P.op("dve", lambda e: e.tensor_scalar(S[:], S[:], cx.flag[:, 0:1], None, ALU.mult), reads=[S_b, cx.c_b], writes=[S_b])
                if bi == 8:
                    P.dma("sp", S[:], d["hstate_in"][h], l_sin, writes=[S_b])
                if not state_only:
                    P.op("act", lambda e, bi=bi: e.copy(Sball[:, bi, :], S[:]), reads=[S_b], writes=[Sball_b])
                P.op("dve", lambda e, bi=bi, c0=c0: e.scalar_tensor_tensor(S[:], S[:], el[:, bi:bi + 1], s2[:, c0:c0 + 128], ALU.mult, ALU.add), reads=[S_b, el_b, s2_b], writes=[S_b])
                if bi == 7:
                    if state_only:
                        P.dma("sp", d["xg_s_in"][h * 128:(h + 1) * 128, :], S[:], l_S, reads=[S_b])
                    else:
                        P.dma("sp", d["hstate_p"][h], S[:], l_S, reads=[S_b])
                if bi == 8:
                    P.dma("sp", d["hstate_s"][h], S[:], l_S, reads=[S_b])
            if not state_only:
                for bi in range(nblk):
                    m = 128 if bi < 8 else 16
                    t0 = bi * 128
                    op_, op_b = cx.psA.get()
                    P.op("pe", lambda e, bi=bi, m=m: e.matmul(op_[:, 0:m], ib[0:m, bi, :], sc[0:m, bi, 0:m], start=True, stop=False), reads=[ib_b, sc_b], writes=[op_b])
                    P.op("pe", lambda e, bi=bi, m=m: e.matmul(op_[:, 0:m], Sball[:, bi, :], QtT[:, bi, 0:m], start=False, stop=True), reads=[Sball_b, QtT_b], writes=[op_b])
                    P.op("act", lambda e, m=m, t0=t0: e.copy(ob[:, t0:t0 + m], op_[:, 0:m]), reads=[op_b], writes=[ob_b])
            if state_only:
                continue
            for ti, (t0, n) in enumerate(TILES):
                P.op("pool", lambda e, t0=t0, n=n: e.tensor_tensor(sq[:, 0:n], ob[:, t0:t0 + n], ob[:, t0:t0 + n], ALU.mult), reads=[ob_b], writes=[sq_b])
                mp, mp_b = cx.psA.get()
                P.op("pe", lambda e, n=n: e.matmul(mp[:, 0:n], cb("inv128"), sq[:, 0:n], start=True, stop=True), reads=[sq_b, cx.c_b], writes=[mp_b])
                P.op("dve", lambda e, n=n: e.tensor_scalar(rs[:, 0:n], mp[:, 0:n], RMS_EPS, None, ALU.add), reads=[mp_b], writes=[rs_b])
                P.op("act", lambda e, n=n: e.activation(rs[:, 0:n], rs[:, 0:n], AF.Sqrt), reads=[rs_b], writes=[rs_b])
                P.op("dve", lambda e, n=n: e.reciprocal(rs[:, 0:n], rs[:, 0:n]), reads=[rs_b], writes=[rs_b])
                P.op("dve", lambda e, t0=t0, n=n: e.tensor_tensor(rs[:, 0:n], ob[:, t0:t0 + n], rs[:, 0:n], ALU.mult), reads=[ob_b, rs_b], writes=[rs_b])
                P.op("dve", lambda e, t0=t0, n=n: e.scalar_tensor_tensor(merged[:, 8 + h, t0:t0 + n], rs[:, 0:n], ng[:, h:h + 1], gt[:, t0:t0 + n], ALU.mult, ALU.mult),
                     reads=[rs_b, sm_b, gt_b], writes=[merged_b[8 + h][ti]])


def fox_heads(cx, tabs, merged, merged_b):
    P = cx.P
    nc = cx.nc
    d = cx.d
    cb = cx.cbv
    fb_loc, fb_rem, fb_cache, fb_sloc, sm_b = tabs["fb_loc"], tabs["fb_rem"], tabs["fb_cache"], tabs["fb_sloc"], tabs["sm_b"]
    K = 2
    with contextlib.ExitStack() as es:
        sbl = lambda name, shape, dt: es.enter_context(nc.sbuf_tensor(name, shape, dt))
        NS = 3
        qT = sbl("fx_qT", [128, NS, NT], BF16)
        kT = sbl("fx_kT", [128, NS, NT], BF16)
        vl = sbl("fx_vl", [128, NS, 9, 128], BF16)
        kTr = sbl("fx_kTr", [128, NS, 1024], BF16)
        vr = sbl("fx_vr", [128, NS, 8, 128], BF16)
        kTc = sbl("fx_kTc", [128, NS, 1024], BF16)
        vc = sbl("fx_vc", [128, NS, 8, 128], BF16)
        in_b = [[Buf() for _ in range(7)] for _ in range(NS)]
        pp = sbl("fx_p", [128, K, 2, 512], BF16)
        pp_b = [[Buf(), Buf()] for _ in range(K)]
        rd = sbl("fx_rd", [128, K, 512], F32)
        rd_b = [Buf() for _ in range(K)]
        spools = [PsPool(cx.ps[4 * k + 2:4 * k + 4]) for k in range(K)]
        loaded = set()

        def load_head(h):
            if h in loaded:
                return
            loaded.add(h)
            s = h % NS
            ib_ = in_b[s]
            L = lambda i: cx.lanes2_[s * 7 + i]
            cs = slice(h * 128, (h + 1) * 128)
            P.dma("sp", qT[:, s, :], d["qa_s"][h], L(0), writes=[ib_[0]])
            P.dma("sp", kT[:, s, :], d["ka_s"][h], L(1), writes=[ib_[1]])
            P.dma("sp", vl[:, s, 0:8, :], d["va_s"][0:1024, cs].rearrange("(i p) f -> p i f", p=128), L(2), writes=[ib_[2]])
            P.dma("sp", vl[0:16, s, 8, :], d["va_s"][1024:1040, cs], L(2), writes=[ib_[2]])
            P.dma("sp", kTr[:, s, :], d["xg_key_out"][h * 128:(h + 1) * 128, :], L(3), reads=[cx.xgo_b], writes=[ib_[3]])
            P.dma("sp", vr[:, s, :, :], d["xg_val_out"][0:1024, cs].rearrange("(i p) f -> p i f", p=128), L(4), reads=[cx.xgo_b], writes=[ib_[4]])
            P.dma("pool", kTc[:, s, :], d["cfkT"][h], L(5), writes=[ib_[5]])
            P.dma("pool", vc[:, s, :, :], d["cfv"][:, cs].rearrange("(i p) f -> p i f", p=128), L(6), writes=[ib_[6]])

        def chain(h, ti, sl):
            load_head(h)
            if h + 1 < 8:
                load_head(h + 1)
            s = h % NS
            ib_ = in_b[s]
            t0, n = TILES[ti]
            if ti < 2:
                chunks = [("rem", i) for i in range(8)] + [("loc", i) for i in range(4 * ti + 4)]
            else:
                chunks = [("cache", i) for i in range(8)] + [("sloc", 8)]
            den, den_b = cx.ps[4 * sl]
            oT, oT_b = cx.ps[4 * sl + 1]
            spool = spools[sl]
            nch = len(chunks)

            def info(ci):
                kind, i = chunks[ci]
                m, c0 = 128, 0
                if kind == "rem":
                    kap, kb_, vap, vb_ = kTr[:, s, i * 128:(i + 1) * 128], ib_[3], vr[:, s, i, :], ib_[4]
                elif kind == "loc":
                    kap, kb_, vap, vb_ = kT[:, s, i * 128:(i + 1) * 128], ib_[1], vl[:, s, i, :], ib_[2]
                    c0 = max(0, (i - 4 * ti) * 128)
                elif kind == "cache":
                    kap, kb_, vap, vb_ = kTc[:, s, i * 128:(i + 1) * 128], ib_[5], vc[:, s, i, :], ib_[6]
                else:
                    m = 16
                    kap, kb_, vap, vb_ = kT[:, s, 1024:1040], ib_[1], vl[0:16, s, 8, :], ib_[2]
                return kind, i, kap, kb_, vap, vb_, m, c0

            sps = {}

            def fst(ci):
                kind, i, kap, kb_, vap, vb_, m, c0 = info(ci)
                sp_, sp_b = spool.get()
                sps[ci] = (sp_, sp_b)
                P.op("pe", lambda e: e.matmul(sp_[0:m, c0:n], kap, qT[:, s, t0 + c0:t0 + n], start=True, stop=True), reads=[kb_, ib_[0]], writes=[sp_b])

            def est(ci):
                kind, i, kap, kb_, vap, vb_, m, c0 = info(ci)
                sp_, sp_b = sps.pop(ci)
                a = ci % 2
                pb = pp_b[sl][a]
                if ti < 2:
                    for sq in range(c0 // 128, 4):
                        jq = 4 * ti + sq
                        cc = slice(sq * 128, (sq + 1) * 128)
                        bias = fb_rem[:, h, jq, i:i + 1] if kind == "rem" else fb_loc[:, h, jq, i:i + 1]
                        P.op("act", lambda e: e.activation(pp[:, sl, a, cc], sp_[:, cc], AF.Exp, bias=bias), reads=[sp_b, sm_b], writes=[pb])
                        if kind == "loc" and i == jq:
                            P.op("pool", lambda e: e.tensor_tensor(pp[:, sl, a, cc], pp[:, sl, a, cc], cb("triinc"), ALU.mult), reads=[pb, cx.c_b], writes=[pb])
                else:
                    bias = fb_cache[:, h, i:i + 1] if kind == "cache" else fb_sloc[0:16, h:h + 1]
                    P.op("act", lambda e: e.activation(pp[0:m, sl, a, 0:n], sp_[0:m, 0:n], AF.Exp, bias=bias), reads=[sp_b, sm_b], writes=[pb])
                    if kind == "sloc":
                        P.op("pool", lambda e: e.tensor_tensor(pp[0:16, sl, a, 0:16], pp[0:16, sl, a, 0:16], cb("triinc", 16, 16), ALU.mult), reads=[pb, cx.c_b], writes=[pb])

            def gst(ci):
                kind, i, kap, kb_, vap, vb_, m, c0 = info(ci)
                a = ci % 2
                pb = pp_b[sl][a]
                P.op("pe", lambda e: e.matmul(den[:, c0:n], cb("ones", m, 128), pp[0:m, sl, a, c0:n], start=(ci == 0), stop=(ci == nch - 1)), reads=[pb, cx.c_b], writes=[den_b])
                P.op("pe", lambda e: e.matmul(oT[:, c0:n], vap, pp[0:m, sl, a, c0:n], start=(ci == 0), stop=(ci == nch - 1)), reads=[pb, vb_], writes=[oT_b])

            fst(0)
            yield
            for ci in range(nch):
                if ci + 1 < nch:
                    fst(ci + 1)
                est(ci)
                yield
                gst(ci)
            yield
            P.op("dve", lambda e: e.reciprocal(rd[:, sl, 0:n], den[:, 0:n]), reads=[den_b], writes=[rd_b[sl]])
            P.op("dve", lambda e: e.tensor_tensor(merged[:, h, t0:t0 + n], oT[:, 0:n], rd[:, sl, 0:n], ALU.mult), reads=[oT_b, rd_b[sl]], writes=[merged_b[h][ti]])

        facs = []
        for h in range(8):
            for ti in (1, 0, 2):
                facs.append(lambda sl, h=h, ti=ti: chain(h, ti, sl))
        run_chains(facs, K)


def tile_cols(W):
    K, N = W.shape
    kc = K // 128
    nch = N // 128
    return np.ascontiguousarray(W.reshape(kc, 128, nch, 128).transpose(2, 1, 0, 3).reshape(nch, 128, kc * 128))


IN_SPECS = {
    "xT": ([KC, 128, NT], F32), "lng": ([128, 8, KC], F32), "lnb": ([128, 8, KC], F32),
    "cf": ([128, CF_N], F32), "cb": ([128, CB_N], F32), "flag": ([128, 2], F32),
    "ev_win": ([56, 128, 2048], F32), "ev_wfa": ([128, 128], F32), "ev_wout": ([16, 128, 2048], F32),
    "fox_bf": ([128, 8], F32), "hgrn_ng": ([128, 8], F32), "cflogf": ([128, 8, 8], F32), "lb_logits": ([128, 2, 1024], F32),
    "cfkT": ([8, 128, 1024], F32), "cfv": ([1024, 1024], F32), "hstate_in": ([8, 128, 128], F32),
    "od_win": ([48, 128, 2048], F32), "od_wout": ([16, 128, 2048], F32),
    "cskT": ([16, 128, 1024], F32), "csv": ([1024, 2048], F32),
    "memT": ([KC, 128, 256], F32),
}
for _l in range(2):
    for _f in (1, 2):
        for _n in ("wg", "wu", "wd"):
            IN_SPECS[f"{_n}{_l}{_f}"] = ([NG, 128, 2048], F32)
    IN_SPECS[f"xwq{_l}"] = ([16, 128, 2048], F32)
    IN_SPECS[f"xwkv{_l}"] = ([32, 128, 2048], F32)
    IN_SPECS[f"xwo{_l}"] = ([16, 128, 2048], F32)
    IN_SPECS[f"cmkT{_l}"] = ([16, 128, 256], F32)
    IN_SPECS[f"cmv{_l}"] = ([256, 2048], F32)
OUT_SPECS = {
    "yT": ([KC, 128, NT], F32), "fox_kT": ([8, 128, NT], F32), "fox_v": ([NT, 1024], F32), "fox_logf": ([NT, 8], F32),
    "hstate_p": ([8, 128, 128], F32), "hstate_s": ([8, 128, 128], F32),
    "sb_kT": ([16, 128, NT], F32), "sb_v": ([NT, 2048], F32),
    "mem_kT0": ([16, 128, 256], F32), "mem_v0": ([256, 2048], F32), "mem_kT1": ([16, 128, 256], F32), "mem_v1": ([256, 2048], F32),
}
INT_SPECS = {
    "ka_s": ([8, 128, NT], BF16), "va_s": ([NT, 1024], BF16), "qa_s": ([8, 128, NT], BF16), "hg_s": ([8, 128, NT], BF16),
    "hraw_s": ([8, 9, 128, 3, 128], F32),
    "xg_key_in": ([1024, 1024], BF16), "xg_key_out": ([2048, 1024], BF16),
    "xg_val_in": ([1024, 1024], BF16), "xg_val_out": ([2048, 1024], BF16),
    "xg_c_in": ([128, 64], F32), "xg_c_out": ([256, 64], F32),
    "xg_s_in": ([1024, 128], F32), "xg_s_out": ([2048, 128], F32),
    "sq_s": ([16, 128, NT], BF16), "sk_s": ([16, 128, NT], BF16), "sv_s": ([NT, 2048], BF16),
    "xg2k0_in": ([1024, 1024], BF16), "xg2k0_out": ([2048, 1024], BF16),
    "xg2k1_in": ([1024, 1024], BF16), "xg2k1_out": ([2048, 1024], BF16),
    "xg2v0_in": ([1024, 1024], BF16), "xg2v0_out": ([2048, 1024], BF16),
    "xg2v1_in": ([1024, 1024], BF16), "xg2v1_out": ([2048, 1024], BF16),
}


def declare(cx, nc, ins=None, outs=None):
    cx.d = {}
    for k, (shape, dt) in IN_SPECS.items():
        if ins is None or k in ins:
            cx.d[k] = nc.dram_tensor(k, shape, dt, kind="ExternalInput").ap()
    for k, (shape, dt) in OUT_SPECS.items():
        if outs is None or k in outs:
            cx.d[k] = nc.dram_tensor(k, shape, dt, kind="ExternalOutput").ap()
    for k, (shape, dt) in INT_SPECS.items():
        t = nc.dram_tensor(k, shape, dt)
        cx.d[k + "_t"] = t
        cx.d[k] = t.ap()


def host_shared(I):
    S = {}
    S["lng"] = np.ascontiguousarray(I["ln_g"].reshape(8, KC, 128).transpose(2, 0, 1))
    S["lnb"] = np.ascontiguousarray(I["ln_b"].reshape(8, KC, 128).transpose(2, 0, 1))
    S["cf"], S["cb"] = host_consts()
    W = I["ev_w_in"][0]
    o = {"qa": 0, "ka": 1024, "va": 2048, "fa": 3072, "qb": 3080, "fb": 4104, "ib": 5128, "gb": 6152}
    cat = np.concatenate([W[:, o[k]:o[k] + 1024] for k in ("ka", "va", "qa", "qb", "fb", "ib", "gb")], axis=1)
    S["ev_win"] = tile_cols(cat)
    S["ev_wfa"] = np.ascontiguousarray(W[:, 3072:3080].reshape(KC, 128, 8).transpose(1, 0, 2).reshape(128, 128))
    S["ev_wout"] = tile_cols(I["ev_w_out"][0])
    S["fox_bf"] = np.ascontiguousarray(np.broadcast_to(I["fox_b_f"][0][None, :], (128, 8)))
    S["hgrn_ng"] = np.ascontiguousarray(I["hgrn_norm_g"][0].reshape(8, 128).T)
    S["lb_logits"] = np.ascontiguousarray(np.broadcast_to(I["hgrn_lb_logits"][None, :, :], (128, 2, 1024)))
    S["od_win"] = tile_cols(I["od_w_in"][0])
    S["od_wout"] = tile_cols(I["od_w_out"][0])
    for l in range(2):
        for f in (1, 2):
            S[f"wg{l}{f}"] = tile_cols(I[f"ffn{f}_w_gate"][l])
            S[f"wu{l}{f}"] = tile_cols(I[f"ffn{f}_w_up"][l])
            S[f"wd{l}{f}"] = np.ascontiguousarray(I[f"ffn{f}_w_down"][l].reshape(NG, 128, 2048))
        S[f"xwq{l}"] = tile_cols(I["x_w_q"][l])
        S[f"xwkv{l}"] = tile_cols(I["x_w_kv"][l])
        S[f"xwo{l}"] = tile_cols(I["x_w_o"][l])
    return S


def host_core(I, c):
    b, hf = c // 2, c % 2
    C = {}
    x = np.concatenate([I["x_prompt"][b, hf * 1024:(hf + 1) * 1024], I["x_sample"][c]], axis=0)
    C["xT"] = np.ascontiguousarray(x.T.reshape(KC, 128, NT))
    fl = np.zeros((128, 2), np.float32)
    fl[:, 0] = hf
    fl[:, 1] = (hf - 1) * BIG
    C["flag"] = fl
    C["cflogf"] = np.ascontiguousarray(I["cache_fox_logf"][0, c].reshape(8, 128, 8).transpose(1, 0, 2))
    C["cfkT"] = np.ascontiguousarray(I["cache_fox_k"][0, c].transpose(1, 2, 0))
    C["cfv"] = np.ascontiguousarray(I["cache_fox_v"][0, c].reshape(1024, 1024))
    C["hstate_in"] = np.ascontiguousarray(I["state_hgrn"][0, c])
    C["cskT"] = np.ascontiguousarray(I["cache_sb_k"][0, c].transpose(1, 2, 0))
    C["csv"] = np.ascontiguousarray(I["cache_sb_v"][0, c].reshape(1024, 2048))
    C["memT"] = np.ascontiguousarray(I["mem_prompt"][b].T.reshape(KC, 128, 256))
    for l in range(2):
        C[f"cmkT{l}"] = np.ascontiguousarray(I["cache_mem_k"][l, c].reshape(256, 2048).T.reshape(16, 128, 256))
        C[f"cmv{l}"] = np.ascontiguousarray(I["cache_mem_v"][l, c].reshape(256, 2048))
    return C


def cross_attn(cx, l):
    P = cx.P
    nc = cx.nc
    d = cx.d
    cb = cx.cbv
    with contextlib.ExitStack() as es:
        sbl = lambda name, shape, dt: es.enter_context(nc.sbuf_tensor(uid(cx, name), shape, dt))
        qx = sbl("xa_qx", [128, KC, NT], BF16)
        qx_b = [[Buf() for _ in range(3)] for _ in range(KC)]
        mkT = sbl("xa_mkT", [128, KC, 256], BF16); mkT_b = Buf()
        mv = sbl("xa_mv", [128, 2, 2048], BF16); mv_b = Buf()
        with contextlib.ExitStack() as es2:
            sb2 = lambda name, shape, dt: es2.enter_context(nc.sbuf_tensor(uid(cx, name), shape, dt))
            es2.enter_context(ring(cx, 6))
            memb = sb2("xa_memb", [128, KC, 256], BF16)
            memb_b = [[Buf()] for _ in range(KC)]
            sb_save = cx.sb
            cx.sb = lambda name, shape, dt: es2.enter_context(nc.sbuf_tensor(name, shape, dt))
            sk = Stage(cx, uid(cx, "xa_sk"), [256], F32)
            sv = Stage(cx, uid(cx, "xa_sv"), [2, 128], F32)
            cx.sb = sb_save
            lm = cx.L(0)
            P.dma("pool", memb[:], d["memT"].rearrange("k p m -> p k m"), lm, writes=[b[0] for b in memb_b])
            W = d[f"xwkv{l}"]
            srcs = []
            for c in range(16):
                srcs += [(W[c], 2048), (W[16 + c], 2048), (d[f"xwq{l}"][c], 2048)]
            ws = WStream(cx, srcs, la=4)
            for c in range(16):
                w, w_b = ws.get(3 * c)
                a, fb_, fl = sk.get()

                def cons(ti, t0, n, ps, ps_b):
                    P.op("act", lambda e: e.copy(sk.t[:, a, :], ps[:, 0:256]), reads=[ps_b], writes=[fb_])
                    P.op("dve", lambda e: e.tensor_copy(mkT[:, c, :], ps[:, 0:256]), reads=[ps_b], writes=[mkT_b])
                proj_fm(cx, w, w_b, memb, memb_b, [(0, 256)], cons)
                P.dma("sp", d[f"mem_kT{l}"][c], sk.t[:, a, :], fl, reads=[fb_])
                w, w_b = ws.get(3 * c + 1)
                a, fb_, fl = sv.get()

                def cons(ci, t0, m, ps, ps_b):
                    P.op("act", lambda e: e.copy(sv.t[:, a, ci, :], ps[:, 0:128]), reads=[ps_b], writes=[fb_])
                    P.op("dve", lambda e: e.tensor_copy(mv[:, ci, c * 128:(c + 1) * 128], ps[:, 0:128]), reads=[ps_b], writes=[mv_b])
                proj_tm(cx, w, w_b, memb, memb_b, [(0, 128, 0), (128, 128, 0)], cons)
                P.dma("sp", d[f"mem_v{l}"][:, c * 128:(c + 1) * 128].rearrange("(i p) f -> p i f", p=128), sv.t[:, a, :, :], fl, reads=[fb_])
                w, w_b = ws.get(3 * c + 2)

                def cons(ti, t0, n, ps, ps_b):
                    P.op("dve", lambda e: e.tensor_scalar(qx[:, c, t0:t0 + n], ps[:, 0:n], SC512, None, ALU.mult), reads=[ps_b], writes=[qx_b[c][ti]])
                proj_fm(cx, w, w_b, cx.xb, cx.xb_b, TILES_F, cons)
            P.barrier()
        cmk = sbl("xa_cmk", [128, 2, 4, 256], BF16)
        cmv = sbl("xa_cmv", [128, 2, 2, 512], BF16)
        cm_b = [[Buf(), Buf()] for _ in range(2)]
        pp2 = sbl("xa_pp", [128, 2, 2, 512], BF16)
        pp_b = [Buf(), Buf()]
        rd = sbl("xa_rd", [128, 512], F32); rd_b = Buf()
        pi = 0
        for h in range(4):
            s = h % 2
            P.dma("pool", cmk[:, s, :, :], d[f"cmkT{l}"][4 * h:4 * h + 4].rearrange("k p m -> p k m"), cx.L(1 + 2 * s), writes=[cm_b[s][0]])
            P.dma("pool", cmv[:, s, :, :], d[f"cmv{l}"][:, h * 512:(h + 1) * 512].rearrange("(i p) f -> p i f", p=128), cx.L(2 + 2 * s), writes=[cm_b[s][1]])
            for ti, (t0, n) in enumerate(TILES):
                a = pi % 2
                pi += 1
                for mc in range(2):
                    sp_, sp_b = cx.psA.get()
                    for dc in range(4):
                        if ti < 2:
                            kap, kb_ = mkT[:, 4 * h + dc, mc * 128:(mc + 1) * 128], mkT_b
                        else:
                            kap, kb_ = cmk[:, s, dc, mc * 128:(mc + 1) * 128], cm_b[s][0]
                        P.op("pe", lambda e: e.matmul(sp_[:, 0:n], kap, qx[:, 4 * h + dc, t0:t0 + n], start=(dc == 0), stop=(dc == 3)), reads=[kb_, qx_b[4 * h + dc][ti]], writes=[sp_b])
                    P.op("act", lambda e: e.activation(pp2[:, a, mc, 0:n], sp_[:, 0:n], AF.Exp), reads=[sp_b], writes=[pp_b[a]])
                den, den_b = cx.psB.get()
                for mc in range(2):
                    P.op("pe", lambda e: e.matmul(den[:, 0:n], cb("ones"), pp2[:, a, mc, 0:n], start=(mc == 0), stop=(mc == 1)), reads=[pp_b[a], cx.c_b], writes=[den_b])
                P.op("dve", lambda e: e.reciprocal(rd[:, 0:n], den[:, 0:n]), reads=[den_b], writes=[rd_b])
                for dc in range(4):
                    o_, o_b = cx.psB.get()
                    for mc in range(2):
                        if ti < 2:
                            vap, vb_ = mv[:, mc, (4 * h + dc) * 128:(4 * h + dc + 1) * 128], mv_b
                        else:
                            vap, vb_ = cmv[:, s, mc, dc * 128:(dc + 1) * 128], cm_b[s][1]
                        P.op("pe", lambda e: e.matmul(o_[:, 0:n], vap, pp2[:, a, mc, 0:n], start=(mc == 0), stop=(mc == 1)), reads=[vb_, pp_b[a]], writes=[o_b])
                    P.op("dve", lambda e: e.tensor_tensor(cx.xb[:, 4 * h + dc, t0:t0 + n], o_[:, 0:n], rd[:, 0:n], ALU.mult), reads=[o_b, rd_b], writes=[cx.xb_b[4 * h + dc][ti]])
        P.barrier()
    out_proj_residual(cx, d[f"xwo{l}"], cx.xb, cx.xb_b)


def odd_mixer(cx):
    P = cx.P
    nc = cx.nc
    d = cx.d
    cb = cx.cbv
    G2 = cx.groups
    with contextlib.ExitStack() as es1:
        sb_save = cx.sb
        cx.sb = lambda name, shape, dt: es1.enter_context(nc.sbuf_tensor(name, shape, dt))
        es1.enter_context(ring(cx))
        sfm = Stage(cx, "o1_sfm", [NT], F32)
        bfm = Stage(cx, "o1_bfm", [NT], BF16)
        stm = Stage(cx, "o1_stm", [9, 128], F32)
        btm = Stage(cx, "o1_btm", [9, 128], BF16)
        cx.sb = sb_save
        W = d["od_win"]
        order = [("k", h, 16 + h) for h in range(16)] + [("v", h, 32 + h) for h in range(16)] + [("q", h, h) for h in range(16)]
        ws = WStream(cx, [(W[c], 2048) for (_, _, c) in order], la=8)
        for oi, (kind, h, c) in enumerate(order):
            w, w_b = ws.get(oi)
            g, hh = h // 8, h % 8
            if kind == "k":
                a, fb_, fl = sfm.get()
                a2, bb_, bl = bfm.get()

                def cons(ti, t0, n, ps, ps_b):
                    P.op("act", lambda e: e.copy(sfm.t[:, a, t0:t0 + n], ps[:, 0:n]), reads=[ps_b], writes=[fb_])
                    P.op("dve", lambda e: e.tensor_copy(bfm.t[:, a2, t0:t0 + n], ps[:, 0:n]), reads=[ps_b], writes=[bb_])
                proj_fm(cx, w, w_b, cx.xb, cx.xb_b, TILES_F, cons)
                P.dma("sp", d["sb_kT"][h], sfm.t[:, a, :], fl, reads=[fb_])
                P.dma("sp", d["sk_s"][h], bfm.t[:, a2, :], bl, reads=[bb_])
                P.dma("sp", d[f"xg2k{g}_in"][hh * 128:(hh + 1) * 128, :], bfm.t[:, a2, 0:1024], bl, reads=[bb_])
            elif kind == "v":
                a, fb_, fl = stm.get()
                a2, bb_, bl = btm.get()

                def cons(ci, t0, m, ps, ps_b):
                    P.op("act", lambda e: e.copy(stm.t[0:m, a, ci, :], ps[0:m, 0:128]), reads=[ps_b], writes=[fb_])
                    P.op("dve", lambda e: e.tensor_copy(btm.t[0:m, a2, ci, :], ps[0:m, 0:128]), reads=[ps_b], writes=[bb_])
                proj_tm(cx, w, w_b, cx.xb, cx.xb_b, TCH, cons)
                cs = slice(h * 128, (h + 1) * 128)
                cs2 = slice(hh * 128, (hh + 1) * 128)
                P.dma("sp", d["sb_v"][0:1024, cs].rearrange("(i p) f -> p i f", p=128), stm.t[:, a, 0:8, :], fl, reads=[fb_])
                P.dma("sp", d["sb_v"][1024:1040, cs], stm.t[0:16, a, 8, :], fl, reads=[fb_])
                P.dma("sp", d["sv_s"][0:1024, cs].rearrange("(i p) f -> p i f", p=128), btm.t[:, a2, 0:8, :], bl, reads=[bb_])
                P.dma("sp", d["sv_s"][1024:1040, cs], btm.t[0:16, a2, 8, :], bl, reads=[bb_])
                P.dma("sp", d[f"xg2v{g}_in"][0:1024, cs2].rearrange("(i p) f -> p i f", p=128), btm.t[:, a2, 0:8, :], bl, reads=[bb_])
            else:
                a2, bb_, bl = bfm.get()

                def cons(ti, t0, n, ps, ps_b):
                    P.op("dve", lambda e: e.tensor_scalar(bfm.t[:, a2, t0:t0 + n], ps[:, 0:n], SC128, None, ALU.mult), reads=[ps_b], writes=[bb_])
                proj_fm(cx, w, w_b, cx.xb, cx.xb_b, TILES_F, cons)
                P.dma("sp", d["sq_s"][h], bfm.t[:, a2, :], bl, reads=[bb_])
        P.barrier()
    for g in range(2):
        P.coll("AllGather", [d[f"xg2k{g}_in_t"].ap().opt()], [d[f"xg2k{g}_out_t"].ap().opt()], G2, cx.cc_lane, writes=[cx.xg2ko_b])
        P.coll("AllGather", [d[f"xg2v{g}_in_t"].ap().opt()], [d[f"xg2v{g}_out_t"].ap().opt()], G2, cx.cc_lane, writes=[cx.xg2vo_b])
    K = 2
    with contextlib.ExitStack() as es:
        sbl = lambda name, shape, dt: es.enter_context(nc.sbuf_tensor(name, shape, dt))
        NS = 3
        qT = sbl("sb_qT", [128, NS, NT], BF16)
        kT = sbl("sb_kT_", [128, NS, NT], BF16)
        vl = sbl("sb_vl", [128, NS, 9, 128], BF16)
        kTr = sbl("sb_kTr", [128, NS, 1024], BF16)
        vr = sbl("sb_vr", [128, NS, 8, 128], BF16)
        kTc = sbl("sb_kTc", [128, NS, 1024], BF16)
        vc = sbl("sb_vc", [128, NS, 8, 128], BF16)
        in_b = [[Buf() for _ in range(7)] for _ in range(NS)]
        e1 = sbl("sb_e1", [128, K, 2, 512], F32); e1_b = [[Buf(), Buf()] for _ in range(K)]
        lp = sbl("sb_lp", [128, K, 2, 512], BF16); lp_b = [[Buf(), Buf()] for _ in range(K)]
        xx = sbl("sb_xx", [128, K, 512], F32); xx_b = [Buf() for _ in range(K)]
        ww = sbl("sb_ww", [128, K, 512], BF16); ww_b = [Buf() for _ in range(K)]
        zpools = [PsPool(cx.ps[4 * k + 2:4 * k + 4]) for k in range(K)]
        loaded = set()

        def load_head(h):
            if h in loaded:
                return
            loaded.add(h)
            s = h % NS
            g, hh = h // 8, h % 8
            ib_ = in_b[s]
            L = lambda i: cx.lanes2_[s * 7 + i]
            cs = slice(h * 128, (h + 1) * 128)
            cs2 = slice(hh * 128, (hh + 1) * 128)
            P.dma("sp", qT[:, s, :], d["sq_s"][h], L(0), writes=[ib_[0]])
            P.dma("sp", kT[:, s, :], d["sk_s"][h], L(1), writes=[ib_[1]])
            P.dma("sp", vl[:, s, 0:8, :], d["sv_s"][0:1024, cs].rearrange("(i p) f -> p i f", p=128), L(2), writes=[ib_[2]])
            P.dma("sp", vl[0:16, s, 8, :], d["sv_s"][1024:1040, cs], L(2), writes=[ib_[2]])
            P.dma("sp", kTr[:, s, :], d[f"xg2k{g}_out"][hh * 128:(hh + 1) * 128, :], L(3), reads=[cx.xg2ko_b], writes=[ib_[3]])
            P.dma("sp", vr[:, s, :, :], d[f"xg2v{g}_out"][0:1024, cs2].rearrange("(i p) f -> p i f", p=128), L(4), reads=[cx.xg2vo_b], writes=[ib_[4]])
            P.dma("pool", kTc[:, s, :], d["cskT"][h], L(5), writes=[ib_[5]])
            P.dma("pool", vc[:, s, :, :], d["csv"][:, cs].rearrange("(i p) f -> p i f", p=128), L(6), writes=[ib_[6]])

        def chain(h, ti, sl):
            load_head(h)
            if h + 1 < 16:
                load_head(h + 1)
            s = h % NS
            ib_ = in_b[s]
            t0, n = TILES[ti]
            if ti < 2:
                chunks = [("loc", i) for i in range(4 * ti + 3, -1, -1)] + [("rem", i) for i in range(7, -1, -1)]
            else:
                chunks = [("sloc", 8)] + [("cache", i) for i in range(7, -1, -1)]
            oT, oT_b = cx.ps[4 * sl]
            A, A_b = cx.ps[4 * sl + 1]
            zpool = zpools[sl]
            P.op("pe", lambda e: e.matmul(oT[:, 0:n], cb("zeros"), qT[:, s, t0:t0 + n], start=True, stop=False), reads=[cx.c_b, ib_[0]], writes=[oT_b])
            P.op("pe", lambda e: e.matmul(A[:, 0:n], cb("zeros"), qT[:, s, t0:t0 + n], start=True, stop=False), reads=[cx.c_b, ib_[0]], writes=[A_b])

            def info(ci):
                kind, i = chunks[ci]
                m, c0, diag, bias = 128, 0, False, 0.0
                if kind == "rem":
                    kap, kb_, vap, vb_ = kTr[:, s, i * 128:(i + 1) * 128], ib_[3], vr[:, s, i, :], ib_[4]
                    bias = cx.flag[:, 1:2]
                elif kind == "loc":
                    kap, kb_, vap, vb_ = kT[:, s, i * 128:(i + 1) * 128], ib_[1], vl[:, s, i, :], ib_[2]
                    c0 = max(0, (i - 4 * ti) * 128)
                    diag = i >= 4 * ti
                elif kind == "cache":
                    kap, kb_, vap, vb_ = kTc[:, s, i * 128:(i + 1) * 128], ib_[5], vc[:, s, i, :], ib_[6]
                else:
                    m = 16
                    kap, kb_, vap, vb_ = kT[:, s, 1024:1040], ib_[1], vl[0:16, s, 8, :], ib_[2]
                    diag = True
                return kap, kb_, vap, vb_, m, c0, diag, bias

            nch = len(chunks)
            zps = {}

            def f1(ci):
                kap, kb_, vap, vb_, m, c0, diag, bias = info(ci)
                zp, zp_b = zpool.get()
                zps[ci] = (zp, zp_b)
                P.op("pe", lambda e: e.matmul(zp[0:m, c0:n], kap, qT[:, s, t0 + c0:t0 + n], start=True, stop=True), reads=[kb_, ib_[0]], writes=[zp_b])

            def f2(ci):
                kap, kb_, vap, vb_, m, c0, diag, bias = info(ci)
                zp, zp_b = zps.pop(ci)
                a = ci % 2
                P.op("act", lambda e: e.activation(e1[0:m, sl, a, c0:n], zp[0:m, c0:n], AF.Exp), reads=[zp_b], writes=[e1_b[sl][a]])

            def f3(ci):
                kap, kb_, vap, vb_, m, c0, diag, bias = info(ci)
                a = ci % 2
                dm = min(m, 128)
                P.op("act", lambda e: e.activation(lp[0:m, sl, a, c0:n], e1[0:m, sl, a, c0:n], AF.Ln, bias=1.0), reads=[e1_b[sl][a]], writes=[lp_b[sl][a]])
                if diag:
                    P.op("pool", lambda e: e.tensor_tensor(lp[0:m, sl, a, c0:c0 + dm], lp[0:m, sl, a, c0:c0 + dm], cb("trilt", m, dm), ALU.mult), reads=[lp_b[sl][a], cx.c_b], writes=[lp_b[sl][a]])

            def b1(ci):
                kap, kb_, vap, vb_, m, c0, diag, bias = info(ci)
                a = ci % 2
                P.op("pe", lambda e: e.matmul(A[:, c0:n], cb("negtrige", m, 128), lp[0:m, sl, a, c0:n], start=False, stop=False), reads=[lp_b[sl][a], cx.c_b], writes=[A_b])

            def b2(ci):
                kap, kb_, vap, vb_, m, c0, diag, bias = info(ci)
                a = ci % 2
                P.op("act", lambda e: e.activation(xx[0:m, sl, c0:n], A[0:m, c0:n], AF.Exp, bias=bias), reads=[A_b, cx.c_b], writes=[xx_b[sl]])

            def b3(ci):
                kap, kb_, vap, vb_, m, c0, diag, bias = info(ci)
                a = ci % 2
                dm = min(m, 128)
                P.op("dve", lambda e: e.tensor_tensor(ww[0:m, sl, c0:n], e1[0:m, sl, a, c0:n], xx[0:m, sl, c0:n], ALU.mult), reads=[e1_b[sl][a], xx_b[sl]], writes=[ww_b[sl]])
                if diag:
                    P.op("pool", lambda e: e.tensor_tensor(ww[0:m, sl, c0:c0 + dm], ww[0:m, sl, c0:c0 + dm], cb("trilt", m, dm), ALU.mult), reads=[ww_b[sl], cx.c_b], writes=[ww_b[sl]])

            def b4(ci):
                kap, kb_, vap, vb_, m, c0, diag, bias = info(ci)
                a = ci % 2
                P.op("pe", lambda e: e.matmul(oT[:, c0:n], vap, ww[0:m, sl, c0:n], start=False, stop=(ci == nch - 1)), reads=[ww_b[sl], vb_], writes=[oT_b])
                P.op("pe", lambda e: e.matmul(A[:, c0:n], cb("negtrilt", m, 128), lp[0:m, sl, a, c0:n], start=False, stop=(ci == nch - 1)), reads=[lp_b[sl][a], ww_b[sl], cx.c_b], writes=[A_b])

            f1(0); yield
            f2(0); yield
            f3(0); yield
            for ci in range(nch):
                nx = ci + 1 < nch
                if nx:
                    f1(ci + 1)
                b1(ci); yield
                if nx:
                    f2(ci + 1)
                b2(ci); yield
                if nx:
                    f3(ci + 1)
                b3(ci); yield
                b4(ci)
            yield
            P.op("act", lambda e: e.copy(cx.xb[:, h, t0:t0 + n], oT[:, 0:n]), reads=[oT_b], writes=[cx.xb_b[h][ti]])

        facs = []
        for h in range(16):
            for ti in (1, 0, 2):
                facs.append(lambda sl, h=h, ti=ti: chain(h, ti, sl))
        run_chains(facs, K)
        P.barrier()
    out_proj_residual(cx, d["od_wout"], cx.xb, cx.xb_b)


def build_program(cx):
    nc = cx.nc
    P = cx.P
    d = cx.d
    setup(nc, cx)
    setup_consts2(cx)
    load_x(cx)
    P.barrier()
    for l in range(2):
        ffn(cx, d[f"wg{l}1"], d[f"wu{l}1"], d[f"wd{l}1"])
        layer_norm(cx, 4 * l + 0)
        if l == 0:
            even_mixer(cx)
        else:
            odd_mixer(cx)
        layer_norm(cx, 4 * l + 1)
        cross_attn(cx, l)
        layer_norm(cx, 4 * l + 2)
        ffn(cx, d[f"wg{l}2"], d[f"wu{l}2"], d[f"wd{l}2"])
        layer_norm(cx, 4 * l + 3, final=(l == 1))
    store_y(cx)
    P.barrier()


_NC_CACHE = {}


def get_nc(ncores=8):
    if ncores in _NC_CACHE:
        return _NC_CACHE[ncores]
    nc = bass.Bass("TRN2", target_bir_lowering=False)
    cx = Ctx()
    cx.nc = nc
    cx.P = Prog(nc)
    cx.groups = [[2 * i, 2 * i + 1] for i in range(ncores // 2)]
    declare(cx, nc)
    with contextlib.ExitStack() as es:
        cx.es = es
        build_program(cx)
    _NC_CACHE[ncores] = nc
    return nc


def kernel(**inputs):
    I = {k: np.asarray(v) for k, v in inputs.items()}
    ncores = 8
    nc = get_nc(ncores)
    S = host_shared(I)
    in_maps = []
    for c in range(ncores):
        C = host_core(I, c)
        C.update(S)
        in_maps.append({k: np.ascontiguousarray(C[k], dtype=np.float32) for k in IN_SPECS})
    res = run_bass_kernel_spmd(nc, in_maps, core_ids=list(range(ncores)))
    R = res.results
    f32 = np.float32
    y_p = np.zeros((4, 2048, 2048), f32); y_s = np.zeros((8, 16, 2048), f32)
    fk_p = np.zeros((1, 4, 2048, 8, 128), f32); fv_p = np.zeros((1, 4, 2048, 8, 128), f32); fl_p = np.zeros((1, 4, 2048, 8), f32)
    hs_p = np.zeros((1, 4, 8, 128, 128), f32)
    sk_p = np.zeros((1, 4, 2048, 16, 128), f32); sv_p = np.zeros((1, 4, 2048, 16, 128), f32)
    mk_p = np.zeros((2, 4, 256, 4, 512), f32); mv_p = np.zeros((2, 4, 256, 4, 512), f32)
    fk_s = np.zeros((1, 8, 16, 8, 128), f32); fv_s = np.zeros((1, 8, 16, 8, 128), f32); fl_s = np.zeros((1, 8, 16, 8), f32)
    hs_s = np.zeros((1, 8, 8, 128, 128), f32)
    sk_s = np.zeros((1, 8, 16, 16, 128), f32); sv_s = np.zeros((1, 8, 16, 16, 128), f32)
    for c in range(ncores):
        r = R[c]
        b, hf = c // 2, c % 2
        sl = slice(hf * 1024, (hf + 1) * 1024)
        y = np.asarray(r["yT"]).reshape(2048, NT).T
        y_p[b, sl] = y[:1024]; y_s[c] = y[1024:]
        k = np.asarray(r["fox_kT"]).transpose(2, 0, 1)
        fk_p[0, b, sl] = k[:1024]; fk_s[0, c] = k[1024:]
        v = np.asarray(r["fox_v"]).reshape(NT, 8, 128)
        fv_p[0, b, sl] = v[:1024]; fv_s[0, c] = v[1024:]
        lf = np.asarray(r["fox_logf"])
        fl_p[0, b, sl] = lf[:1024]; fl_s[0, c] = lf[1024:]
        if hf == 1:
            hs_p[0, b] = np.asarray(r["hstate_p"])
        hs_s[0, c] = np.asarray(r["hstate_s"])
        k = np.asarray(r["sb_kT"]).transpose(2, 0, 1)
        sk_p[0, b, sl] = k[:1024]; sk_s[0, c] = k[1024:]
        v = np.asarray(r["sb_v"]).reshape(NT, 16, 128)
        sv_p[0, b, sl] = v[:1024]; sv_s[0, c] = v[1024:]
        if hf == 0:
            for l in range(2):
                mk_p[l, b] = np.asarray(r[f"mem_kT{l}"]).reshape(2048, 256).T.reshape(256, 4, 512)
                mv_p[l, b] = np.asarray(r[f"mem_v{l}"]).reshape(256, 4, 512)
    return (y_p, y_s, fk_p, fv_p, fl_p, hs_p, sk_p, sv_p, mk_p, mv_p, fk_s, fv_s, fl_s, hs_s, sk_s, sv_s)
```
